# Optimizing a Trainium2 kernel written in Bass

```python
import jax, jax.numpy as jnp
from jax import lax
import numpy as np

D_MODEL = 2048
BATCH = 2
SEQ = 4096
DEPTH = 2

CHUNK = 64
HEAD_DIM = 128
N_MIXERS = 4
HEADS_PER_MIXER = D_MODEL // (N_MIXERS * HEAD_DIM)
D_GROUP = HEADS_PER_MIXER * HEAD_DIM
D_MIX = N_MIXERS * D_GROUP
DN_CONV = 4
B_PREV_CHUNKS = 8
REL_CLIP = 256
SWA_WINDOW = 128
C_PREV_CHUNKS = SWA_WINDOW // CHUNK
C_KV_HEADS = HEADS_PER_MIXER // 2
SGU_BLOCK = 128
D_FF = ((8 * D_MODEL // 3 + 127) // 128) * 128
FFN_CONV = 3
EPS = 1e-6
IN_SPLITS = ((D_GROUP,) * 4 + (HEADS_PER_MIXER,) * 2 + (D_GROUP,) * 3
             + (D_GROUP, C_KV_HEADS * HEAD_DIM, C_KV_HEADS * HEAD_DIM) + (D_GROUP, D_GROUP))
N_IN = sum(IN_SPLITS)

kernel_name = 'hybrid_parallel_group_streaming_encoder'


def rms_norm(x, g):
    xf = x.astype(jnp.float32)
    y = xf * lax.rsqrt(jnp.mean(xf * xf, axis=-1, keepdims=True) + EPS)
    return (y * g.astype(jnp.float32)).astype(x.dtype)


def layer_norm(x, g):
    xf = x.astype(jnp.float32)
    mu = jnp.mean(xf, axis=-1, keepdims=True)
    var = jnp.mean(jnp.square(xf - mu), axis=-1, keepdims=True)
    return ((xf - mu) * lax.rsqrt(var + EPS) * g.astype(jnp.float32)).astype(x.dtype)


def l2_norm(x):
    xf = x.astype(jnp.float32)
    return xf * lax.rsqrt(jnp.sum(xf * xf, axis=-1, keepdims=True) + EPS)


def causal_dwconv(x, w):
    k, ch = w.shape
    return lax.conv_general_dilated(x, w[:, None, :].astype(x.dtype), window_strides=(1,),
                                    padding=[(k - 1, 0)], dimension_numbers=('NWC', 'WIO', 'NWC'),
                                    feature_group_count=ch)


def chunk_band(t, n_prev):
    b, s, h, d = t.shape
    nc = s // CHUNK
    tp = jnp.pad(t.reshape(b, nc, CHUNK, h, d), ((0, 0), (n_prev, 0), (0, 0), (0, 0), (0, 0)))
    idx = jnp.arange(nc)[:, None] + jnp.arange(n_prev + 1)[None, :]
    return tp[:, idx].reshape(b, nc, (n_prev + 1) * CHUNK, h, d)


def band_valid(nc, n_prev):
    chunk_id = jnp.arange(nc)[:, None] - n_prev + jnp.arange(n_prev + 1)[None, :]
    return jnp.repeat(chunk_id >= 0, CHUNK, axis=1)


def chunked_delta_rule(q, k, v, log_alpha, beta):
    f32 = jnp.float32
    b, s, h, dk = q.shape
    dv = v.shape[-1]
    nc = s // CHUNK

    def chunks(t):
        t = t.astype(f32).reshape(b, nc, CHUNK, h, *t.shape[3:])
        return jnp.moveaxis(t, 3, 1)

    q = chunks(q) * (dk ** -0.5)
    k = chunks(k)
    v = chunks(v)
    beta = chunks(beta)
    g = jnp.cumsum(chunks(log_alpha), axis=-1)
    i = jnp.arange(CHUNK)
    causal = i[:, None] >= i[None, :]
    strict = i[:, None] > i[None, :]
    decay = jnp.exp(jnp.where(causal, g[..., :, None] - g[..., None, :], -jnp.inf))
    k_beta = k * beta[..., None]
    a_mat = jnp.where(strict, jnp.einsum('bhncd,bhnsd->bhncs', k_beta, k) * decay, 0.0)
    eye = jnp.broadcast_to(jnp.eye(CHUNK, dtype=f32), a_mat.shape)
    t_inv = lax.linalg.triangular_solve(a_mat, eye, left_side=True, lower=True, unit_diagonal=True)
    u = t_inv @ (v * beta[..., None])
    w = t_inv @ (k_beta * jnp.exp(g)[..., None])
    attn = jnp.einsum('bhncd,bhnsd->bhncs', q, k) * decay
    q_dec = q * jnp.exp(g)[..., None]
    g_last = g[..., -1]
    k_dec = k * jnp.exp(g_last[..., None] - g)[..., None]

    def step(state, xs):
        u_c, w_c, q_c, k_c, a_c, gl_c = xs
        v_new = u_c - jnp.einsum('bhcd,bhde->bhce', w_c, state)
        o = jnp.einsum('bhcd,bhde->bhce', q_c, state) + jnp.einsum('bhcs,bhse->bhce', a_c, v_new)
        state = state * jnp.exp(gl_c)[..., None, None] + jnp.einsum('bhcd,bhce->bhde', k_c, v_new)
        return state, o

    xs = (jnp.moveaxis(u, 2, 0), jnp.moveaxis(w, 2, 0), jnp.moveaxis(q_dec, 2, 0),
          jnp.moveaxis(k_dec, 2, 0), jnp.moveaxis(attn, 2, 0), jnp.moveaxis(g_last, 2, 0))
    _, o = lax.scan(step, jnp.zeros((b, h, dk, dv), f32), xs)
    return jnp.transpose(o, (1, 0, 3, 2, 4)).reshape(b, s, h, dv)


def gated_deltanet(q, k, v, gate, beta_logit, a_logit, conv_w, a_log, dt_bias, norm_g):
    b, s, _ = q.shape
    h = HEADS_PER_MIXER
    qkv = jax.nn.silu(causal_dwconv(jnp.concatenate([q, k, v], axis=-1), conv_w))
    q, k, v = jnp.split(qkv, 3, axis=-1)
    q = l2_norm(q.reshape(b, s, h, HEAD_DIM))
    k = l2_norm(k.reshape(b, s, h, HEAD_DIM))
    v = v.reshape(b, s, h, HEAD_DIM)
    beta = jax.nn.sigmoid(beta_logit.astype(jnp.float32))
    log_alpha = -jnp.exp(a_log.astype(jnp.float32)) * jax.nn.softplus(
        a_logit.astype(jnp.float32) + dt_bias.astype(jnp.float32))
    o = chunked_delta_rule(q, k, v, log_alpha, beta)
    o = rms_norm(o, norm_g) * jax.nn.silu(gate.astype(jnp.float32).reshape(b, s, h, HEAD_DIM))
    return o.reshape(b, s, h * HEAD_DIM).astype(gate.dtype)


def chunked_relbias_attention(q, k, v, rel_bias):
    b, s, h, d = q.shape
    nc = s // CHUNK
    band = (B_PREV_CHUNKS + 1) * CHUNK
    qc = q.reshape(b, nc, CHUNK, h, d)
    kb = chunk_band(k, B_PREV_CHUNKS)
    vb = chunk_band(v, B_PREV_CHUNKS)
    dist = B_PREV_CHUNKS * CHUNK + jnp.arange(CHUNK)[:, None] - jnp.arange(band)[None, :]
    bias = rel_bias[:, jnp.clip(dist, -REL_CLIP, REL_CLIP) + REL_CLIP].astype(jnp.float32)
    scores = jnp.einsum('bnqhd,bnkhd->bhnqk', qc, kb).astype(jnp.float32) * (d ** -0.5) + bias[:, None]
    scores = jnp.where(band_valid(nc, B_PREV_CHUNKS)[:, None, :], scores, -jnp.inf)
    p = jax.nn.softmax(scores, axis=-1).astype(v.dtype)
    o = jnp.einsum('bhnqk,bnkhd->bnqhd', p, vb)
    return o.reshape(b, s, h * d)


def swa_sink_attention(q, k, v, sinks, slopes):
    b, s, hq, d = q.shape
    hk = k.shape[2]
    g = hq // hk
    nc = s // CHUNK
    band = (C_PREV_CHUNKS + 1) * CHUNK
    qc = q.reshape(b, nc, CHUNK, hk, g, d)
    kb = chunk_band(k, C_PREV_CHUNKS)
    vb = chunk_band(v, C_PREV_CHUNKS)
    dist = jnp.abs(C_PREV_CHUNKS * CHUNK + jnp.arange(CHUNK)[:, None]
                   - jnp.arange(band)[None, :]).astype(jnp.float32)
    alibi = -slopes.reshape(hk, g)[:, :, None, None] * dist
    scores = jnp.einsum('bnqkgd,bnskd->bkgnqs', qc, kb).astype(jnp.float32) * (d ** -0.5) + alibi[:, :, None]
    scores = jnp.where(band_valid(nc, C_PREV_CHUNKS)[:, None, :], scores, -jnp.inf)
    sink = sinks.astype(jnp.float32).reshape(hk, g)[:, :, None, None, None]
    m = jnp.maximum(jnp.max(scores, axis=-1, keepdims=True), sink)
    p = jnp.exp(scores - m)
    p = (p / (jnp.sum(p, axis=-1, keepdims=True) + jnp.exp(sink - m))).astype(v.dtype)
    o = jnp.einsum('bkgnqs,bnskd->bnqkgd', p, vb)
    return o.reshape(b, s, hq * d)


def spatial_gating(u, v, norm_g, w_s, b_s):
    b, s, dg = u.shape
    ng = w_s.shape[0]
    cg = dg // ng
    nb = s // SGU_BLOCK
    u = jax.nn.gelu(u)
    v = layer_norm(jax.nn.gelu(v), norm_g)
    tri = jnp.tril(jnp.ones((SGU_BLOCK, SGU_BLOCK), dtype=bool))
    w = jnp.where(tri, w_s, 0.0).astype(v.dtype)
    vr = v.reshape(b, nb, SGU_BLOCK, ng, cg)
    mixed = jnp.einsum('gts,bnsgc->bntgc', w, vr) + b_s.T.astype(v.dtype)[:, :, None]
    return u * mixed.reshape(b, s, dg)


def token_mixers(h, w_in, dn_conv_w, dn_a_log, dn_dt_bias, dn_norm_g, rel_bias, sinks, slopes,
                 sgu_norm_g, sgu_w, sgu_b, w_out):
    b, s, _ = h.shape
    offsets = [int(o) for o in np.cumsum(IN_SPLITS)[:-1]]
    (a_q, a_k, a_v, a_gate, a_beta, a_alpha, b_q, b_k, b_v,
     c_q, c_k, c_v, d_u, d_v) = jnp.split(h @ w_in, offsets, axis=-1)
    out_a = gated_deltanet(a_q, a_k, a_v, a_gate, a_beta, a_alpha, dn_conv_w, dn_a_log, dn_dt_bias, dn_norm_g)
    out_b = chunked_relbias_attention(b_q.reshape(b, s, HEADS_PER_MIXER, HEAD_DIM),
                                      b_k.reshape(b, s, HEADS_PER_MIXER, HEAD_DIM),
                                      b_v.reshape(b, s, HEADS_PER_MIXER, HEAD_DIM), rel_bias)
    out_c = swa_sink_attention(c_q.reshape(b, s, HEADS_PER_MIXER, HEAD_DIM),
                               c_k.reshape(b, s, C_KV_HEADS, HEAD_DIM),
                               c_v.reshape(b, s, C_KV_HEADS, HEAD_DIM), sinks, slopes)
    out_d = spatial_gating(d_u, d_v, sgu_norm_g, sgu_w, sgu_b)
    return jnp.concatenate([out_a, out_b, out_c, out_d], axis=-1) @ w_out


def conv_glu_ffn(h, w_up, conv_w, conv_b, w_down):
    a, gt = jnp.split(h @ w_up, 2, axis=-1)
    a = causal_dwconv(a, conv_w) + conv_b
    return (jax.nn.gelu(a) * gt) @ w_down


def setup_inputs(seed: int = 0) -> dict:
    key = jax.random.key(seed)
    ks = iter(jax.random.split(key, 32))
    L, D, H = DEPTH, D_MODEL, HEADS_PER_MIXER
    f32 = jnp.float32

    def nrm(shape, std):
        return std * jax.random.normal(next(ks), shape, f32)

    def gain(shape):
        return 1.0 + 0.02 * jax.random.normal(next(ks), shape, f32)

    x = nrm((BATCH, SEQ, D), 1.0)
    c = nrm((BATCH, D), 1.0)
    ada_w = nrm((L, D, 6 * D), D ** -0.5)
    ada_b = nrm((L, 6 * D), 0.01)
    mix_pre_g = gain((L, D))
    mix_post_g = gain((L, D))
    w_in = nrm((L, D, N_IN), D ** -0.5)
    dn_conv_w = nrm((L, DN_CONV, 3 * D_GROUP), DN_CONV ** -0.5)
    dn_a_log = jnp.log(jax.random.uniform(next(ks), (L, H), f32, minval=1.0, maxval=16.0))
    dt = jnp.exp(jax.random.uniform(next(ks), (L, H), f32, minval=float(np.log(1e-3)), maxval=float(np.log(1e-1))))
    dn_dt_bias = dt + jnp.log(-jnp.expm1(-dt))
    dn_norm_g = gain((L, HEAD_DIM))
    rel_bias = nrm((L, H, 2 * REL_CLIP + 1), 0.1)
    sinks = nrm((L, H), 0.5)
    sgu_norm_g = gain((L, D_GROUP))
    sgu_w = nrm((L, H, SGU_BLOCK, SGU_BLOCK), 0.5 * SGU_BLOCK ** -0.5)
    sgu_b = gain((L, H, SGU_BLOCK))
    w_out = nrm((L, D_MIX, D), D_MIX ** -0.5)
    ffn_pre_g = gain((L, D))
    ffn_post_g = gain((L, D))
    ffn_w_up = nrm((L, D, 2 * D_FF), D ** -0.5)
    ffn_conv_w = nrm((L, FFN_CONV, D_FF), FFN_CONV ** -0.5)
    ffn_conv_b = nrm((L, D_FF), 0.01)
    ffn_w_down = nrm((L, D_FF, D), D_FF ** -0.5)
    return {'x': x, 'c': c, 'ada_w': ada_w, 'ada_b': ada_b, 'mix_pre_g': mix_pre_g,
            'mix_post_g': mix_post_g, 'w_in': w_in, 'dn_conv_w': dn_conv_w, 'dn_a_log': dn_a_log,
            'dn_dt_bias': dn_dt_bias, 'dn_norm_g': dn_norm_g, 'rel_bias': rel_bias, 'sinks': sinks,
            'sgu_norm_g': sgu_norm_g, 'sgu_w': sgu_w, 'sgu_b': sgu_b, 'w_out': w_out,
            'ffn_pre_g': ffn_pre_g, 'ffn_post_g': ffn_post_g, 'ffn_w_up': ffn_w_up,
            'ffn_conv_w': ffn_conv_w, 'ffn_conv_b': ffn_conv_b, 'ffn_w_down': ffn_w_down}


def reference(x, c, ada_w, ada_b, mix_pre_g, mix_post_g, w_in, dn_conv_w, dn_a_log, dn_dt_bias,
              dn_norm_g, rel_bias, sinks, sgu_norm_g, sgu_w, sgu_b, w_out, ffn_pre_g, ffn_post_g,
              ffn_w_up, ffn_conv_w, ffn_conv_b, ffn_w_down):
    slopes = jnp.asarray(2.0 ** (-8.0 * np.arange(1, HEADS_PER_MIXER + 1) / HEADS_PER_MIXER), dtype=jnp.float32)
    cond = jax.nn.silu(c)
    for l in range(DEPTH):
        mod = cond @ ada_w[l] + ada_b[l]
        sh_m, sc_m, gt_m, sh_f, sc_f, gt_f = jnp.split(mod[:, None, :], 6, axis=-1)
        h = rms_norm(x, mix_pre_g[l]) * (1.0 + sc_m) + sh_m
        y = token_mixers(h, w_in[l], dn_conv_w[l], dn_a_log[l], dn_dt_bias[l], dn_norm_g[l], rel_bias[l],
                         sinks[l], slopes, sgu_norm_g[l], sgu_w[l], sgu_b[l], w_out[l])
        x = x + gt_m * rms_norm(y, mix_post_g[l])
        h = rms_norm(x, ffn_pre_g[l]) * (1.0 + sc_f) + sh_f
        y = conv_glu_ffn(h, ffn_w_up[l], ffn_conv_w[l], ffn_conv_b[l], ffn_w_down[l])
        x = x + gt_f * rms_norm(y, ffn_post_g[l])
    return x
```

```python
import numpy as np
import ml_dtypes
from contextlib import ExitStack
import concourse.bass as bass
import concourse.mybir as mybir
from concourse.bass_utils import run_bass_kernel_spmd

F32 = mybir.dt.float32
BF16 = mybir.dt.bfloat16
AF = mybir.ActivationFunctionType
ALU = mybir.AluOpType
AX = mybir.AxisListType
NPBF = ml_dtypes.bfloat16

NCORES = 8
D = 2048
KC = 16
S_LEN = 4096
TOK = 1024
DFF = 5504
NFF = 43
EPS = 1e-6
ENG = ['pe', 'dve', 'act', 'pool', 'sp']


class Sched:
    def __init__(self, nc, stack):
        self.nc = nc
        self.stack = stack
        self.plan = {e: [] for e in ENG}
        self.cnt = {e: 0 for e in ENG}
        self.seen = {e: {} for e in ENG}
        self.sems = {}
        self.dcnt = {}
        self.res = {}
        for e in ENG:
            self.sems[e] = stack.enter_context(nc.semaphore('s_' + e))

    def _st(self, r):
        st = self.res.get(r)
        if st is None:
            st = {'w': None, 'r': {}}
            self.res[r] = st
        return st

    def _deps(self, eng, reads, writes):
        deps = {}

        def add(k, v):
            if deps.get(k, 0) < v:
                deps[k] = v
        for r in reads:
            w = self._st(r)['w']
            if w is not None:
                add(*w)
        for wr in writes:
            st = self._st(wr)
            if st['w'] is not None:
                add(*st['w'])
            for k, v in st['r'].items():
                if k == eng:
                    continue
                add(k, v)
        return deps

    def _waits(self, eng, deps):
        waits = []
        for k, v in deps.items():
            if k == eng and eng == 'pe':
                continue
            if self.seen[eng].get(k, 0) >= v:
                continue
            self.seen[eng][k] = v
            waits.append((k, v))
        return waits

    def _commit(self, tok, reads, writes):
        k, v = tok
        for r in reads:
            st = self._st(r)
            if st['r'].get(k, 0) < v:
                st['r'][k] = v
        for w in writes:
            st = self._st(w)
            st['w'] = tok
            st['r'] = {}

    def op(self, eng, fn, reads=(), writes=()):
        deps = self._deps(eng, reads, writes)
        waits = self._waits(eng, deps)
        self.cnt[eng] += 1
        tok = (eng, self.cnt[eng])
        self.plan[eng].append((waits, fn, eng, 1))
        self._commit(tok, reads, writes)

    def dma(self, q, key, pairs, reads=(), writes=()):
        if key not in self.sems:
            self.sems[key] = self.stack.enter_context(self.nc.semaphore('d_' + str(key)))
            self.dcnt[key] = 0
        deps = self._deps(q, reads, writes)
        waits = self._waits(q, deps)
        for i, (o, a) in enumerate(pairs):
            self.dcnt[key] += 16
            self.plan[q].append((waits if i == 0 else [],
                                 (lambda e, o=o, a=a: e.dma_start(out=o, in_=a)), key, 16))
        tok = (key, self.dcnt[key])
        self._commit(tok, reads, writes)
        return tok

    def wait_tok(self, eng, tok):
        waits = self._waits(eng, {tok[0]: tok[1]})
        if waits:
            self.plan[eng].append((waits, None, None, 0))

    def emit(self):
        nc = self.nc
        engs = {'pe': 'tensor', 'dve': 'vector', 'act': 'scalar', 'pool': 'gpsimd', 'sp': 'sync'}
        with nc.Block() as block:
            for e in ENG:
                plan = self.plan[e]
                if not plan:
                    continue

                def body(engine, plan=plan):
                    for waits, fn, k, inc in plan:
                        for (wk, wv) in waits:
                            engine.wait_ge(self.sems[wk], wv)
                        if fn is not None:
                            fn(engine).then_inc(self.sems[k], inc)
                getattr(block, engs[e])(body)


class KB:
    def __init__(self):
        self.nc = bass.Bass("TRN2", target_bir_lowering=False)
        self.st = ExitStack()
        self.S = Sched(self.nc, self.st)
        self.out_toks = []
        self.nbank = 0

    def din(self, name, shape, dt=F32):
        return self.nc.dram_tensor(name, list(shape), dt, kind="ExternalInput").ap()

    def dout(self, name, shape, dt=F32):
        return self.nc.dram_tensor(name, list(shape), dt, kind="ExternalOutput").ap()

    def sb(self, name, shape, dt=F32):
        return self.st.enter_context(self.nc.sbuf_tensor('sb_' + name, list(shape), dt))

    def ps(self, name, shape, dt=F32):
        return self.st.enter_context(self.nc.psum_tensor('ps_' + name, list(shape), dt))

    def finish(self):
        for t in self.out_toks:
            self.S.wait_tok('sp', t)
        self.S.emit()
        self.st.close()
        return self.nc

    def mm(self, out, lhsT, rhs, start, stop, r, w):
        self.S.op('pe', lambda e: e.matmul(out, lhsT=lhsT, rhs=rhs, start=start, stop=stop), r, w)

    def tr(self, out, in_, ident, r, w):
        self.S.op('pe', lambda e: e.transpose(out=out, in_=in_, identity=ident), r, w)

    def act(self, out, in_, func, r, w, bias=None, scale=None, accum_out=None):
        kw = {}
        if bias is not None:
            kw['bias'] = bias
        if scale is not None:
            kw['scale'] = scale
        if accum_out is not None:
            kw['accum_out'] = accum_out
        self.S.op('act', lambda e: e.activation(out=out, in_=in_, func=func, **kw), r, w)

    def tt(self, eng, out, in0, in1, op, r, w):
        self.S.op(eng, lambda e: e.tensor_tensor(out=out, in0=in0, in1=in1, op=op), r, w)

    def ts(self, eng, out, in0, s1, s2, op0, op1, r, w, accum_out=None):
        if op1 is None:
            self.S.op(eng, lambda e: e.tensor_scalar(out=out, in0=in0, scalar1=s1, scalar2=None, op0=op0), r, w)
        elif accum_out is None:
            self.S.op(eng, lambda e: e.tensor_scalar(out=out, in0=in0, scalar1=s1, scalar2=s2, op0=op0, op1=op1), r, w)
        else:
            self.S.op(eng, lambda e: e.tensor_scalar(out=out, in0=in0, scalar1=s1, scalar2=s2, op0=op0, op1=op1,
                                                     accum_out=accum_out), r, w)

    def stt(self, out, in0, scalar, in1, op0, op1, r, w):
        self.S.op('dve', lambda e: e.scalar_tensor_tensor(out=out, in0=in0, scalar=scalar, in1=in1,
                                                          op0=op0, op1=op1), r, w)

    def copy(self, eng, out, in_, r, w):
        if eng == 'act':
            self.S.op('act', lambda e: e.copy(out=out, in_=in_), r, w)
        else:
            self.S.op(eng, lambda e: e.tensor_copy(out=out, in_=in_), r, w)

    def memset(self, eng, ap, val, w):
        self.S.op(eng, lambda e: e.memset(ap, val), (), w)

    def recip(self, out, in_, r, w):
        self.S.op('dve', lambda e: e.reciprocal(out=out, in_=in_), r, w)

    def dma(self, q, key, out, in_, r=(), w=()):
        return self.S.dma(q, key, [(out, in_)], r, w)


def fm(ap):
    return ap.rearrange('(kc p) t -> p kc t', p=128)


def vec_fm(v):
    v = np.asarray(v)
    return np.ascontiguousarray(v.reshape(-1, 128).T)


def setup_consts(kb):
    ones = kb.sb('ones', [128, 128], BF16)
    kb.memset('pool', ones[:], 1.0, ['ones'])
    epst = kb.sb('epst', [128, 1])
    kb.memset('pool', epst[:], EPS, ['epst'])
    return ones, epst


def rstd_from_ss(kb, ssp, ssp_res, out, out_res, epst, n, tag):
    kb.act(out, ssp, AF.Sqrt, [ssp_res, 'epst'], [out_res], bias=epst[:], scale=1.0 / n)
    kb.recip(out, out, [out_res], [out_res])


def norm_bufs(kb):
    return {
        'ssp': kb.ps('nm_ssp', [128, 512]),
        'sq': [kb.sb(f'nm_sq{i}', [128, 512], BF16) for i in range(2)],
        't': [kb.sb(f'nm_t{i}', [128, 512]) for i in range(2)],
        'rstd': kb.sb('nm_rstd', [128, 512]),
    }


def norm_mod_half(kb, nb, xt, xres, hf, gs, gsres, sh, shres, ones, epst, hb, hbres):
    sl = slice(hf * 512, (hf + 1) * 512)
    ssp, rstd = nb['ssp'], nb['rstd']
    for kc in range(KC):
        q = nb['sq'][kc % 2]
        qn = f'nm_sq{kc % 2}'
        kb.act(q[:], xt[:, kc, sl], AF.Square, [(xres, kc, hf)], [qn])
        kb.mm(ssp[:], ones[:], q[:], kc == 0, kc == KC - 1, ['ones', qn], ['nm_ssp'])
    rstd_from_ss(kb, ssp[:], 'nm_ssp', rstd[:], 'nm_rstd', epst, D, 'nm')
    for kc in range(KC):
        t = nb['t'][kc % 2]
        tn = f'nm_t{kc % 2}'
        kb.stt(t[:], xt[:, kc, sl], gs[:, kc:kc + 1], rstd[:], ALU.mult, ALU.mult,
               [(xres, kc, hf), gsres, 'nm_rstd'], [tn])
        kb.act(hb[:, kc, :], t[:], AF.Identity, [tn, shres], [(hbres, kc)], bias=sh[:, kc:kc + 1])


def build_L0():
    kb = KB()
    cT = kb.din('cT', [128, KC, 2])
    W = kb.din('W', [D, 3072])
    bias = kb.din('bias', [1, 3072])
    mod = kb.dout('mod', [2, 3072])
    sc = kb.sb('sc', [128, KC, 2])
    scs = kb.sb('scs', [128, KC, 2], BF16)
    bt = kb.sb('bt', [2, 3072])
    ot = kb.sb('ot', [2, 3072])
    wb = [kb.sb(f'wb{i}', [128, KC, 512], BF16) for i in range(2)]
    pb = [kb.ps(f'pb{i}', [2, 512]) for i in range(2)]
    kb.dma('sp', 'ld_c', sc[:], cT[:, :, :], w=['sc'])
    kb.dma('sp', 'ld_b', bt[:], bias.partition_broadcast(2), w=['bt'])
    kb.act(scs[:], sc[:], AF.Silu, ['sc'], ['scs'])
    Wv = fm(W)
    for j in range(6):
        i = j % 2
        kb.dma('pool', f'ld_w{i}', wb[i][:], Wv[:, :, j * 512:(j + 1) * 512], w=[f'wb{i}'])
        for kc in range(KC):
            kb.mm(pb[i][:], scs[:, kc, :], wb[i][:, kc, :], kc == 0, kc == KC - 1, ['scs', f'wb{i}'], [f'pb{i}'])
        kb.tt('dve', ot[:, j * 512:(j + 1) * 512], pb[i][:], bt[:, j * 512:(j + 1) * 512], ALU.add,
              [f'pb{i}', 'bt'], [('ot', j)])
    kb.out_toks.append(kb.dma('sp', 'st', mod[:, :], ot[:], r=[('ot', j) for j in range(6)]))
    return kb.finish()


def build_N1():
    kb = KB()
    xT = kb.din('xT', [D, TOK])
    nv = kb.din('nv', [128, 3, KC])
    hT = kb.dout('hT', [D, TOK], BF16)
    ones, epst = setup_consts(kb)
    xt = kb.sb('xt', [128, KC, TOK])
    nvt = kb.sb('nvt', [128, 3, KC])
    gs = kb.sb('gs', [128, KC])
    kb.dma('sp', 'ld_nv', nvt[:], nv[:, :, :], w=['nv'])
    xv = fm(xT)
    for hf in range(2):
        for g4 in range(4):
            kb.dma('sp', f'ld_x{hf}{g4}', xt[:, g4 * 4:(g4 + 1) * 4, hf * 512:(hf + 1) * 512],
                   xv[:, g4 * 4:(g4 + 1) * 4, hf * 512:(hf + 1) * 512],
                   w=[('x', kc, hf) for kc in range(g4 * 4, g4 * 4 + 4)])
    kb.ts('dve', gs[:], nvt[:, 1, :], 1.0, None, ALU.add, None, ['nv'], ['gs'])
    kb.tt('dve', gs[:], gs[:], nvt[:, 0, :], ALU.mult, ['gs', 'nv'], ['gs'])
    nb = norm_bufs(kb)
    hbs = [kb.sb(f'hb{i}', [128, KC, 512], BF16) for i in range(2)]
    hv = fm(hT)
    for hf in range(2):
        norm_mod_half(kb, nb, xt, 'x', hf, gs, 'gs', nvt[:, 2, :], 'nv', ones, epst, hbs[hf], f'hb{hf}')
        kb.out_toks.append(kb.dma('sp', f'st_h{hf}', hv[:, :, hf * 512:(hf + 1) * 512], hbs[hf][:],
                                  r=[(f'hb{hf}', kc) for kc in range(KC)]))
    return kb.finish()


def build_C1():
    kb = KB()
    xT = kb.din('xT', [D, TOK])
    moT = kb.din('moT', [D, TOK], BF16)
    w_out = kb.din('w_out', [D, D])
    cv = kb.din('cv', [128, 5, KC])
    xo = kb.dout('xo', [D, TOK])
    h2T = kb.dout('h2T', [D, TOK], BF16)
    ones, epst = setup_consts(kb)
    xt = kb.sb('xt', [128, KC, TOK])
    mo = kb.sb('mo', [128, KC, TOK], BF16)
    cvt = kb.sb('cvt', [128, 5, KC])
    gtg = kb.sb('gtg', [128, KC])
    gs = kb.sb('gs', [128, KC])
    kb.dma('sp', 'ld_cv', cvt[:], cv[:, :, :], w=['cv'])
    xv = fm(xT)
    mv = fm(moT)
    for hf in range(2):
        for g4 in range(4):
            kb.dma('sp', f'ld_m{hf}{g4}', mo[:, g4 * 4:(g4 + 1) * 4, hf * 512:(hf + 1) * 512],
                   mv[:, g4 * 4:(g4 + 1) * 4, hf * 512:(hf + 1) * 512],
                   w=[('mo', kc, hf) for kc in range(g4 * 4, g4 * 4 + 4)])
    for hf in range(2):
        for g4 in range(4):
            kb.dma('sp', f'ld_x{hf}{g4}', xt[:, g4 * 4:(g4 + 1) * 4, hf * 512:(hf + 1) * 512],
                   xv[:, g4 * 4:(g4 + 1) * 4, hf * 512:(hf + 1) * 512],
                   w=[('x', kc, hf) for kc in range(g4 * 4, g4 * 4 + 4)])
    kb.tt('dve', gtg[:], cvt[:, 0, :], cvt[:, 1, :], ALU.mult, ['cv'], ['gtg'])
    kb.ts('dve', gs[:], cvt[:, 3, :], 1.0, None, ALU.add, None, ['cv'], ['gs'])
    kb.tt('dve', gs[:], gs[:], cvt[:, 2, :], ALU.mult, ['gs', 'cv'], ['gs'])
    wg = [kb.sb(f'wg{i}', [128, KC, 256], BF16) for i in range(2)]
    yh = kb.sb('yh', [128, KC, 512])
    psy = [kb.ps(f'psy{i}', [128, 512]) for i in range(3)]
    ssy = kb.ps('ssy', [128, 512])
    sqb = [kb.sb(f'sq{i}', [128, 512], BF16) for i in range(2)]
    tb = [kb.sb(f'tb{i}', [128, 512]) for i in range(2)]
    rstd = kb.sb('rstd', [128, 512])
    nb = norm_bufs(kb)
    hbs = [kb.sb(f'hb{i}', [128, KC, 512], BF16) for i in range(2)]
    wv = fm(w_out)
    xov = fm(xo)
    hv = fm(h2T)
    nld = 0
    for hf in range(2):
        sl = slice(hf * 512, (hf + 1) * 512)
        for dc in range(KC):
            if dc % 2 == 0:
                wi = nld % 2
                nld += 1
                kb.dma('pool', f'ld_w{wi}', wg[wi][:], wv[:, :, dc * 128:dc * 128 + 256], w=[f'wg{wi}'])
            wcur = wg[wi]
            p = psy[dc % 3]
            pn = f'psy{dc % 3}'
            for kc in range(KC):
                kb.mm(p[:], wcur[:, kc, (dc % 2) * 128:(dc % 2) * 128 + 128], mo[:, kc, sl], kc == 0, kc == KC - 1,
                      [f'wg{wi}', ('mo', kc, hf)], [pn])
            kb.copy('act', yh[:, dc, :], p[:], [pn], [('yh', dc)])
            q = sqb[dc % 2]
            qn = f'sq{dc % 2}'
            kb.act(q[:], p[:], AF.Square, [pn], [qn])
            kb.mm(ssy[:], ones[:], q[:], dc == 0, dc == KC - 1, ['ones', qn], ['ssy'])
        rstd_from_ss(kb, ssy[:], 'ssy', rstd[:], 'rstd', epst, D, 'y')
        for dc in range(KC):
            t = tb[dc % 2]
            tn = f'tb{dc % 2}'
            kb.stt(t[:], yh[:, dc, :], gtg[:, dc:dc + 1], rstd[:], ALU.mult, ALU.mult,
                   [('yh', dc), 'gtg', 'rstd'], [tn])
            kb.tt('dve', xt[:, dc, sl], xt[:, dc, sl], t[:], ALU.add, [('x', dc, hf), tn], [('x', dc, hf)])
        kb.out_toks.append(kb.dma('sp', f'st_x{hf}', xov[:, :, sl], xt[:, :, sl],
                                  r=[('x', dc, hf) for dc in range(KC)]))
        norm_mod_half(kb, nb, xt, 'x', hf, gs, 'gs', cvt[:, 4, :], 'cv', ones, epst, hbs[hf], f'hb{hf}')
        kb.out_toks.append(kb.dma('sp', f'st_h{hf}', hv[:, :, sl], hbs[hf][:],
                                  r=[(f'hb{hf}', kc) for kc in range(KC)]))
    return kb.finish()


def build_C2a():
    kb = KB()
    h2e = kb.din('h2e', [D, TOK + 2], BF16)
    w_up = kb.din('w_up', [D, 2 * DFF])
    cw = kb.din('cw', [128, NFF, 4])
    actT = kb.dout('actT', [DFF, TOK], BF16)
    he = kb.sb('he', [128, KC, TOK + 2], BF16)
    cwt = kb.sb('cwt', [128, NFF, 4])
    kb.dma('sp', 'ld_cw', cwt[:], cw[:, :, :], w=['cw'])
    hv = fm(h2e)
    for g4 in range(4):
        kb.dma('sp', f'ld_h{g4}', he[:, g4 * 4:(g4 + 1) * 4, :], hv[:, g4 * 4:(g4 + 1) * 4, :],
               w=[('he', kc) for kc in range(g4 * 4, g4 * 4 + 4)])
    wa = [kb.sb(f'wa{i}', [128, KC, 256], BF16) for i in range(2)]
    wgt = [kb.sb(f'wgt{i}', [128, KC, 256], BF16) for i in range(2)]
    psa = [[kb.ps(f'psa{i}{s}', [128, 512]) for s in range(3)] for i in range(2)]
    psg = [kb.ps(f'psg{s}', [128, 512]) for s in range(2)]
    aext = [kb.sb(f'aext{i}', [128, TOK + 2]) for i in range(2)]
    gS = [kb.sb(f'gS{i}', [128, TOK]) for i in range(2)]
    acc = [kb.sb(f'acc{i}', [128, TOK]) for i in range(2)]
    ga = [kb.sb(f'ga{i}', [128, TOK]) for i in range(2)]
    ab = [kb.sb(f'ab{i}', [128, 2, TOK], BF16) for i in range(2)]
    wv = fm(w_up)
    av = actT.rearrange('(j p) t -> p j t', p=128)
    W3 = 342
    for jb in range(22):
        bi = jb % 2
        ncol = 256 if jb < 21 else 128
        kb.dma('pool', f'ld_wa{bi}', wa[bi][:, :, 0:ncol], wv[:, :, jb * 256:jb * 256 + ncol], w=[f'wa{bi}'])
        kb.dma('pool', f'ld_wg{bi}', wgt[bi][:, :, 0:ncol], wv[:, :, DFF + jb * 256:DFF + jb * 256 + ncol],
               w=[f'wgt{bi}'])
        njj = ncol // 128
        for jj in range(njj):
            j = jb * 2 + jj
            i = j % 2
            cs = slice(jj * 128, jj * 128 + 128)
            for s in range(3):
                for kc in range(KC):
                    kb.mm(psa[i][s][:, 0:W3], wa[bi][:, kc, cs], he[:, kc, s * W3:(s + 1) * W3], kc == 0, kc == KC - 1,
                          [f'wa{bi}', ('he', kc)], [f'psa{i}{s}'])
            for s in range(2):
                for kc in range(KC):
                    kb.mm(psg[s][:], wgt[bi][:, kc, cs], he[:, kc, 2 + s * 512:2 + (s + 1) * 512], kc == 0, kc == KC - 1,
                          [f'wgt{bi}', ('he', kc)], [f'psg{s}'])
            for s in range(3):
                kb.copy('act', aext[i][:, s * W3:(s + 1) * W3], psa[i][s][:, 0:W3], [f'psa{i}{s}'], [(f'aext{i}', s)])
            for s in range(2):
                kb.copy('act', gS[i][:, s * 512:(s + 1) * 512], psg[s][:], [f'psg{s}'], [(f'gS{i}', s)])
            ar = [(f'aext{i}', s) for s in range(3)]
            kb.ts('dve', acc[i][:], aext[i][:, 2:TOK + 2], cwt[:, j, 2:3], cwt[:, j, 3:4], ALU.mult, ALU.add,
                  ar + ['cw'], [f'acc{i}'])
            kb.stt(acc[i][:], aext[i][:, 1:TOK + 1], cwt[:, j, 1:2], acc[i][:], ALU.mult, ALU.add,
                   ar + ['cw', f'acc{i}'], [f'acc{i}'])
            kb.stt(acc[i][:], aext[i][:, 0:TOK], cwt[:, j, 0:1], acc[i][:], ALU.mult, ALU.add,
                   ar + ['cw', f'acc{i}'], [f'acc{i}'])
            kb.act(ga[i][:], acc[i][:], AF.Gelu_apprx_tanh, [f'acc{i}'], [f'ga{i}'])
            kb.tt('dve', ab[bi][:, jj, :], ga[i][:], gS[i][:], ALU.mult,
                  [f'ga{i}', (f'gS{i}', 0), (f'gS{i}', 1)], [(f'ab{bi}', jj)])
        kb.out_toks.append(kb.dma('sp', f'st_a{bi}', av[:, jb * 2:jb * 2 + njj, :], ab[bi][:, 0:njj, :],
                                  r=[(f'ab{bi}', jj) for jj in range(njj)]))
    return kb.finish()


def build_C2b():
    kb = KB()
    actT = kb.din('actT', [DFF, TOK], BF16)
    w_down = kb.din('w_down', [DFF, D])
    xT = kb.din('xT', [D, TOK])
    fv = kb.din('fv', [128, 2, KC])
    xo = kb.dout('xo', [D, TOK])
    ones, epst = setup_consts(kb)
    at = kb.sb('at', [128, NFF, TOK], BF16)
    y2 = kb.sb('y2', [128, KC, TOK])
    fvt = kb.sb('fvt', [128, 2, KC])
    gtg = kb.sb('gtg', [128, KC])
    kb.dma('sp', 'ld_fv', fvt[:], fv[:, :, :], w=['fv'])
    kb.tt('dve', gtg[:], fvt[:, 0, :], fvt[:, 1, :], ALU.mult, ['fv'], ['gtg'])
    av = actT.rearrange('(j p) t -> p j t', p=128)
    for g in range(0, NFF, 8):
        n = min(8, NFF - g)
        kb.dma('sp', f'ld_a{g}', at[:, g:g + n, :], av[:, g:g + n, :], w=[('at', j) for j in range(g, g + n)])
    wd = [kb.sb(f'wd{i}', [128, NFF, 128], BF16) for i in range(2)]
    psy = [kb.ps(f'psy{i}', [128, 512]) for i in range(3)]
    ss = [kb.ps(f'ss{i}', [128, 512]) for i in range(2)]
    sqb = [kb.sb(f'sq{i}', [128, 512], BF16) for i in range(2)]
    rstd = kb.sb('rstd', [128, TOK])
    wv = w_down.rearrange('(j p) n -> p j n', p=128)
    n = 0
    for dc in range(KC):
        wi = dc % 2
        kb.dma('pool', f'ld_w{wi}', wd[wi][:], wv[:, :, dc * 128:(dc + 1) * 128], w=[f'wd{wi}'])
        for hf in range(2):
            p = psy[n % 3]
            pn = f'psy{n % 3}'
            for j in range(NFF):
                kb.mm(p[:], wd[wi][:, j, :], at[:, j, hf * 512:(hf + 1) * 512], j == 0, j == NFF - 1,
                      [f'wd{wi}', ('at', j)], [pn])
            kb.copy('act', y2[:, dc, hf * 512:(hf + 1) * 512], p[:], [pn], [('y2', dc, hf)])
            q = sqb[n % 2]
            qn = f'sq{n % 2}'
            kb.act(q[:], p[:], AF.Square, [pn], [qn])
            kb.mm(ss[hf][:], ones[:], q[:], dc == 0, dc == KC - 1, ['ones', qn], [f'ss{hf}'])
            n += 1
    for hf in range(2):
        rstd_from_ss(kb, ss[hf][:], f'ss{hf}', rstd[:, hf * 512:(hf + 1) * 512], ('rstd', hf), epst, D, f'y{hf}')
    xin = [kb.sb(f'xin{i}', [128, TOK]) for i in range(4)]
    xv = fm(xT)
    xov = fm(xo)
    for dc in range(KC):
        xi = dc % 4
        kb.dma('sp', f'ld_x{xi}', xin[xi][:], xv[:, dc, :], w=[f'xin{xi}'])
        yr = [('y2', dc, 0), ('y2', dc, 1)]
        kb.stt(y2[:, dc, :], y2[:, dc, :], gtg[:, dc:dc + 1], rstd[:], ALU.mult, ALU.mult,
               yr + ['gtg', ('rstd', 0), ('rstd', 1)], yr)
        kb.tt('dve', y2[:, dc, :], y2[:, dc, :], xin[xi][:], ALU.add, yr + [f'xin{xi}'], yr)
        kb.out_toks.append(kb.dma('act' if dc % 2 else 'sp', f'st_x{dc % 4}', xov[:, dc, :], y2[:, dc, :], r=yr))
    return kb.finish()


_CACHE = {}


def get_prog(name):
    if name not in _CACHE:
        _CACHE[name] = {'L0': build_L0, 'N1': build_N1, 'C1': build_C1, 'C2a': build_C2a, 'C2b': build_C2b}[name]()
    return _CACHE[name]


def run(name, in_maps):
    nc = {'L0': build_L0, 'N1': build_N1, 'C1': build_C1, 'C2a': build_C2a, 'C2b': build_C2b, 'M': build_M}[name]()
    res = run_bass_kernel_spmd(nc, in_maps, core_ids=list(range(NCORES)))
    return res.results


def core_bq(cid):
    return cid // 4, cid % 4


def tok_shard_T(a, cid):
    b, q = core_bq(cid)
    return np.ascontiguousarray(a[b, q * TOK:(q + 1) * TOK, :].T)


def stack_vecs(vs):
    return np.ascontiguousarray(np.stack([vec_fm(v) for v in vs], axis=1).astype(np.float32))


def host_L0(c, ada_w, ada_b):
    cT = np.ascontiguousarray(c.T.reshape(KC, 128, 2).transpose(1, 0, 2))
    in_maps = []
    for cid in range(NCORES):
        l, j = cid // 4, cid % 4
        in_maps.append({'cT': cT, 'W': np.ascontiguousarray(ada_w[l][:, j * 3072:(j + 1) * 3072]),
                        'bias': np.ascontiguousarray(ada_b[l][None, j * 3072:(j + 1) * 3072])})
    res = run('L0', in_maps)
    mod = np.zeros((2, 2, 6 * D), np.float32)
    for cid in range(NCORES):
        l, j = cid // 4, cid % 4
        mod[l, :, j * 3072:(j + 1) * 3072] = res[cid]['mod']
    return mod


def split_mod(mod_l):
    names = ['sh_m', 'sc_m', 'gt_m', 'sh_f', 'sc_f', 'gt_f']
    return {n: mod_l[:, i * D:(i + 1) * D] for i, n in enumerate(names)}


def host_N1(xT_shards, g, m):
    in_maps = []
    for cid in range(NCORES):
        b, q = core_bq(cid)
        in_maps.append({'xT': xT_shards[cid], 'nv': stack_vecs([g, m['sc_m'][b], m['sh_m'][b]])})
    res = run('N1', in_maps)
    return [r['hT'] for r in res]


def host_C1(xT_shards, moT_shards, w_out, m, post_g, pre_g):
    in_maps = []
    for cid in range(NCORES):
        b, q = core_bq(cid)
        in_maps.append({'xT': xT_shards[cid], 'moT': moT_shards[cid], 'w_out': w_out,
                        'cv': stack_vecs([m['gt_m'][b], post_g, pre_g, m['sc_f'][b], m['sh_f'][b]])})
    res = run('C1', in_maps)
    return [r['xo'] for r in res], [r['h2T'] for r in res]


def host_C2a(h2T_shards, w_up, conv_w, conv_b):
    cw = np.zeros((128, NFF, 4), np.float32)
    for k in range(3):
        cw[:, :, k] = vec_fm(conv_w[k])
    cw[:, :, 3] = vec_fm(conv_b)
    in_maps = []
    for cid in range(NCORES):
        b, q = core_bq(cid)
        if q == 0:
            halo = np.zeros((D, 2), NPBF)
        else:
            halo = h2T_shards[cid - 1][:, -2:]
        in_maps.append({'h2e': np.ascontiguousarray(np.concatenate([halo, h2T_shards[cid]], axis=1)),
                        'w_up': w_up, 'cw': cw})
    res = run('C2a', in_maps)
    return [r['actT'] for r in res]


def host_C2b(actT_shards, xT_shards, w_down, m, post_g):
    in_maps = []
    for cid in range(NCORES):
        b, q = core_bq(cid)
        in_maps.append({'actT': actT_shards[cid], 'w_down': w_down, 'xT': xT_shards[cid],
                        'fv': stack_vecs([m['gt_f'][b], post_g])})
    res = run('C2b', in_maps)
    return [r['xo'] for r in res]


class V:
    def __init__(self, ap, res):
        self.ap = ap
        self.res = list(res) if isinstance(res, (list, tuple)) and not (len(res) and isinstance(res[0], str) and False) else [res]

    def __getitem__(self, idx):
        v = V.__new__(V)
        v.ap = self.ap[idx]
        v.res = self.res
        return v

    def r(self, res):
        v = V.__new__(V)
        v.ap = self.ap
        v.res = list(res)
        return v


def _rs(*vs):
    out = []
    for v in vs:
        if isinstance(v, V):
            out.extend(v.res)
    return out


def _isps(r):
    return isinstance(r, str) and r.startswith('ps:')


def _rw(ins, outs):
    rs = _rs(*ins)
    ws = _rs(*outs)
    return [r for r in rs if not _isps(r)], ws + [r for r in rs if _isps(r)]


def _ap(v):
    return v.ap if isinstance(v, V) else v


class KM(KB):
    def vsb(self, name, shape, dt=F32):
        return V(self.sb(name, shape, dt)[:], [name])

    def vmm(self, out, lhsT, rhs, start=True, stop=True):
        self.mm(_ap(out), _ap(lhsT), _ap(rhs), start, stop, *_rw((lhsT, rhs), (out,)))

    def vtr(self, out, in_, ident):
        self.tr(_ap(out), _ap(in_), _ap(ident), *_rw((in_, ident), (out,)))

    def vact(self, out, in_, func, bias=None, scale=None, accum_out=None):
        self.act(_ap(out), _ap(in_), func, *_rw((in_, bias, scale), (out, accum_out)),
                 bias=_ap(bias) if bias is not None else None, scale=_ap(scale) if scale is not None else None,
                 accum_out=_ap(accum_out) if accum_out is not None else None)

    def vtt(self, eng, out, in0, in1, op):
        self.tt(eng, _ap(out), _ap(in0), _ap(in1), op, *_rw((in0, in1), (out,)))

    def vts(self, eng, out, in0, s1, s2, op0, op1=None):
        self.ts(eng, _ap(out), _ap(in0), _ap(s1), _ap(s2), op0, op1, *_rw((in0, s1, s2), (out,)))

    def vstt(self, out, in0, scalar, in1, op0, op1):
        self.stt(_ap(out), _ap(in0), _ap(scalar), _ap(in1), op0, op1, *_rw((in0, scalar, in1), (out,)))

    def vcopy(self, eng, out, in_):
        self.copy(eng, _ap(out), _ap(in_), *_rw((in_,), (out,)))

    def vrecip(self, out, in_):
        self.recip(_ap(out), _ap(in_), *_rw((in_,), (out,)))

    def vdma(self, q, key, out, in_):
        return self.S.dma(q, key, [(_ap(out), _ap(in_))], _rs(in_), _rs(out))

    def vreduce(self, out, in_, op):
        o, i = _ap(out), _ap(in_)
        self.S.op('dve', lambda e: e.tensor_reduce(out=o, in_=i, axis=AX.X, op=op), *_rw((in_,), (out,)))


NBLK = 8
QSCALE = 128 ** -0.5
NFM = 8
NTM1 = 768
NTM2 = 130


def build_M():
    kb = KM()
    hT = kb.din('hT', [D, S_LEN], BF16)
    wfm_d = kb.din('wfm', [D, NFM * 128])
    wtm_d = kb.din('wtm', [D, NTM1 + NTM2])
    cwq_d = kb.din('cwq', [128, 3, 4])
    dnc_d = kb.din('dnc', [128, 2])
    dng_d = kb.din('dng', [64, 128])
    bias_d = kb.din('biasT', [128, 640])
    maskb_d = kb.din('maskb', [128, 640])
    am_d = kb.din('amc', [128, 256])
    sink_d = kb.din('sink', [128, 1])
    sgg_d = kb.din('sgg', [128, 128])
    sgw_d = kb.din('sgwT', [128, 128])
    sgb_d = kb.din('sgb', [128, 512])
    tri_d = kb.din('tri', [128, 128])
    id_d = kb.din('ident', [128, 128], BF16)
    lt_d = kb.din('lt', [64, 64], BF16)
    m64_d = kb.din('m64', [64, 9, 64])
    oa_d = kb.dout('out_a', [S_LEN, 128], BF16)
    ob_d = kb.dout('out_b', [S_LEN, 128], BF16)
    oc_d = kb.dout('out_c', [S_LEN, 128], BF16)
    od_d = kb.dout('out_d', [128, S_LEN], BF16)

    def load(name, d, shape, dt=F32, q='sp'):
        v = kb.vsb(name, shape, dt)
        kb.vdma(q, 'ld_' + name, v, V(d, []))
        return v
    wfm = kb.vsb('wfm', [128, KC, NFM * 128], BF16)
    wtm = kb.vsb('wtm', [128, KC, NTM1 + NTM2], BF16)
    for h2 in range(2):
        kb.vdma('pool', f'ld_wfm{h2}', wfm[:, h2 * 8:(h2 + 1) * 8, :].r([('wfm', h2)]),
                V(fm(wfm_d)[:, h2 * 8:(h2 + 1) * 8, :], []))
        kb.vdma('pool', f'ld_wtm{h2}', wtm[:, h2 * 8:(h2 + 1) * 8, :].r([('wtm', h2)]),
                V(fm(wtm_d)[:, h2 * 8:(h2 + 1) * 8, :], []))
    wfm = wfm.r([('wfm', 0), ('wfm', 1)])
    wtm = wtm.r([('wtm', 0), ('wtm', 1)])
    cwq = load('cwq', cwq_d[:, :, :], [128, 3, 4])
    dnc = load('dnc', dnc_d[:, :], [128, 2])
    dng = load('dng', dng_d[:, :], [64, 128])
    biasT = load('biasT', bias_d[:, :], [128, 640])
    maskb = load('maskb', maskb_d[:, :], [128, 640])
    amc = load('amc', am_d[:, :], [128, 256])
    sgg = load('sgg', sgg_d[:, :], [128, 128])
    sgw = load('sgw', sgw_d[:, :], [128, 128])
    sgb = load('sgb', sgb_d[:, 0:128], [128, 128])
    tri = load('tri', tri_d[:, :], [128, 128])
    ident = load('ident', id_d[:, :], [128, 128], BF16)
    lt = load('lt', lt_d[:, :], [64, 64], BF16)
    m64 = load('m64', m64_d[:, :, :], [64, 9, 64])
    negC, strict, causal, id64 = m64[:, 0, :], m64[:, 1, :], m64[:, 2, :], m64[:, 3, :]
    md8, md8T = m64[:, 4, :], m64[:, 5, :]
    mkT = [m64[:, 6, :], m64[:, 7, :], m64[:, 8, :]]
    onesb = kb.vsb('onesb', [128, 128], BF16)
    kb.S.op('pool', lambda e: e.memset(onesb.ap, 1.0), (), onesb.res)
    onec = kb.vsb('onec', [128, 1])
    kb.S.op('pool', lambda e: e.memset(onec.ap, 1.0), (), onec.res)
    epsc = kb.vsb('epsc', [128, 1])
    kb.S.op('pool', lambda e: e.memset(epsc.ap, EPS), (), epsc.res)
    kb.vtt('dve', biasT, biasT, maskb, ALU.add)
    BM = biasT
    sgwb = kb.vsb('sgwb', [128, 128], BF16)
    kb.vtt('dve', sgwb, sgw, tri, ALU.mult)
    negA = kb.vsb('negA', [128, 1])
    kb.vact(negA, dnc[:, 0:1], AF.Exp)
    kb.vts('dve', negA, negA, -1.0, None, ALU.mult)
    dtb = dnc[:, 1:2]
    sC = kb.vsb('sC', [128, 257])
    kb.vdma('sp', 'ld_sink', sC[:, 0:1].r(['sC_sink']), V(sink_d[:, :], []))

    KTb = kb.vsb('KTb', [128, S_LEN], BF16)
    KTc = kb.vsb('KTc', [128, S_LEN], BF16)
    Vb = kb.vsb('Vb', [128, 32, 128], BF16)
    Vc = kb.vsb('Vc', [128, 32, 128], BF16)
    raw = [kb.vsb(f'raw{g}', [128, 515]) for g in range(3)]
    for g in range(3):
        kb.S.op('pool', (lambda e, a=raw[g].ap[:, 0:3]: e.memset(a, 0.0)), (), [(f'raw{g}', 'h')])
    Sst = kb.vsb('Sst', [128, 128])
    Sbf = kb.vsb('Sbf', [128, 128], BF16)
    kb.S.op('pool', lambda e: e.memset(Sst.ap, 0.0), (), Sst.res)
    kb.S.op('pool', lambda e: e.memset(Sbf.ap, 0.0), (), Sbf.res)

    pA = [V(kb.ps(f'pA{i}', [128, 512])[:], [f'ps:A{i}']) for i in range(2)]
    pS = kb.ps('pS', [128, 1024])
    SB_RES = ['ps:S0', 'ps:S1']
    pSG = V(pS[:, 640:768], ['ps:S1'])
    pSc = V(pS[:, 768:1024], ['ps:S1'])
    pT = kb.ps('pT', [128, 1024], BF16)
    pKV = [V(pT[0:64, 640 + 192 * p:768 + 192 * p], ['ps:T']) for p in range(2)]
    pNA = [V(pT[0:64, 768 + 192 * p:832 + 192 * p], ['ps:T']) for p in range(2)]
    pset = []
    for p in range(2):
        pb_ = kb.ps(f'pset{p}', [128, 512])
        r = [f'ps:P{p}']
        pset.append({'G': V(pb_[:, 0:64], r), 'KK': V(pb_[0:64, 64:128], r), 'QK': V(pb_[0:64, 128:192], r),
                     'Mk': V(pb_[0:64, 192:256], r), 'MkT': V(pb_[0:64, 256:320], r),
                     'Tu': V(pb_[0:64, 320:384], r), 'Tv': V(pb_[0:64, 384:448], r),
                     'U': V(pb_[0:64, 192:320], r), 'wT': V(pb_[:, 0:64], r)})
    p7 = kb.ps('p7', [128, 512])
    pU = V(p7[0:64, 0:128], ['ps:7'])
    pwT = V(p7[:, 128:192], ['ps:7'])
    pwS = V(p7[0:64, 192:320], ['ps:7'])
    pO = V(p7[0:64, 320:448], ['ps:7'])
    pgc = V(p7[0:64, 448:456], ['ps:7'])
    pgl = V(p7[:, 456:464], ['ps:7'])
    GV0 = 0
    npA = [0]

    def nextA():
        npA[0] += 1
        return pA[npA[0] % 2]
    pP = [pA[0], pA[1], V(pS[:, 0:512], ['ps:S0']), V(pS[:, 512:1024], ['ps:S1'])]
    npP = [0]

    def nextP():
        npP[0] += 1
        return pP[npP[0] % 4]

    hb = [kb.vsb(f'hb{i}', [128, KC, 512], BF16) for i in range(2)]
    QTb = kb.vsb('QTb', [128, 512], BF16)
    QTc = kb.vsb('QTc', [128, 512], BF16)
    uT = kb.vsb('uT', [128, 512])
    cacc = [kb.vsb(f'cacc{g}', [128, 512]) for g in range(3)]
    xs = cacc
    sqb2 = [kb.vsb(f'sqb{i}', [128, 512], BF16) for i in range(2)]
    rn2 = [kb.vsb(f'rn{i}', [128, 512]) for i in range(2)]
    QTn = kb.vsb('QTn', [128, 512], BF16)
    KTn = kb.vsb('KTn', [128, 512], BF16)
    VTa = kb.vsb('VTa', [128, 512], BF16)
    gv4 = kb.vsb('gv4', [128, 4, 512], BF16)
    bnst = kb.vsb('bnst', [128, 6])
    bnag = kb.vsb('bnag', [128, 2])
    lrs = kb.vsb('lrs', [128, 1])
    vn = kb.vsb('vn', [128, 128])
    vtok = [kb.vsb(f'vtok{i}', [128, 128], BF16) for i in range(2)]
    sgt = kb.vsb('sgt', [128, 512])
    odst = [kb.vsb(f'odst{i}', [128, 512], BF16) for i in range(2)]
    sB = kb.vsb('sB', [128, 640])
    pB = kb.vsb('pB', [128, 640], BF16)
    PTs = kb.vsb('PTs', [128, 640], BF16)
    mx = kb.vsb('mx', [128, 1])
    nmx = kb.vsb('nmx', [128, 1])
    rsum = kb.vsb('rsum', [128, 1])
    rrec = kb.vsb('rrec', [128, 1])
    pC = kb.vsb('pC', [128, 257], BF16)
    mxc = kb.vsb('mxc', [128, 1])
    nmxc = kb.vsb('nmxc', [128, 1])
    rsumc = kb.vsb('rsumc', [128, 1])
    rrecc = kb.vsb('rrecc', [128, 1])
    obst = [kb.vsb(f'obst{i}', [128, 4, 128], BF16) for i in range(2)]
    ocst = [kb.vsb(f'ocst{i}', [128, 4, 128], BF16) for i in range(2)]
    oast = [kb.vsb(f'oast{i}', [64, 8, 128], BF16) for i in range(2)]
    ba = kb.vsb('ba', [64, 8, 2])
    beta = kb.vsb('beta', [64, 8])
    nbeta = kb.vsb('nbeta', [64, 8])
    spx = kb.vsb('spx', [64, 8])
    spa = kb.vsb('spa', [64, 8])
    spe = kb.vsb('spe', [64, 8])
    la = kb.vsb('la', [64, 8])
    lahi = kb.vsb('lahi', [64, 8], BF16)
    lalo = kb.vsb('lalo', [64, 8], BF16)
    gcol = kb.vsb('gcol', [64, 8])
    egc = kb.vsb('egc', [64, 8])
    bgc = kb.vsb('bgc', [64, 8])
    kds = kb.vsb('kds', [64, 8])
    egl = kb.vsb('egl', [128, 8])
    sgate = kb.vsb('sgate', [64, 8, 128])
    def two(name, shape, dt=F32):
        return [kb.vsb(f'{name}{i}', shape, dt) for i in range(2)]
    dfm, E_, Es_, Ec_ = two('dfm', [64, 64]), two('E', [64, 64]), two('Es', [64, 64]), two('Ec', [64, 64])
    Nb, NTb = two('Nb', [64, 64], BF16), two('NTb', [64, 64], BF16)
    Ma, MaT = two('Ma', [64, 64], BF16), two('MaT', [64, 64], BF16)
    Mb, MbT = two('Mb', [64, 64], BF16), two('MbT', [64, 64], BF16)
    Tb_, TTb = two('Tb', [64, 64], BF16), two('TTb', [64, 64], BF16)
    N8, N8T = two('N8', [64, 64], BF16), two('N8T', [64, 64], BF16)
    BT = [two(f'BT{k}', [64, 64], BF16) for k in range(3)]
    Yb = two('Yb', [64, 64], BF16)
    vbt, kbg = two('vbt', [64, 128], BF16), two('kbg', [64, 128], BF16)
    kdec = [kb.vsb(f'kdec{i}', [64, 128], BF16) for i in range(4)]
    def four(name, shape, dt=F32):
        return [kb.vsb(f'{name}{i}', shape, dt) for i in range(4)]
    ut, wTb = four('ut', [64, 128]), four('wTb', [128, 64], BF16)
    attb, attT = two('attb', [64, 64], BF16), four('attT', [64, 64], BF16)
    EGr, qdT = two('EGr', [128, 64]), four('qdT', [128, 64], BF16)
    vnew = two('vnew', [64, 128], BF16)
    gng = four('gng', [64, 128])
    olg = two('olg', [64, 1])
    osq = two('osq', [64, 128])
    oss, orr = two('oss', [64, 1]), two('orr', [64, 1])

    hTv = fm(hT)
    oav = oa_d.rearrange('(n c p) d -> n p c d', c=8, p=64)
    obv = ob_d.rearrange('(n i p) d -> n p i d', i=4, p=128)
    ocv = oc_d.rearrange('(n i p) d -> n p i d', i=4, p=128)

    for tb in range(NBLK):
        h = hb[tb % 2]
        kb.vdma('sp', f'ld_h{tb % 2}', h, V(hTv[:, :, tb * 512:(tb + 1) * 512], []))
        bsl = slice(tb * 512, (tb + 1) * 512)
        for g in range(NFM):
            p = nextP()
            for kc in range(KC):
                kb.vmm(p, wfm[:, kc, g * 128:(g + 1) * 128], h[:, kc, :], kc == 0, kc == KC - 1)
            if g < 3:
                kb.vcopy('act', raw[g][:, 3:515].r([(f'raw{g}', 'c')]), p)
                rw = raw[g].r([(f'raw{g}', 'c'), (f'raw{g}', 'h')])
                kb.vts('dve', cacc[g], rw[:, 3:515], cwq[:, g, 3:4], None, ALU.mult)
                for k in (2, 1, 0):
                    kb.vstt(cacc[g], rw[:, k:k + 512], cwq[:, g, k:k + 1], cacc[g], ALU.mult, ALU.add)
                kb.S.op('pool', (lambda e, o=raw[g].ap[:, 0:3], i=raw[g].ap[:, 512:515]: e.tensor_copy(out=o, in_=i)),
                        [(f'raw{g}', 'c')], [(f'raw{g}', 'h')])
            elif g == 3:
                kb.vcopy('act', QTb, p)
            elif g == 4:
                kb.vcopy('act', KTb[:, bsl].r([('KTb', tb)]), p)
            elif g == 5:
                kb.vcopy('act', QTc, p)
            elif g == 6:
                kb.vcopy('act', KTc[:, bsl].r([('KTc', tb)]), p)
            else:
                kb.vact(uT, p, AF.Gelu_apprx_tanh)
        for i in range(4):
            j = tb * 4 + i
            tsl = slice(i * 128, (i + 1) * 128)
            p1 = nextP()
            for kc in range(KC):
                kb.vmm(p1, h[:, kc, tsl], wtm[:, kc, 0:512], kc == 0, kc == KC - 1)
            kb.vact(gv4[:, i, :].r([('gv4', i)]), p1, AF.Gelu_apprx_tanh)
            p2 = nextP()
            for kc in range(KC):
                kb.vmm(p2[:, 0:256], h[:, kc, tsl], wtm[:, kc, 512:768], kc == 0, kc == KC - 1)
            kb.vcopy('act', Vb[:, j, :].r([('Vb', j)]), p2[:, 0:128])
            kb.vcopy('act', Vc[:, j, :].r([('Vc', j)]), p2[:, 128:256])
        for c in range(8):
            p = nextP()
            pc = p[0:64, 0:NTM2]
            for kc in range(KC):
                kb.vmm(pc, h[:, kc, c * 64:(c + 1) * 64], wtm[:, kc, NTM1:NTM1 + NTM2], kc == 0, kc == KC - 1)
            kb.vact(sgate[:, c, :].r([('sgate', c)]), pc[:, 0:128], AF.Silu)
            kb.vcopy('dve', ba[:, c, :].r([('ba', c)]), pc[:, 128:130])
        kb.vact(xs[0], cacc[0], AF.Silu)
        kb.vact(xs[1], cacc[1], AF.Silu)
        kb.vact(VTa, cacc[2], AF.Silu)
        pq = []
        for g in range(2):
            kb.vact(sqb2[g], xs[g], AF.Square)
            p = nextP()
            kb.vmm(p, onesb, sqb2[g])
            pq.append(p)
        for g in range(2):
            kb.vact(rn2[g], pq[g], AF.Ln, bias=epsc, scale=1.0)
        for g in range(2):
            kb.vact(rn2[g], rn2[g], AF.Exp, scale=-0.5)
            kb.vtt('dve', QTn if g == 0 else KTn, xs[g], rn2[g], ALU.mult)
        bar = ba.r([('ba', c) for c in range(8)])
        kb.vact(beta, bar[:, :, 0], AF.Exp, scale=-1.0)
        kb.vts('dve', beta, beta, 1.0, None, ALU.add)
        kb.vrecip(beta, beta)
        kb.vts('dve', nbeta, beta, -1.0, None, ALU.mult)
        kb.vts('dve', spx, bar[:, :, 1], dtb[0:64, :], None, ALU.add)
        kb.vact(spa, spx, AF.Abs)
        kb.vact(spe, spa, AF.Exp, scale=-1.0)
        kb.vact(spe, spe, AF.Ln, bias=onec[0:64, :], scale=1.0)
        kb.vstt(spa, spx, 0.0, spe, ALU.max, ALU.add)
        kb.vts('dve', la, spa, negA[0:64, :], None, ALU.mult)
        kb.vcopy('dve', lahi, la)
        kb.vtt('dve', lalo, la, lahi, ALU.subtract)
        kb.vmm(pgc, lt, lahi, True, False)
        kb.vmm(pgc, lt, lalo, False, True)
        kb.vmm(pgl, onesb[0:64, :], lahi, True, False)
        kb.vmm(pgl, onesb[0:64, :], lalo, False, True)
        kb.vcopy('dve', gcol, pgc)
        kb.vact(egc, pgc, AF.Exp)
        kb.vact(egl, pgl, AF.Exp)
        kb.vtt('dve', kds, pgl[0:64, :], gcol, ALU.subtract)
        kb.vact(kds, kds, AF.Exp)
        kb.vtt('dve', bgc, beta, egc, ALU.mult)

        def chunk_par(c):
            n = tb * 8 + c
            q = n % 2
            q4 = n % 4
            ps_ = pset[q]
            csl = slice(c * 64, (c + 1) * 64)
            kb.vtr(pKV[q], KTn[:, csl], ident)
            kb.vts('dve', kbg[q], pKV[q], bgc[:, c:c + 1], None, ALU.mult)
            kb.vts('dve', kdec[q4], pKV[q], kds[:, c:c + 1], None, ALU.mult)
            kb.vmm(ps_['G'], V(lahi.ap[:, c:c + 1].to_broadcast([64, 128]), lahi.res), lt, True, False)
            kb.vmm(ps_['G'], V(lalo.ap[:, c:c + 1].to_broadcast([64, 128]), lalo.res), lt, False, True)
            kb.vmm(ps_['KK'], KTn[:, csl], KTn[:, csl])
            kb.vmm(ps_['QK'], QTn[:, csl], KTn[:, csl])
            yield
            kb.vtr(pKV[q], VTa[:, csl], ident)
            kb.vts('dve', vbt[q], pKV[q], beta[:, c:c + 1], None, ALU.mult)
            kb.vstt(dfm[q], ps_['G'][0:64, :], gcol[:, c:c + 1], negC, ALU.subtract, ALU.mult)
            kb.vact(EGr[q], ps_['G'], AF.Exp)
            yield
            kb.vact(E_[q], dfm[q], AF.Exp)
            kb.vstt(qdT[q4], QTn[:, csl], QSCALE, EGr[q], ALU.mult, ALU.mult)
            kb.vtt('pool', gng[q4], sgate[:, c, :].r([('sgate', c)]), dng, ALU.mult)
            yield
            kb.vtt('pool', Es_[q], E_[q], strict, ALU.mult)
            kb.vtt('pool', Ec_[q], E_[q], causal, ALU.mult)
            yield
            kb.vstt(Nb[q], ps_['KK'], nbeta[:, c:c + 1], Es_[q], ALU.mult, ALU.mult)
            kb.vstt(attb[q], ps_['QK'], QSCALE, Ec_[q], ALU.mult, ALU.mult)
            yield
            kb.vtr(pNA[q], Nb[q], ident[0:64, 0:64])
            kb.vtt('pool', N8[q], Nb[q], md8, ALU.mult)
            yield
            kb.vcopy('act', NTb[q], pNA[q])
            kb.vtt('pool', Tb_[q], N8[q], id64, ALU.add)
            yield
            kb.vtr(pNA[q], attb[q], ident[0:64, 0:64])
            kb.vtt('pool', N8T[q], NTb[q], md8T, ALU.mult)
            yield
            kb.vcopy('act', attT[q4], pNA[q])
            kb.vtt('pool', TTb[q], N8T[q], id64, ALU.add)
            for k in range(3):
                kb.vtt('pool', BT[k][q], NTb[q], mkT[k], ALU.mult)
            yield
            M, MT = N8[q], N8T[q]
            for lev in range(2):
                Mn, MnT = (Ma[q], MaT[q]) if lev == 0 else (Mb[q], MbT[q])
                kb.vmm(ps_['Mk'], MT, M)
                kb.vmm(ps_['MkT'], M, MT)
                yield
                kb.vcopy('act', Mn, ps_['Mk'])
                kb.vcopy('dve', MnT, ps_['MkT'])
                yield
                kb.vmm(ps_['Tu'], MnT, Tb_[q])
                kb.vmm(ps_['Tv'], Mn, TTb[q])
                yield
                kb.vtt('dve', Tb_[q], Tb_[q], ps_['Tu'], ALU.add)
                kb.vtt('dve', TTb[q], TTb[q], ps_['Tv'], ALU.add)
                yield
                M, MT = Mn, MnT
            for k in range(3):
                kb.vmm(ps_['Mk'], BT[k][q], Tb_[q])
                yield
                kb.vcopy('act', Yb[q], ps_['Mk'])
                yield
                if k < 2:
                    kb.vmm(ps_['Tu'], TTb[q], Yb[q])
                kb.vmm(ps_['MkT'], Yb[q], TTb[q])
                yield
                if k < 2:
                    kb.vtt('dve', Tb_[q], Tb_[q], ps_['Tu'], ALU.add)
                kb.vtt('dve', TTb[q], TTb[q], ps_['MkT'], ALU.add)
                yield
            kb.vmm(ps_['U'], TTb[q], vbt[q])
            kb.vmm(ps_['wT'], kbg[q], TTb[q])
            yield
            kb.vcopy('act', ut[q4], ps_['U'])
            kb.vcopy('act', wTb[q4], ps_['wT'])
            yield

        def chunk_seq(c):
            n = tb * 8 + c
            q = n % 2
            q4 = n % 4
            kb.vmm(pwS, wTb[q4], Sbf)
            yield
            kb.vtt('dve', vnew[q], ut[q4], pwS, ALU.subtract)
            yield
            yield
            kb.vmm(pO, qdT[q4], Sbf, True, False)
            kb.vmm(pO, attT[q4], vnew[q], False, True)
            pn = nextA()
            pSn = pn[:, 0:128]
            kb.vmm(pSn, kdec[q4], vnew[q])
            yield
            kb.vstt(Sbf, Sst, egl[:, c:c + 1], pSn, ALU.mult, ALU.add)
            kb.vstt(Sst, Sst, egl[:, c:c + 1], pSn, ALU.mult, ALU.add)
            yield
            kb.vact(osq[q], pO, AF.Square, accum_out=oss[q])
            kb.vact(olg[q], oss[q], AF.Ln, bias=epsc[0:64, :], scale=1.0 / 128)
            kb.vact(orr[q], olg[q], AF.Exp, scale=-0.5)
            kb.vstt(oast[tb % 2][:, c, :].r([(f'oast{tb % 2}', c)]), pO, orr[q], gng[q4], ALU.mult, ALU.mult)
            yield

        def tile_work(i):
            j = tb * 4 + i
            tsl = slice(i * 128, (i + 1) * 128)
            gvi = gv4[:, i, :].r([('gv4', i)])
            kb.S.op('dve', (lambda e, o=bnst.ap, a=gvi.ap: e.bn_stats(out=o, in_=a)), gvi.res, bnst.res)
            kb.S.op('dve', (lambda e, o=bnag.ap, a=bnst.ap: e.bn_aggr(out=o, in_=a)), bnst.res, bnag.res)
            kb.vact(lrs, bnag[:, 1:2], AF.Ln, bias=epsc, scale=1.0)
            kb.vact(lrs, lrs, AF.Exp, scale=-0.5)
            yield
            kb.vts('dve', vn, gvi[:, GV0:GV0 + 128], bnag[:, 0:1], lrs, ALU.subtract, ALU.mult)
            vt = vtok[i % 2]
            kb.vtt('pool', vt, vn, sgg, ALU.mult)
            yield
            yield
            kb.vmm(pSG, vt, sgwb)
            kb.vtt('dve', sgt[:, tsl].r([('sgt', i)]), pSG, sgb[:, 0:128], ALU.add)
            kb.vtt('pool', odst[tb % 2][:, tsl].r([(f'odst{tb % 2}', i)]), sgt[:, tsl].r([('sgt', i)]), uT[:, tsl], ALU.mult)
            yield
            kt0 = max(0, j - 4)
            nk = j + 1 - kt0
            W = nk * 128
            off = (5 - nk) * 128
            kres = [('KTb', t) for t in range(kt0 * 128 // 512, tb + 1)]
            for (a, b2) in ((0, min(W, 512)), (512, W)):
                if b2 > a:
                    kb.vmm(V(pS[:, a:b2], ['ps:S0' if a == 0 else 'ps:S1']), QTb[:, tsl],
                           KTb[:, kt0 * 128 + a:kt0 * 128 + b2].r(kres))
            kb.vstt(sB[:, 0:W], V(pS[:, 0:W], SB_RES if W > 512 else ['ps:S0']), QSCALE, BM[:, off:off + W],
                    ALU.mult, ALU.add)
            yield
            kb.vreduce(mx, sB[:, 0:W], ALU.max)
            kb.vts('dve', nmx, mx, -1.0, None, ALU.mult)
            yield
            kb.vact(pB[:, 0:W], sB[:, 0:W], AF.Exp, bias=nmx, scale=1.0, accum_out=rsum)
            yield
            yield
            for t in range(nk):
                kb.vtr(V(pT[:, t * 128:(t + 1) * 128], ['ps:T']), pB[:, t * 128:(t + 1) * 128], ident)
            kb.vcopy('dve', PTs[:, 0:W], V(pT[:, 0:W], ['ps:T']))
            yield
            yield
            pn = nextA()
            pOb = pn[:, 0:128]
            for t in range(nk):
                kb.vmm(pOb, PTs[:, t * 128:(t + 1) * 128], Vb[:, kt0 + t, :].r([('Vb', kt0 + t)]), t == 0, t == nk - 1)
            kb.vrecip(rrec, rsum)
            kb.vts('dve', obst[tb % 2][:, i, :].r([(f'obst{tb % 2}', i)]), pOb, rrec, None, ALU.mult)
            yield
            kt0 = max(0, j - 1)
            nk = j + 1 - kt0
            W = nk * 128
            off = (2 - nk) * 128
            kres = [('KTc', t) for t in range(kt0 * 128 // 512, tb + 1)]
            kb.vmm(pSc[:, 0:W], QTc[:, tsl], KTc[:, kt0 * 128:kt0 * 128 + W].r(kres))
            kb.vstt(sC[:, 1:1 + W].r(['sC']), pSc[:, 0:W], QSCALE, amc[:, off:off + W], ALU.mult, ALU.add)
            scr = sC[:, 0:1 + W].r(['sC', 'sC_sink'])
            yield
            kb.vreduce(mxc, scr, ALU.max)
            kb.vts('dve', nmxc, mxc, -1.0, None, ALU.mult)
            yield
            kb.vact(pC[:, 0:1 + W], scr, AF.Exp, bias=nmxc, scale=1.0, accum_out=rsumc)
            yield
            yield
            for t in range(nk):
                kb.vtr(V(pT[:, t * 128:(t + 1) * 128], ['ps:T']), pC[:, 1 + t * 128:1 + (t + 1) * 128], ident)
            kb.vcopy('dve', PTs[:, 0:W], V(pT[:, 0:W], ['ps:T']))
            yield
            yield
            pn = nextA()
            pOc = pn[:, 0:128]
            for t in range(nk):
                kb.vmm(pOc, PTs[:, t * 128:(t + 1) * 128], Vc[:, kt0 + t, :].r([('Vc', kt0 + t)]), t == 0, t == nk - 1)
            kb.vrecip(rrecc, rsumc)
            kb.vts('dve', ocst[tb % 2][:, i, :].r([(f'ocst{tb % 2}', i)]), pOc, rrecc, None, ALU.mult)
            yield

        def seq_pair(i):
            for c in (2 * i, 2 * i + 1):
                yield from chunk_seq(c)

        def rr(gens):
            gens = list(gens)
            while gens:
                for g in list(gens):
                    try:
                        next(g)
                    except StopIteration:
                        gens.remove(g)

        for i in range(4):
            gl = [chunk_par(2 * i), chunk_par(2 * i + 1), tile_work(i)]
            if i > 0:
                gl.append(seq_pair(i - 1))
            rr(gl)
        rr([seq_pair(3)])
        kb.out_toks.append(kb.vdma('sp', f'st_d{tb % 2}', V(od_d[:, bsl], []),
                                   odst[tb % 2].r([(f'odst{tb % 2}', i) for i in range(4)])))
        kb.out_toks.append(kb.vdma('sp', f'st_a{tb % 2}', V(oav[tb], []),
                                   oast[tb % 2].r([(f'oast{tb % 2}', c) for c in range(8)])))
        kb.out_toks.append(kb.vdma('sp', f'st_b{tb % 2}', V(obv[tb], []),
                                   obst[tb % 2].r([(f'obst{tb % 2}', i) for i in range(4)])))
        kb.out_toks.append(kb.vdma('sp', f'st_c{tb % 2}', V(ocv[tb], []),
                                   ocst[tb % 2].r([(f'ocst{tb % 2}', i) for i in range(4)])))
    return kb.finish()


OFF = {'a_q': 0, 'a_k': 512, 'a_v': 1024, 'a_gate': 1536, 'a_beta': 2048, 'a_alpha': 2052, 'b_q': 2056,
       'b_k': 2568, 'b_v': 3080, 'c_q': 3592, 'c_k': 4104, 'c_v': 4360, 'd_u': 4616, 'd_v': 5128}


def m_consts():
    q = np.arange(128)[:, None]
    kk = np.arange(640)[None, :]
    hi = (q >= 64).astype(np.int64)
    validb = (kk // 64 >= hi) & (kk // 64 <= 8 + hi)
    maskb = np.where(validb, 0.0, -30000.0).astype(np.float32)
    idxb = np.clip(512 + q - kk, -256, 256) + 256
    kc = np.arange(256)[None, :]
    validc = (kc // 64 >= hi) & (kc // 64 <= 2 + hi)
    distc = np.abs(128 + q - kc).astype(np.float32)
    i64 = np.arange(64)
    cge = (i64[:, None] >= i64[None, :])
    ci, si = i64[:, None], i64[None, :]
    md8 = ((ci // 8 == si // 8) & (ci > si)).astype(np.float32)

    def mk(b):
        return ((ci // (2 * b) == si // (2 * b)) & (ci % (2 * b) >= b) & (si % (2 * b) < b)).astype(np.float32)
    m64 = np.stack([np.where(cge, -1.0, 0.0), (ci > si).astype(np.float32), cge.astype(np.float32), np.eye(64),
                    md8, md8.T, mk(8).T, mk(16).T, mk(32).T], axis=1).astype(np.float32)
    i128 = np.arange(128)
    return {
        'maskb': maskb, 'idxb': idxb, 'validc': validc, 'distc': distc,
        'm64': np.ascontiguousarray(m64),
        'lt': (i64[:, None] <= i64[None, :]).astype(NPBF),
        'ident': np.eye(128).astype(NPBF),
        'tri': (i128[:, None] <= i128[None, :]).astype(np.float32),
    }


def host_M(hT_shards, P, l):
    C = m_consts()
    w_in = P['w_in'][l]
    in_maps = []
    for cid in range(NCORES):
        b, hd = cid // 4, cid % 4
        kvh = hd // 2

        def cols(name, h, n=128):
            return w_in[:, OFF[name] + h * n:OFF[name] + (h + 1) * n]
        wfm = np.concatenate([cols('a_q', hd), cols('a_k', hd), cols('a_v', hd), cols('b_q', hd), cols('b_k', hd),
                              cols('c_q', hd), cols('c_k', kvh), cols('d_u', hd)], axis=1)
        others = [g for g in range(4) if g != hd]
        wtm = np.concatenate([cols('d_v', hd)] + [cols('d_v', g) for g in others] +
                             [cols('b_v', hd), cols('c_v', kvh), cols('a_gate', hd),
                              w_in[:, OFF['a_beta'] + hd:OFF['a_beta'] + hd + 1],
                              w_in[:, OFF['a_alpha'] + hd:OFF['a_alpha'] + hd + 1]], axis=1)
        cw = P['dn_conv_w'][l]
        cwq = np.stack([cw[:, g * 512 + hd * 128:g * 512 + (hd + 1) * 128].T for g in range(3)], axis=1)
        slope = np.float32(2.0 ** (-8.0 * (hd + 1) / 4))
        amc = np.where(C['validc'], -slope * C['distc'], np.float32(-30000.0)).astype(np.float32)
        in_maps.append({
            'hT': np.ascontiguousarray(np.concatenate(hT_shards[b * 4:(b + 1) * 4], axis=1)),
            'wfm': np.ascontiguousarray(wfm), 'wtm': np.ascontiguousarray(wtm),
            'cwq': np.ascontiguousarray(cwq.astype(np.float32)),
            'dnc': np.ascontiguousarray(np.broadcast_to(
                np.array([P['dn_a_log'][l][hd], P['dn_dt_bias'][l][hd]], np.float32)[None, :], (128, 2))),
            'dng': np.ascontiguousarray(np.broadcast_to(P['dn_norm_g'][l][None, :], (64, 128))),
            'biasT': np.ascontiguousarray(P['rel_bias'][l][hd][C['idxb']]),
            'maskb': C['maskb'], 'amc': amc,
            'sink': np.full((128, 1), P['sinks'][l][hd], np.float32),
            'sgg': np.ascontiguousarray(np.broadcast_to(P['sgu_norm_g'][l][hd * 128:(hd + 1) * 128][None, :], (128, 128))),
            'sgwT': np.ascontiguousarray(P['sgu_w'][l][hd].T),
            'sgb': np.ascontiguousarray(np.broadcast_to(np.tile(P['sgu_b'][l][hd], 4)[None, :], (128, 512))),
            'tri': C['tri'], 'ident': C['ident'], 'lt': C['lt'], 'm64': C['m64'],
        })
    res = run('M', in_maps)
    shards = []
    for cid in range(NCORES):
        b, q = core_bq(cid)
        tsl = slice(q * TOK, (q + 1) * TOK)
        rows = []
        for nm in ('out_a', 'out_b', 'out_c'):
            for hd in range(4):
                rows.append(res[b * 4 + hd][nm][tsl, :].T)
        for hd in range(4):
            rows.append(res[b * 4 + hd]['out_d'][:, tsl])
        shards.append(np.ascontiguousarray(np.concatenate(rows, axis=0)))
    return shards


def kernel(x, c, ada_w, ada_b, mix_pre_g, mix_post_g, w_in, dn_conv_w, dn_a_log, dn_dt_bias, dn_norm_g,
           rel_bias, sinks, sgu_norm_g, sgu_w, sgu_b, w_out, ffn_pre_g, ffn_post_g, ffn_w_up, ffn_conv_w,
           ffn_conv_b, ffn_w_down):
    f = lambda a: np.asarray(a, dtype=np.float32)
    x, c, ada_w, ada_b = f(x), f(c), f(ada_w), f(ada_b)
    P = {'w_in': f(w_in), 'dn_conv_w': f(dn_conv_w), 'dn_a_log': f(dn_a_log), 'dn_dt_bias': f(dn_dt_bias),
         'dn_norm_g': f(dn_norm_g), 'rel_bias': f(rel_bias), 'sinks': f(sinks), 'sgu_norm_g': f(sgu_norm_g),
         'sgu_w': f(sgu_w), 'sgu_b': f(sgu_b)}
    mix_pre_g, mix_post_g, ffn_pre_g, ffn_post_g = f(mix_pre_g), f(mix_post_g), f(ffn_pre_g), f(ffn_post_g)
    w_out, ffn_w_up, ffn_conv_w, ffn_conv_b, ffn_w_down = f(w_out), f(ffn_w_up), f(ffn_conv_w), f(ffn_conv_b), f(ffn_w_down)
    mod = host_L0(c, ada_w, ada_b)
    xs = [tok_shard_T(x, cid) for cid in range(NCORES)]
    for l in range(2):
        m = split_mod(mod[l])
        hs = host_N1(xs, mix_pre_g[l], m)
        mo = host_M(hs, P, l)
        xs, h2 = host_C1(xs, mo, np.ascontiguousarray(w_out[l]), m, mix_post_g[l], ffn_pre_g[l])
        acts = host_C2a(h2, np.ascontiguousarray(ffn_w_up[l]), ffn_conv_w[l], ffn_conv_b[l])
        xs = host_C2b(acts, xs, np.ascontiguousarray(ffn_w_down[l]), m, ffn_post_g[l])
    out = np.zeros((2, S_LEN, D), np.float32)
    for cid in range(NCORES):
        b, q = core_bq(cid)
        out[b, q * TOK:(q + 1) * TOK, :] = xs[cid].T
    return out
```

```python
import numpy as np
import ml_dtypes
from contextlib import ExitStack
import concourse.bass as bass
import concourse.mybir as mybir
from concourse.bass_utils import run_bass_kernel_spmd

F32 = mybir.dt.float32
BF16 = mybir.dt.bfloat16
AF = mybir.ActivationFunctionType
ALU = mybir.AluOpType
AX = mybir.AxisListType
NPBF = ml_dtypes.bfloat16

NCORES = 8
D = 2048
KC = 16
S_LEN = 4096
TOK = 1024
DFF = 5504
NFF = 43
EPS = 1e-6
ENG = ['pe', 'dve', 'act', 'pool', 'sp']


class Sched:
    def __init__(self, nc, stack):
        self.nc = nc
        self.stack = stack
        self.plan = {e: [] for e in ENG}
        self.cnt = {e: 0 for e in ENG}
        self.seen = {e: {} for e in ENG}
        self.sems = {}
        self.dcnt = {}
        self.res = {}
        for e in ENG:
            self.sems[e] = stack.enter_context(nc.semaphore('s_' + e))

    def _st(self, r):
        st = self.res.get(r)
        if st is None:
            st = {'w': None, 'r': {}}
            self.res[r] = st
        return st

    def _deps(self, eng, reads, writes):
        deps = {}

        def add(k, v):
            if deps.get(k, 0) < v:
                deps[k] = v
        for r in reads:
            w = self._st(r)['w']
            if w is not None:
                add(*w)
        for wr in writes:
            st = self._st(wr)
            if st['w'] is not None:
                add(*st['w'])
            for k, v in st['r'].items():
                if k == eng:
                    continue
                add(k, v)
        return deps

    def _waits(self, eng, deps):
        waits = []
        for k, v in deps.items():
            if k == eng and eng == 'pe':
                continue
            if self.seen[eng].get(k, 0) >= v:
                continue
            self.seen[eng][k] = v
            waits.append((k, v))
        return waits

    def _commit(self, tok, reads, writes):
        k, v = tok
        for r in reads:
            st = self._st(r)
            if st['r'].get(k, 0) < v:
                st['r'][k] = v
        for w in writes:
            st = self._st(w)
            st['w'] = tok
            st['r'] = {}

    def op(self, eng, fn, reads=(), writes=()):
        deps = self._deps(eng, reads, writes)
        waits = self._waits(eng, deps)
        self.cnt[eng] += 1
        tok = (eng, self.cnt[eng])
        self.plan[eng].append((waits, fn, eng, 1))
        self._commit(tok, reads, writes)

    def dma(self, q, key, pairs, reads=(), writes=()):
        if key not in self.sems:
            self.sems[key] = self.stack.enter_context(self.nc.semaphore('d_' + str(key)))
            self.dcnt[key] = 0
        deps = self._deps(q, reads, writes)
        waits = self._waits(q, deps)
        for i, (o, a) in enumerate(pairs):
            self.dcnt[key] += 16
            self.plan[q].append((waits if i == 0 else [],
                                 (lambda e, o=o, a=a: e.dma_start(out=o, in_=a)), key, 16))
        tok = (key, self.dcnt[key])
        self._commit(tok, reads, writes)
        return tok

    def wait_tok(self, eng, tok):
        waits = self._waits(eng, {tok[0]: tok[1]})
        if waits:
            self.plan[eng].append((waits, None, None, 0))

    def emit(self):
        nc = self.nc
        engs = {'pe': 'tensor', 'dve': 'vector', 'act': 'scalar', 'pool': 'gpsimd', 'sp': 'sync'}
        with nc.Block() as block:
            for e in ENG:
                plan = self.plan[e]
                if not plan:
                    continue

                def body(engine, plan=plan):
                    for waits, fn, k, inc in plan:
                        for (wk, wv) in waits:
                            engine.wait_ge(self.sems[wk], wv)
                        if fn is not None:
                            fn(engine).then_inc(self.sems[k], inc)
                getattr(block, engs[e])(body)


class KB:
    def __init__(self):
        self.nc = bass.Bass("TRN2", target_bir_lowering=False)
        self.st = ExitStack()
        self.S = Sched(self.nc, self.st)
        self.out_toks = []
        self.nbank = 0

    def din(self, name, shape, dt=F32):
        return self.nc.dram_tensor(name, list(shape), dt, kind="ExternalInput").ap()

    def dout(self, name, shape, dt=F32):
        return self.nc.dram_tensor(name, list(shape), dt, kind="ExternalOutput").ap()

    def sb(self, name, shape, dt=F32):
        return self.st.enter_context(self.nc.sbuf_tensor('sb_' + name, list(shape), dt))

    def ps(self, name, shape, dt=F32):
        return self.st.enter_context(self.nc.psum_tensor('ps_' + name, list(shape), dt))

    def finish(self):
        for t in self.out_toks:
            self.S.wait_tok('sp', t)
        self.S.emit()
        self.st.close()
        return self.nc

    def mm(self, out, lhsT, rhs, start, stop, r, w):
        self.S.op('pe', lambda e: e.matmul(out, lhsT=lhsT, rhs=rhs, start=start, stop=stop), r, w)

    def tr(self, out, in_, ident, r, w):
        self.S.op('pe', lambda e: e.transpose(out=out, in_=in_, identity=ident), r, w)

    def act(self, out, in_, func, r, w, bias=None, scale=None, accum_out=None):
        kw = {}
        if bias is not None:
            kw['bias'] = bias
        if scale is not None:
            kw['scale'] = scale
        if accum_out is not None:
            kw['accum_out'] = accum_out
        self.S.op('act', lambda e: e.activation(out=out, in_=in_, func=func, **kw), r, w)

    def tt(self, eng, out, in0, in1, op, r, w):
        self.S.op(eng, lambda e: e.tensor_tensor(out=out, in0=in0, in1=in1, op=op), r, w)

    def ts(self, eng, out, in0, s1, s2, op0, op1, r, w, accum_out=None):
        if op1 is None:
            self.S.op(eng, lambda e: e.tensor_scalar(out=out, in0=in0, scalar1=s1, scalar2=None, op0=op0), r, w)
        elif accum_out is None:
            self.S.op(eng, lambda e: e.tensor_scalar(out=out, in0=in0, scalar1=s1, scalar2=s2, op0=op0, op1=op1), r, w)
        else:
            self.S.op(eng, lambda e: e.tensor_scalar(out=out, in0=in0, scalar1=s1, scalar2=s2, op0=op0, op1=op1,
                                                     accum_out=accum_out), r, w)

    def stt(self, out, in0, scalar, in1, op0, op1, r, w):
        self.S.op('dve', lambda e: e.scalar_tensor_tensor(out=out, in0=in0, scalar=scalar, in1=in1,
                                                          op0=op0, op1=op1), r, w)

    def copy(self, eng, out, in_, r, w):
        if eng == 'act':
            self.S.op('act', lambda e: e.copy(out=out, in_=in_), r, w)
        else:
            self.S.op(eng, lambda e: e.tensor_copy(out=out, in_=in_), r, w)

    def memset(self, eng, ap, val, w):
        self.S.op(eng, lambda e: e.memset(ap, val), (), w)

    def recip(self, out, in_, r, w):
        self.S.op('dve', lambda e: e.reciprocal(out=out, in_=in_), r, w)

    def dma(self, q, key, out, in_, r=(), w=()):
        return self.S.dma(q, key, [(out, in_)], r, w)


def fm(ap):
    return ap.rearrange('(kc p) t -> p kc t', p=128)


def vec_fm(v):
    v = np.asarray(v)
    return np.ascontiguousarray(v.reshape(-1, 128).T)


def setup_consts(kb):
    ones = kb.sb('ones', [128, 128], BF16)
    kb.memset('pool', ones[:], 1.0, ['ones'])
    epst = kb.sb('epst', [128, 1])
    kb.memset('pool', epst[:], EPS, ['epst'])
    return ones, epst


def rstd_from_ss(kb, ssp, ssp_res, out, out_res, epst, n, tag):
    kb.act(out, ssp, AF.Sqrt, [ssp_res, 'epst'], [out_res], bias=epst[:], scale=1.0 / n)
    kb.recip(out, out, [out_res], [out_res])


def norm_bufs(kb):
    return {
        'ssp': kb.ps('nm_ssp', [128, 512]),
        'sq': [kb.sb(f'nm_sq{i}', [128, 512], BF16) for i in range(2)],
        't': [kb.sb(f'nm_t{i}', [128, 512]) for i in range(2)],
        'rstd': kb.sb('nm_rstd', [128, 512]),
    }


def norm_mod_half(kb, nb, xt, xres, hf, gs, gsres, sh, shres, ones, epst, hb, hbres):
    sl = slice(hf * 512, (hf + 1) * 512)
    ssp, rstd = nb['ssp'], nb['rstd']
    for kc in range(KC):
        q = nb['sq'][kc % 2]
        qn = f'nm_sq{kc % 2}'
        kb.act(q[:], xt[:, kc, sl], AF.Square, [(xres, kc, hf)], [qn])
        kb.mm(ssp[:], ones[:], q[:], kc == 0, kc == KC - 1, ['ones', qn], ['nm_ssp'])
    rstd_from_ss(kb, ssp[:], 'nm_ssp', rstd[:], 'nm_rstd', epst, D, 'nm')
    for kc in range(KC):
        t = nb['t'][kc % 2]
        tn = f'nm_t{kc % 2}'
        kb.stt(t[:], xt[:, kc, sl], gs[:, kc:kc + 1], rstd[:], ALU.mult, ALU.mult,
               [(xres, kc, hf), gsres, 'nm_rstd'], [tn])
        kb.act(hb[:, kc, :], t[:], AF.Identity, [tn, shres], [(hbres, kc)], bias=sh[:, kc:kc + 1])


def build_L0():
    kb = KB()
    cT = kb.din('cT', [128, KC, 2])
    W = kb.din('W', [D, 3072])
    bias = kb.din('bias', [1, 3072])
    mod = kb.dout('mod', [2, 3072])
    sc = kb.sb('sc', [128, KC, 2])
    scs = kb.sb('scs', [128, KC, 2], BF16)
    bt = kb.sb('bt', [2, 3072])
    ot = kb.sb('ot', [2, 3072])
    wb = [kb.sb(f'wb{i}', [128, KC, 512], BF16) for i in range(2)]
    pb = [kb.ps(f'pb{i}', [2, 512]) for i in range(2)]
    kb.dma('sp', 'ld_c', sc[:], cT[:, :, :], w=['sc'])
    kb.dma('sp', 'ld_b', bt[:], bias.partition_broadcast(2), w=['bt'])
    kb.act(scs[:], sc[:], AF.Silu, ['sc'], ['scs'])
    Wv = fm(W)
    for j in range(6):
        i = j % 2
        kb.dma('pool', f'ld_w{i}', wb[i][:], Wv[:, :, j * 512:(j + 1) * 512], w=[f'wb{i}'])
        for kc in range(KC):
            kb.mm(pb[i][:], scs[:, kc, :], wb[i][:, kc, :], kc == 0, kc == KC - 1, ['scs', f'wb{i}'], [f'pb{i}'])
        kb.tt('dve', ot[:, j * 512:(j + 1) * 512], pb[i][:], bt[:, j * 512:(j + 1) * 512], ALU.add,
              [f'pb{i}', 'bt'], [('ot', j)])
    kb.out_toks.append(kb.dma('sp', 'st', mod[:, :], ot[:], r=[('ot', j) for j in range(6)]))
    return kb.finish()


def build_N1():
    kb = KB()
    xT = kb.din('xT', [D, TOK])
    nv = kb.din('nv', [128, 3, KC])
    hT = kb.dout('hT', [D, TOK], BF16)
    ones, epst = setup_consts(kb)
    xt = kb.sb('xt', [128, KC, TOK])
    nvt = kb.sb('nvt', [128, 3, KC])
    gs = kb.sb('gs', [128, KC])
    kb.dma('sp', 'ld_nv', nvt[:], nv[:, :, :], w=['nv'])
    xv = fm(xT)
    for hf in range(2):
        for g4 in range(4):
            kb.dma('sp', f'ld_x{hf}{g4}', xt[:, g4 * 4:(g4 + 1) * 4, hf * 512:(hf + 1) * 512],
                   xv[:, g4 * 4:(g4 + 1) * 4, hf * 512:(hf + 1) * 512],
                   w=[('x', kc, hf) for kc in range(g4 * 4, g4 * 4 + 4)])
    kb.ts('dve', gs[:], nvt[:, 1, :], 1.0, None, ALU.add, None, ['nv'], ['gs'])
    kb.tt('dve', gs[:], gs[:], nvt[:, 0, :], ALU.mult, ['gs', 'nv'], ['gs'])
    nb = norm_bufs(kb)
    hbs = [kb.sb(f'hb{i}', [128, KC, 512], BF16) for i in range(2)]
    hv = fm(hT)
    for hf in range(2):
        norm_mod_half(kb, nb, xt, 'x', hf, gs, 'gs', nvt[:, 2, :], 'nv', ones, epst, hbs[hf], f'hb{hf}')
        kb.out_toks.append(kb.dma('sp', f'st_h{hf}', hv[:, :, hf * 512:(hf + 1) * 512], hbs[hf][:],
                                  r=[(f'hb{hf}', kc) for kc in range(KC)]))
    return kb.finish()


def build_C1():
    kb = KB()
    xT = kb.din('xT', [D, TOK])
    moT = kb.din('moT', [D, TOK], BF16)
    w_out = kb.din('w_out', [D, D])
    cv = kb.din('cv', [128, 5, KC])
    xo = kb.dout('xo', [D, TOK])
    h2T = kb.dout('h2T', [D, TOK], BF16)
    ones, epst = setup_consts(kb)
    xt = kb.sb('xt', [128, KC, TOK])
    mo = kb.sb('mo', [128, KC, TOK], BF16)
    cvt = kb.sb('cvt', [128, 5, KC])
    gtg = kb.sb('gtg', [128, KC])
    gs = kb.sb('gs', [128, KC])
    kb.dma('sp', 'ld_cv', cvt[:], cv[:, :, :], w=['cv'])
    xv = fm(xT)
    mv = fm(moT)
    for hf in range(2):
        for g4 in range(4):
            kb.dma('sp', f'ld_m{hf}{g4}', mo[:, g4 * 4:(g4 + 1) * 4, hf * 512:(hf + 1) * 512],
                   mv[:, g4 * 4:(g4 + 1) * 4, hf * 512:(hf + 1) * 512],
                   w=[('mo', kc, hf) for kc in range(g4 * 4, g4 * 4 + 4)])
    for hf in range(2):
        for g4 in range(4):
            kb.dma('sp', f'ld_x{hf}{g4}', xt[:, g4 * 4:(g4 + 1) * 4, hf * 512:(hf + 1) * 512],
                   xv[:, g4 * 4:(g4 + 1) * 4, hf * 512:(hf + 1) * 512],
                   w=[('x', kc, hf) for kc in range(g4 * 4, g4 * 4 + 4)])
    kb.tt('dve', gtg[:], cvt[:, 0, :], cvt[:, 1, :], ALU.mult, ['cv'], ['gtg'])
    kb.ts('dve', gs[:], cvt[:, 3, :], 1.0, None, ALU.add, None, ['cv'], ['gs'])
    kb.tt('dve', gs[:], gs[:], cvt[:, 2, :], ALU.mult, ['gs', 'cv'], ['gs'])
    wg = [kb.sb(f'wg{i}', [128, KC, 256], BF16) for i in range(2)]
    yh = kb.sb('yh', [128, KC, 512])
    psy = [kb.ps(f'psy{i}', [128, 512]) for i in range(3)]
    ssy = kb.ps('ssy', [128, 512])
    sqb = [kb.sb(f'sq{i}', [128, 512], BF16) for i in range(2)]
    tb = [kb.sb(f'tb{i}', [128, 512]) for i in range(2)]
    rstd = kb.sb('rstd', [128, 512])
    nb = norm_bufs(kb)
    hbs = [kb.sb(f'hb{i}', [128, KC, 512], BF16) for i in range(2)]
    wv = fm(w_out)
    xov = fm(xo)
    hv = fm(h2T)
    nld = 0
    for hf in range(2):
        sl = slice(hf * 512, (hf + 1) * 512)
        pend = None
        for dc in range(KC):
            if dc % 2 == 0:
                wi = nld % 2
                nld += 1
                kb.dma('pool', f'ld_w{wi}', wg[wi][:], wv[:, :, dc * 128:dc * 128 + 256], w=[f'wg{wi}'])
            wcur = wg[wi]
            p = psy[dc % 3]
            pn = f'psy{dc % 3}'
            for kc in range(KC):
                kb.mm(p[:], wcur[:, kc, (dc % 2) * 128:(dc % 2) * 128 + 128], mo[:, kc, sl], kc == 0, kc == KC - 1,
                      [f'wg{wi}', ('mo', kc, hf)], [pn])
            kb.copy('act', yh[:, dc, :], p[:], [pn], [('yh', dc)])
            q = sqb[dc % 2]
            qn = f'sq{dc % 2}'
            kb.act(q[:], p[:], AF.Square, [pn], [qn])
            if pend is not None:
                pend()
            pend = (lambda q=q, qn=qn, dc=dc: kb.mm(ssy[:], ones[:], q[:], dc == 0, dc == KC - 1, ['ones', qn], ['ssy']))
        pend()
        rstd_from_ss(kb, ssy[:], 'ssy', rstd[:], 'rstd', epst, D, 'y')
        for dc in range(KC):
            t = tb[dc % 2]
            tn = f'tb{dc % 2}'
            kb.stt(t[:], yh[:, dc, :], gtg[:, dc:dc + 1], rstd[:], ALU.mult, ALU.mult,
                   [('yh', dc), 'gtg', 'rstd'], [tn])
            kb.tt('dve', xt[:, dc, sl], xt[:, dc, sl], t[:], ALU.add, [('x', dc, hf), tn], [('x', dc, hf)])
        kb.out_toks.append(kb.dma('sp', f'st_x{hf}', xov[:, :, sl], xt[:, :, sl],
                                  r=[('x', dc, hf) for dc in range(KC)]))
        norm_mod_half(kb, nb, xt, 'x', hf, gs, 'gs', cvt[:, 4, :], 'cv', ones, epst, hbs[hf], f'hb{hf}')
        kb.out_toks.append(kb.dma('sp', f'st_h{hf}', hv[:, :, sl], hbs[hf][:],
                                  r=[(f'hb{hf}', kc) for kc in range(KC)]))
    return kb.finish()


def build_C2a():
    kb = KB()
    h2e = kb.din('h2e', [D, TOK + 2], BF16)
    w_up = kb.din('w_up', [D, 2 * DFF])
    cw = kb.din('cw', [128, NFF, 4])
    actT = kb.dout('actT', [DFF, TOK], BF16)
    he = kb.sb('he', [128, KC, TOK + 2], BF16)
    cwt = kb.sb('cwt', [128, NFF, 4])
    kb.dma('sp', 'ld_cw', cwt[:], cw[:, :, :], w=['cw'])
    hv = fm(h2e)
    for g4 in range(4):
        kb.dma('sp', f'ld_h{g4}', he[:, g4 * 4:(g4 + 1) * 4, :], hv[:, g4 * 4:(g4 + 1) * 4, :],
               w=[('he', kc) for kc in range(g4 * 4, g4 * 4 + 4)])
    wa = [kb.sb(f'wa{i}', [128, KC, 256], BF16) for i in range(2)]
    wgt = [kb.sb(f'wgt{i}', [128, KC, 256], BF16) for i in range(2)]
    psa = [[kb.ps(f'psa{i}{s}', [128, 512]) for s in range(3)] for i in range(2)]
    psg = [kb.ps(f'psg{s}', [128, 512]) for s in range(2)]
    aext = [kb.sb(f'aext{i}', [128, TOK + 2]) for i in range(2)]
    gS = [kb.sb(f'gS{i}', [128, TOK]) for i in range(2)]
    acc = [kb.sb(f'acc{i}', [128, TOK]) for i in range(2)]
    ga = [kb.sb(f'ga{i}', [128, TOK]) for i in range(2)]
    ab = [kb.sb(f'ab{i}', [128, 2, TOK], BF16) for i in range(2)]
    wv = fm(w_up)
    av = actT.rearrange('(j p) t -> p j t', p=128)
    W3 = 342
    for jb in range(22):
        bi = jb % 2
        ncol = 256 if jb < 21 else 128
        kb.dma('pool', f'ld_wa{bi}', wa[bi][:, :, 0:ncol], wv[:, :, jb * 256:jb * 256 + ncol], w=[f'wa{bi}'])
        kb.dma('pool', f'ld_wg{bi}', wgt[bi][:, :, 0:ncol], wv[:, :, DFF + jb * 256:DFF + jb * 256 + ncol],
               w=[f'wgt{bi}'])
        njj = ncol // 128
        for jj in range(njj):
            j = jb * 2 + jj
            i = j % 2
            cs = slice(jj * 128, jj * 128 + 128)
            for s in range(3):
                for kc in range(KC):
                    kb.mm(psa[i][s][:, 0:W3], wa[bi][:, kc, cs], he[:, kc, s * W3:(s + 1) * W3], kc == 0, kc == KC - 1,
                          [f'wa{bi}', ('he', kc)], [f'psa{i}{s}'])
            for s in range(2):
                for kc in range(KC):
                    kb.mm(psg[s][:], wgt[bi][:, kc, cs], he[:, kc, 2 + s * 512:2 + (s + 1) * 512], kc == 0, kc == KC - 1,
                          [f'wgt{bi}', ('he', kc)], [f'psg{s}'])
            for s in range(3):
                kb.copy('act', aext[i][:, s * W3:(s + 1) * W3], psa[i][s][:, 0:W3], [f'psa{i}{s}'], [(f'aext{i}', s)])
            for s in range(2):
                kb.copy('act', gS[i][:, s * 512:(s + 1) * 512], psg[s][:], [f'psg{s}'], [(f'gS{i}', s)])
            ar = [(f'aext{i}', s) for s in range(3)]
            kb.ts('dve', acc[i][:], aext[i][:, 2:TOK + 2], cwt[:, j, 2:3], cwt[:, j, 3:4], ALU.mult, ALU.add,
                  ar + ['cw'], [f'acc{i}'])
            kb.stt(acc[i][:], aext[i][:, 1:TOK + 1], cwt[:, j, 1:2], acc[i][:], ALU.mult, ALU.add,
                   ar + ['cw', f'acc{i}'], [f'acc{i}'])
            kb.stt(acc[i][:], aext[i][:, 0:TOK], cwt[:, j, 0:1], acc[i][:], ALU.mult, ALU.add,
                   ar + ['cw', f'acc{i}'], [f'acc{i}'])
            kb.act(ga[i][:], acc[i][:], AF.Gelu_apprx_tanh, [f'acc{i}'], [f'ga{i}'])
            kb.tt('dve', ab[bi][:, jj, :], ga[i][:], gS[i][:], ALU.mult,
                  [f'ga{i}', (f'gS{i}', 0), (f'gS{i}', 1)], [(f'ab{bi}', jj)])
        kb.out_toks.append(kb.dma('sp', f'st_a{bi}', av[:, jb * 2:jb * 2 + njj, :], ab[bi][:, 0:njj, :],
                                  r=[(f'ab{bi}', jj) for jj in range(njj)]))
    return kb.finish()


def build_C2b():
    kb = KB()
    actT = kb.din('actT', [DFF, TOK], BF16)
    w_down = kb.din('w_down', [DFF, D])
    xT = kb.din('xT', [D, TOK])
    fv = kb.din('fv', [128, 2, KC])
    xo = kb.dout('xo', [D, TOK])
    ones, epst = setup_consts(kb)
    at = kb.sb('at', [128, NFF, TOK], BF16)
    y2 = kb.sb('y2', [128, KC, TOK])
    fvt = kb.sb('fvt', [128, 2, KC])
    gtg = kb.sb('gtg', [128, KC])
    kb.dma('sp', 'ld_fv', fvt[:], fv[:, :, :], w=['fv'])
    kb.tt('dve', gtg[:], fvt[:, 0, :], fvt[:, 1, :], ALU.mult, ['fv'], ['gtg'])
    av = actT.rearrange('(j p) t -> p j t', p=128)
    for g in range(0, NFF, 8):
        n = min(8, NFF - g)
        kb.dma('sp', f'ld_a{g}', at[:, g:g + n, :], av[:, g:g + n, :], w=[('at', j) for j in range(g, g + n)])
    wd = [kb.sb(f'wd{i}', [128, NFF, 128], BF16) for i in range(2)]
    psy = [kb.ps(f'psy{i}', [128, 512]) for i in range(3)]
    ss = [kb.ps(f'ss{i}', [128, 512]) for i in range(2)]
    sqb = [kb.sb(f'sq{i}', [128, 512], BF16) for i in range(2)]
    rstd = kb.sb('rstd', [128, TOK])
    wv = w_down.rearrange('(j p) n -> p j n', p=128)
    n = 0
    pend = None
    for dc in range(KC):
        wi = dc % 2
        kb.dma('pool', f'ld_w{wi}', wd[wi][:], wv[:, :, dc * 128:(dc + 1) * 128], w=[f'wd{wi}'])
        for hf in range(2):
            p = psy[n % 3]
            pn = f'psy{n % 3}'
            for j in range(NFF):
                kb.mm(p[:], wd[wi][:, j, :], at[:, j, hf * 512:(hf + 1) * 512], j == 0, j == NFF - 1,
                      [f'wd{wi}', ('at', j)], [pn])
            kb.copy('act', y2[:, dc, hf * 512:(hf + 1) * 512], p[:], [pn], [('y2', dc, hf)])
            q = sqb[n % 2]
            qn = f'sq{n % 2}'
            kb.act(q[:], p[:], AF.Square, [pn], [qn])
            if pend is not None:
                pend()
            pend = (lambda q=q, qn=qn, dc=dc, hf=hf: kb.mm(ss[hf][:], ones[:], q[:], dc == 0, dc == KC - 1,
                                                         ['ones', qn], [f'ss{hf}']))
            n += 1
    pend()
    for hf in range(2):
        rstd_from_ss(kb, ss[hf][:], f'ss{hf}', rstd[:, hf * 512:(hf + 1) * 512], ('rstd', hf), epst, D, f'y{hf}')
    xin = [kb.sb(f'xin{i}', [128, TOK]) for i in range(4)]
    xv = fm(xT)
    xov = fm(xo)
    for dc in range(KC):
        xi = dc % 4
        kb.dma('sp', f'ld_x{xi}', xin[xi][:], xv[:, dc, :], w=[f'xin{xi}'])
        yr = [('y2', dc, 0), ('y2', dc, 1)]
        kb.stt(y2[:, dc, :], y2[:, dc, :], gtg[:, dc:dc + 1], rstd[:], ALU.mult, ALU.mult,
               yr + ['gtg', ('rstd', 0), ('rstd', 1)], yr)
        kb.tt('dve', y2[:, dc, :], y2[:, dc, :], xin[xi][:], ALU.add, yr + [f'xin{xi}'], yr)
        kb.out_toks.append(kb.dma('act' if dc % 2 else 'sp', f'st_x{dc % 4}', xov[:, dc, :], y2[:, dc, :], r=yr))
    return kb.finish()


_CACHE = {}


def get_prog(name):
    if name not in _CACHE:
        _CACHE[name] = {'L0': build_L0, 'N1': build_N1, 'C1': build_C1, 'C2a': build_C2a, 'C2b': build_C2b}[name]()
    return _CACHE[name]


def run(name, in_maps):
    nc = {'L0': build_L0, 'N1': build_N1, 'C1': build_C1, 'C2a': build_C2a, 'C2b': build_C2b, 'M': build_M}[name]()
    res = run_bass_kernel_spmd(nc, in_maps, core_ids=list(range(NCORES)))
    return res.results


def core_bq(cid):
    return cid // 4, cid % 4


def tok_shard_T(a, cid):
    b, q = core_bq(cid)
    return np.ascontiguousarray(a[b, q * TOK:(q + 1) * TOK, :].T)


def stack_vecs(vs):
    return np.ascontiguousarray(np.stack([vec_fm(v) for v in vs], axis=1).astype(np.float32))


def host_L0(c, ada_w, ada_b):
    cT = np.ascontiguousarray(c.T.reshape(KC, 128, 2).transpose(1, 0, 2))
    in_maps = []
    for cid in range(NCORES):
        l, j = cid // 4, cid % 4
        in_maps.append({'cT': cT, 'W': np.ascontiguousarray(ada_w[l][:, j * 3072:(j + 1) * 3072]),
                        'bias': np.ascontiguousarray(ada_b[l][None, j * 3072:(j + 1) * 3072])})
    res = run('L0', in_maps)
    mod = np.zeros((2, 2, 6 * D), np.float32)
    for cid in range(NCORES):
        l, j = cid // 4, cid % 4
        mod[l, :, j * 3072:(j + 1) * 3072] = res[cid]['mod']
    return mod


def split_mod(mod_l):
    names = ['sh_m', 'sc_m', 'gt_m', 'sh_f', 'sc_f', 'gt_f']
    return {n: mod_l[:, i * D:(i + 1) * D] for i, n in enumerate(names)}


def host_N1(xT_shards, g, m):
    in_maps = []
    for cid in range(NCORES):
        b, q = core_bq(cid)
        in_maps.append({'xT': xT_shards[cid], 'nv': stack_vecs([g, m['sc_m'][b], m['sh_m'][b]])})
    res = run('N1', in_maps)
    return [r['hT'] for r in res]


def host_C1(xT_shards, moT_shards, w_out, m, post_g, pre_g):
    in_maps = []
    for cid in range(NCORES):
        b, q = core_bq(cid)
        in_maps.append({'xT': xT_shards[cid], 'moT': moT_shards[cid], 'w_out': w_out,
                        'cv': stack_vecs([m['gt_m'][b], post_g, pre_g, m['sc_f'][b], m['sh_f'][b]])})
    res = run('C1', in_maps)
    return [r['xo'] for r in res], [r['h2T'] for r in res]


def host_C2a(h2T_shards, w_up, conv_w, conv_b):
    cw = np.zeros((128, NFF, 4), np.float32)
    for k in range(3):
        cw[:, :, k] = vec_fm(conv_w[k])
    cw[:, :, 3] = vec_fm(conv_b)
    in_maps = []
    for cid in range(NCORES):
        b, q = core_bq(cid)
        if q == 0:
            halo = np.zeros((D, 2), NPBF)
        else:
            halo = h2T_shards[cid - 1][:, -2:]
        in_maps.append({'h2e': np.ascontiguousarray(np.concatenate([halo, h2T_shards[cid]], axis=1)),
                        'w_up': w_up, 'cw': cw})
    res = run('C2a', in_maps)
    return [r['actT'] for r in res]


def host_C2b(actT_shards, xT_shards, w_down, m, post_g):
    in_maps = []
    for cid in range(NCORES):
        b, q = core_bq(cid)
        in_maps.append({'actT': actT_shards[cid], 'w_down': w_down, 'xT': xT_shards[cid],
                        'fv': stack_vecs([m['gt_f'][b], post_g])})
    res = run('C2b', in_maps)
    return [r['xo'] for r in res]


class V:
    def __init__(self, ap, res):
        self.ap = ap
        self.res = list(res) if isinstance(res, (list, tuple)) and not (len(res) and isinstance(res[0], str) and False) else [res]

    def __getitem__(self, idx):
        v = V.__new__(V)
        v.ap = self.ap[idx]
        v.res = self.res
        return v

    def r(self, res):
        v = V.__new__(V)
        v.ap = self.ap
        v.res = list(res)
        return v


def _rs(*vs):
    out = []
    for v in vs:
        if isinstance(v, V):
            out.extend(v.res)
    return out


def _isps(r):
    return isinstance(r, str) and r.startswith('ps:')


def _rw(ins, outs):
    rs = _rs(*ins)
    ws = _rs(*outs)
    return [r for r in rs if not _isps(r)], ws + [r for r in rs if _isps(r)]


def _ap(v):
    return v.ap if isinstance(v, V) else v


class KM(KB):
    def vsb(self, name, shape, dt=F32):
        return V(self.sb(name, shape, dt)[:], [name])

    def vmm(self, out, lhsT, rhs, start=True, stop=True):
        self.mm(_ap(out), _ap(lhsT), _ap(rhs), start, stop, *_rw((lhsT, rhs), (out,)))

    def vtr(self, out, in_, ident):
        self.tr(_ap(out), _ap(in_), _ap(ident), *_rw((in_, ident), (out,)))

    def vact(self, out, in_, func, bias=None, scale=None, accum_out=None):
        self.act(_ap(out), _ap(in_), func, *_rw((in_, bias, scale), (out, accum_out)),
                 bias=_ap(bias) if bias is not None else None, scale=_ap(scale) if scale is not None else None,
                 accum_out=_ap(accum_out) if accum_out is not None else None)

    def vtt(self, eng, out, in0, in1, op):
        self.tt(eng, _ap(out), _ap(in0), _ap(in1), op, *_rw((in0, in1), (out,)))

    def vts(self, eng, out, in0, s1, s2, op0, op1=None):
        self.ts(eng, _ap(out), _ap(in0), _ap(s1), _ap(s2), op0, op1, *_rw((in0, s1, s2), (out,)))

    def vstt(self, out, in0, scalar, in1, op0, op1):
        self.stt(_ap(out), _ap(in0), _ap(scalar), _ap(in1), op0, op1, *_rw((in0, scalar, in1), (out,)))

    def vcopy(self, eng, out, in_):
        self.copy(eng, _ap(out), _ap(in_), *_rw((in_,), (out,)))

    def vrecip(self, out, in_):
        self.recip(_ap(out), _ap(in_), *_rw((in_,), (out,)))

    def vdma(self, q, key, out, in_):
        return self.S.dma(q, key, [(_ap(out), _ap(in_))], _rs(in_), _rs(out))

    def vreduce(self, out, in_, op):
        o, i = _ap(out), _ap(in_)
        self.S.op('dve', lambda e: e.tensor_reduce(out=o, in_=i, axis=AX.X, op=op), *_rw((in_,), (out,)))


NBLK = 8
QSCALE = 128 ** -0.5
NFM = 8
NTM1 = 768
NTM2 = 130


def build_M():
    kb = KM()
    hT = kb.din('hT', [D, S_LEN], BF16)
    wfm_d = kb.din('wfm', [D, NFM * 128])
    wtm_d = kb.din('wtm', [D, NTM1 + NTM2])
    cwq_d = kb.din('cwq', [128, 3, 4])
    dnc_d = kb.din('dnc', [128, 2])
    dng_d = kb.din('dng', [64, 128])
    bias_d = kb.din('biasT', [128, 640])
    maskb_d = kb.din('maskb', [128, 640])
    am_d = kb.din('amc', [128, 256])
    sink_d = kb.din('sink', [128, 1])
    sgg_d = kb.din('sgg', [128, 128])
    sgw_d = kb.din('sgwT', [128, 128])
    sgb_d = kb.din('sgb', [128, 512])
    tri_d = kb.din('tri', [128, 128])
    id_d = kb.din('ident', [128, 128], BF16)
    lt_d = kb.din('lt', [64, 64], BF16)
    m64_d = kb.din('m64', [64, 9, 64])
    oa_d = kb.dout('out_a', [S_LEN, 128], BF16)
    ob_d = kb.dout('out_b', [S_LEN, 128], BF16)
    oc_d = kb.dout('out_c', [S_LEN, 128], BF16)
    od_d = kb.dout('out_d', [128, S_LEN], BF16)

    def load(name, d, shape, dt=F32, q='sp'):
        v = kb.vsb(name, shape, dt)
        kb.vdma(q, 'ld_' + name, v, V(d, []))
        return v
    wfm = kb.vsb('wfm', [128, KC, NFM * 128], BF16)
    wtm = kb.vsb('wtm', [128, KC, NTM1 + NTM2], BF16)
    for h2 in range(2):
        kb.vdma('pool', f'ld_wfm{h2}', wfm[:, h2 * 8:(h2 + 1) * 8, :].r([('wfm', h2)]),
                V(fm(wfm_d)[:, h2 * 8:(h2 + 1) * 8, :], []))
        kb.vdma('pool', f'ld_wtm{h2}', wtm[:, h2 * 8:(h2 + 1) * 8, :].r([('wtm', h2)]),
                V(fm(wtm_d)[:, h2 * 8:(h2 + 1) * 8, :], []))
    wfm = wfm.r([('wfm', 0), ('wfm', 1)])
    wtm = wtm.r([('wtm', 0), ('wtm', 1)])
    cwq = load('cwq', cwq_d[:, :, :], [128, 3, 4])
    dnc = load('dnc', dnc_d[:, :], [128, 2])
    dng = load('dng', dng_d[:, :], [64, 128])
    biasT = load('biasT', bias_d[:, :], [128, 640])
    maskb = load('maskb', maskb_d[:, :], [128, 640])
    amc = load('amc', am_d[:, :], [128, 256])
    sgg = load('sgg', sgg_d[:, :], [128, 128])
    sgw = load('sgw', sgw_d[:, :], [128, 128])
    sgb = load('sgb', sgb_d[:, 0:128], [128, 128])
    tri = load('tri', tri_d[:, :], [128, 128])
    ident = load('ident', id_d[:, :], [128, 128], BF16)
    lt = load('lt', lt_d[:, :], [64, 64], BF16)
    m64 = load('m64', m64_d[:, :, :], [64, 9, 64])
    negC, strict, causal, id64 = m64[:, 0, :], m64[:, 1, :], m64[:, 2, :], m64[:, 3, :]
    md8, md8T = m64[:, 4, :], m64[:, 5, :]
    mkT = [m64[:, 6, :], m64[:, 7, :], m64[:, 8, :]]
    onesb = kb.vsb('onesb', [128, 128], BF16)
    kb.S.op('pool', lambda e: e.memset(onesb.ap, 1.0), (), onesb.res)
    onec = kb.vsb('onec', [128, 1])
    kb.S.op('pool', lambda e: e.memset(onec.ap, 1.0), (), onec.res)
    epsc = kb.vsb('epsc', [128, 1])
    kb.S.op('pool', lambda e: e.memset(epsc.ap, EPS), (), epsc.res)
    kb.vtt('dve', biasT, biasT, maskb, ALU.add)
    BM = biasT
    sgwb = kb.vsb('sgwb', [128, 128], BF16)
    kb.vtt('dve', sgwb, sgw, tri, ALU.mult)
    negA = kb.vsb('negA', [128, 1])
    kb.vact(negA, dnc[:, 0:1], AF.Exp)
    kb.vts('dve', negA, negA, -1.0, None, ALU.mult)
    dtb = dnc[:, 1:2]
    sC = kb.vsb('sC', [128, 257])
    kb.vdma('sp', 'ld_sink', sC[:, 0:1].r(['sC_sink']), V(sink_d[:, :], []))

    KTb = kb.vsb('KTb', [128, S_LEN], BF16)
    KTc = kb.vsb('KTc', [128, S_LEN], BF16)
    Vb = kb.vsb('Vb', [128, 32, 128], BF16)
    Vc = kb.vsb('Vc', [128, 32, 128], BF16)
    raw = [kb.vsb(f'raw{g}', [128, 515]) for g in range(3)]
    for g in range(3):
        kb.S.op('pool', (lambda e, a=raw[g].ap[:, 0:3]: e.memset(a, 0.0)), (), [(f'raw{g}', 'h')])
    Sst = kb.vsb('Sst', [128, 128])
    Sbf = kb.vsb('Sbf', [128, 128], BF16)
    kb.S.op('pool', lambda e: e.memset(Sst.ap, 0.0), (), Sst.res)
    kb.S.op('pool', lambda e: e.memset(Sbf.ap, 0.0), (), Sbf.res)

    pA = [V(kb.ps(f'pA{i}', [128, 512])[:], [f'ps:A{i}']) for i in range(2)]
    pS = kb.ps('pS', [128, 1024])
    SB_RES = ['ps:S0', 'ps:S1']
    pSG = V(pS[:, 640:768], ['ps:S1'])
    pSc = V(pS[:, 768:1024], ['ps:S1'])
    pT = kb.ps('pT', [128, 1024], BF16)
    pKV = [V(pT[0:64, 640 + 192 * p:768 + 192 * p], ['ps:T']) for p in range(2)]
    pNA = [V(pT[0:64, 768 + 192 * p:832 + 192 * p], ['ps:T']) for p in range(2)]
    pset = []
    for p in range(2):
        pb_ = kb.ps(f'pset{p}', [128, 512])
        r = [f'ps:P{p}']
        pset.append({'G': V(pb_[:, 0:64], r), 'KK': V(pb_[0:64, 64:128], r), 'QK': V(pb_[0:64, 128:192], r),
                     'Mk': V(pb_[0:64, 192:256], r), 'MkT': V(pb_[0:64, 256:320], r),
                     'Tu': V(pb_[0:64, 320:384], r), 'Tv': V(pb_[0:64, 384:448], r),
                     'U': V(pb_[0:64, 192:320], r), 'wT': V(pb_[:, 0:64], r)})
    p7 = kb.ps('p7', [128, 512])
    pU = V(p7[0:64, 0:128], ['ps:7'])
    pwT = V(p7[:, 128:192], ['ps:7'])
    pwS = V(p7[0:64, 192:320], ['ps:7'])
    pO = V(p7[0:64, 320:448], ['ps:7'])
    pgc = V(p7[0:64, 448:456], ['ps:7'])
    pgl = V(p7[:, 456:464], ['ps:7'])
    GV0 = 0
    npA = [0]

    def nextA():
        npA[0] += 1
        return pA[npA[0] % 2]
    pP = [pA[0], pA[1], V(pS[:, 0:512], ['ps:S0']), V(pS[:, 512:1024], ['ps:S1'])]
    npP = [0]

    def nextP():
        npP[0] += 1
        return pP[npP[0] % 4]

    hb = [kb.vsb(f'hb{i}', [128, KC, 512], BF16) for i in range(2)]
    QTb = kb.vsb('QTb', [128, 512], BF16)
    QTc = kb.vsb('QTc', [128, 512], BF16)
    uT = kb.vsb('uT', [128, 512])
    cacc = [kb.vsb(f'cacc{g}', [128, 512]) for g in range(3)]
    xs = cacc
    sqb2 = [kb.vsb(f'sqb{i}', [128, 512], BF16) for i in range(2)]
    rn2 = [kb.vsb(f'rn{i}', [128, 512]) for i in range(2)]
    QTn = kb.vsb('QTn', [128, 512], BF16)
    KTn = kb.vsb('KTn', [128, 512], BF16)
    VTa = kb.vsb('VTa', [128, 512], BF16)
    gv4 = kb.vsb('gv4', [128, 4, 512], BF16)
    bnst = kb.vsb('bnst', [128, 6])
    bnag = kb.vsb('bnag', [128, 2])
    lrs = kb.vsb('lrs', [128, 1])
    vn = kb.vsb('vn', [128, 128])
    vtok = [kb.vsb(f'vtok{i}', [128, 128], BF16) for i in range(2)]
    sgt = kb.vsb('sgt', [128, 512])
    odst = [kb.vsb(f'odst{i}', [128, 512], BF16) for i in range(2)]
    sB = kb.vsb('sB', [128, 640])
    pB = kb.vsb('pB', [128, 640], BF16)
    PTs = kb.vsb('PTs', [128, 640], BF16)
    mx = kb.vsb('mx', [128, 1])
    nmx = kb.vsb('nmx', [128, 1])
    rsum = kb.vsb('rsum', [128, 1])
    rrec = kb.vsb('rrec', [128, 1])
    pC = kb.vsb('pC', [128, 257], BF16)
    mxc = kb.vsb('mxc', [128, 1])
    nmxc = kb.vsb('nmxc', [128, 1])
    rsumc = kb.vsb('rsumc', [128, 1])
    rrecc = kb.vsb('rrecc', [128, 1])
    obst = [kb.vsb(f'obst{i}', [128, 4, 128], BF16) for i in range(2)]
    ocst = [kb.vsb(f'ocst{i}', [128, 4, 128], BF16) for i in range(2)]
    oast = [kb.vsb(f'oast{i}', [64, 8, 128], BF16) for i in range(2)]
    ba = kb.vsb('ba', [64, 8, 2])
    beta = kb.vsb('beta', [64, 8])
    nbeta = kb.vsb('nbeta', [64, 8])
    spx = kb.vsb('spx', [64, 8])
    spa = kb.vsb('spa', [64, 8])
    spe = kb.vsb('spe', [64, 8])
    la = kb.vsb('la', [64, 8])
    lahi = kb.vsb('lahi', [64, 8], BF16)
    lalo = kb.vsb('lalo', [64, 8], BF16)
    gcol = kb.vsb('gcol', [64, 8])
    egc = kb.vsb('egc', [64, 8])
    bgc = kb.vsb('bgc', [64, 8])
    kds = kb.vsb('kds', [64, 8])
    egl = kb.vsb('egl', [128, 8])
    sgate = kb.vsb('sgate', [64, 8, 128])
    def two(name, shape, dt=F32):
        return [kb.vsb(f'{name}{i}', shape, dt) for i in range(2)]
    dfm, E_, Es_, Ec_ = two('dfm', [64, 64]), two('E', [64, 64]), two('Es', [64, 64]), two('Ec', [64, 64])
    Nb, NTb = two('Nb', [64, 64], BF16), two('NTb', [64, 64], BF16)
    Ma, MaT = two('Ma', [64, 64], BF16), two('MaT', [64, 64], BF16)
    Mb, MbT = two('Mb', [64, 64], BF16), two('MbT', [64, 64], BF16)
    Tb_, TTb = two('Tb', [64, 64], BF16), two('TTb', [64, 64], BF16)
    N8, N8T = two('N8', [64, 64], BF16), two('N8T', [64, 64], BF16)
    BT = [two(f'BT{k}', [64, 64], BF16) for k in range(3)]
    Yb = two('Yb', [64, 64], BF16)
    vbt, kbg = two('vbt', [64, 128], BF16), two('kbg', [64, 128], BF16)
    kdec = [kb.vsb(f'kdec{i}', [64, 128], BF16) for i in range(4)]
    def four(name, shape, dt=F32):
        return [kb.vsb(f'{name}{i}', shape, dt) for i in range(4)]
    ut, wTb = four('ut', [64, 128]), four('wTb', [128, 64], BF16)
    attb, attT = two('attb', [64, 64], BF16), four('attT', [64, 64], BF16)
    EGr, qdT = two('EGr', [128, 64]), four('qdT', [128, 64], BF16)
    vnew = two('vnew', [64, 128], BF16)
    gng = four('gng', [64, 128])
    olg = two('olg', [64, 1])
    osq = two('osq', [64, 128])
    oss, orr = two('oss', [64, 1]), two('orr', [64, 1])

    hTv = fm(hT)
    oav = oa_d.rearrange('(n c p) d -> n p c d', c=8, p=64)
    obv = ob_d.rearrange('(n i p) d -> n p i d', i=4, p=128)
    ocv = oc_d.rearrange('(n i p) d -> n p i d', i=4, p=128)

    for tb in range(NBLK):
        h = hb[tb % 2]
        kb.vdma('sp', f'ld_h{tb % 2}', h, V(hTv[:, :, tb * 512:(tb + 1) * 512], []))
        bsl = slice(tb * 512, (tb + 1) * 512)
        for g in range(NFM):
            p = nextP()
            for kc in range(KC):
                kb.vmm(p, wfm[:, kc, g * 128:(g + 1) * 128], h[:, kc, :], kc == 0, kc == KC - 1)
            if g < 3:
                kb.vcopy('act', raw[g][:, 3:515].r([(f'raw{g}', 'c')]), p)
                rw = raw[g].r([(f'raw{g}', 'c'), (f'raw{g}', 'h')])
                kb.vts('dve', cacc[g], rw[:, 3:515], cwq[:, g, 3:4], None, ALU.mult)
                for k in (2, 1, 0):
                    kb.vstt(cacc[g], rw[:, k:k + 512], cwq[:, g, k:k + 1], cacc[g], ALU.mult, ALU.add)
                kb.S.op('pool', (lambda e, o=raw[g].ap[:, 0:3], i=raw[g].ap[:, 512:515]: e.tensor_copy(out=o, in_=i)),
                        [(f'raw{g}', 'c')], [(f'raw{g}', 'h')])
            elif g == 3:
                kb.vcopy('act', QTb, p)
            elif g == 4:
                kb.vcopy('act', KTb[:, bsl].r([('KTb', tb)]), p)
            elif g == 5:
                kb.vcopy('act', QTc, p)
            elif g == 6:
                kb.vcopy('act', KTc[:, bsl].r([('KTc', tb)]), p)
            else:
                kb.vact(uT, p, AF.Gelu_apprx_tanh)
        for i in range(4):
            j = tb * 4 + i
            tsl = slice(i * 128, (i + 1) * 128)
            p1 = nextP()
            for kc in range(KC):
                kb.vmm(p1, h[:, kc, tsl], wtm[:, kc, 0:512], kc == 0, kc == KC - 1)
            kb.vact(gv4[:, i, :].r([('gv4', i)]), p1, AF.Gelu_apprx_tanh)
            p2 = nextP()
            for kc in range(KC):
                kb.vmm(p2[:, 0:256], h[:, kc, tsl], wtm[:, kc, 512:768], kc == 0, kc == KC - 1)
            kb.vcopy('act', Vb[:, j, :].r([('Vb', j)]), p2[:, 0:128])
            kb.vcopy('act', Vc[:, j, :].r([('Vc', j)]), p2[:, 128:256])
        for c in range(8):
            p = nextP()
            pc = p[0:64, 0:NTM2]
            for kc in range(KC):
                kb.vmm(pc, h[:, kc, c * 64:(c + 1) * 64], wtm[:, kc, NTM1:NTM1 + NTM2], kc == 0, kc == KC - 1)
            kb.vact(sgate[:, c, :].r([('sgate', c)]), pc[:, 0:128], AF.Silu)
            kb.vcopy('dve', ba[:, c, :].r([('ba', c)]), pc[:, 128:130])
        kb.vact(xs[0], cacc[0], AF.Silu)
        kb.vact(xs[1], cacc[1], AF.Silu)
        kb.vact(VTa, cacc[2], AF.Silu)
        pq = []
        for g in range(2):
            kb.vact(sqb2[g], xs[g], AF.Square)
            p = nextP()
            kb.vmm(p, onesb, sqb2[g])
            pq.append(p)
        for g in range(2):
            kb.vact(rn2[g], pq[g], AF.Ln, bias=epsc, scale=1.0)
        for g in range(2):
            kb.vact(rn2[g], rn2[g], AF.Exp, scale=-0.5)
            kb.vtt('dve', QTn if g == 0 else KTn, xs[g], rn2[g], ALU.mult)
        bar = ba.r([('ba', c) for c in range(8)])
        kb.vact(beta, bar[:, :, 0], AF.Exp, scale=-1.0)
        kb.vts('dve', beta, beta, 1.0, None, ALU.add)
        kb.vrecip(beta, beta)
        kb.vts('dve', nbeta, beta, -1.0, None, ALU.mult)
        kb.vts('dve', spx, bar[:, :, 1], dtb[0:64, :], None, ALU.add)
        kb.vact(spa, spx, AF.Abs)
        kb.vact(spe, spa, AF.Exp, scale=-1.0)
        kb.vact(spe, spe, AF.Ln, bias=onec[0:64, :], scale=1.0)
        kb.vstt(spa, spx, 0.0, spe, ALU.max, ALU.add)
        kb.vts('dve', la, spa, negA[0:64, :], None, ALU.mult)
        kb.vcopy('dve', lahi, la)
        kb.vtt('dve', lalo, la, lahi, ALU.subtract)
        kb.vmm(pgc, lt, lahi, True, False)
        kb.vmm(pgc, lt, lalo, False, True)
        kb.vmm(pgl, onesb[0:64, :], lahi, True, False)
        kb.vmm(pgl, onesb[0:64, :], lalo, False, True)
        kb.vcopy('dve', gcol, pgc)
        kb.vact(egc, pgc, AF.Exp)
        kb.vact(egl, pgl, AF.Exp)
        kb.vtt('dve', kds, pgl[0:64, :], gcol, ALU.subtract)
        kb.vact(kds, kds, AF.Exp)
        kb.vtt('dve', bgc, beta, egc, ALU.mult)

        def chunk_par(c):
            n = tb * 8 + c
            q = n % 2
            q4 = n % 4
            ps_ = pset[q]
            csl = slice(c * 64, (c + 1) * 64)
            kb.vtr(pKV[q], KTn[:, csl], ident)
            kb.vts('dve', kbg[q], pKV[q], bgc[:, c:c + 1], None, ALU.mult)
            kb.vts('dve', kdec[q4], pKV[q], kds[:, c:c + 1], None, ALU.mult)
            kb.vmm(ps_['G'], V(lahi.ap[:, c:c + 1].to_broadcast([64, 128]), lahi.res), lt, True, False)
            kb.vmm(ps_['G'], V(lalo.ap[:, c:c + 1].to_broadcast([64, 128]), lalo.res), lt, False, True)
            kb.vmm(ps_['KK'], KTn[:, csl], KTn[:, csl])
            kb.vmm(ps_['QK'], QTn[:, csl], KTn[:, csl])
            yield
            kb.vtr(pKV[q], VTa[:, csl], ident)
            kb.vts('dve', vbt[q], pKV[q], beta[:, c:c + 1], None, ALU.mult)
            kb.vstt(dfm[q], ps_['G'][0:64, :], gcol[:, c:c + 1], negC, ALU.subtract, ALU.mult)
            kb.vact(EGr[q], ps_['G'], AF.Exp)
            yield
            kb.vact(E_[q], dfm[q], AF.Exp)
            kb.vstt(qdT[q4], QTn[:, csl], QSCALE, EGr[q], ALU.mult, ALU.mult)
            kb.vtt('pool', gng[q4], sgate[:, c, :].r([('sgate', c)]), dng, ALU.mult)
            yield
            kb.vtt('pool', Es_[q], E_[q], strict, ALU.mult)
            kb.vtt('pool', Ec_[q], E_[q], causal, ALU.mult)
            yield
            kb.vstt(Nb[q], ps_['KK'], nbeta[:, c:c + 1], Es_[q], ALU.mult, ALU.mult)
            kb.vstt(attb[q], ps_['QK'], QSCALE, Ec_[q], ALU.mult, ALU.mult)
            yield
            kb.vtr(pNA[q], Nb[q], ident[0:64, 0:64])
            kb.vtt('pool', N8[q], Nb[q], md8, ALU.mult)
            yield
            kb.vcopy('act', NTb[q], pNA[q])
            kb.vtt('pool', Tb_[q], N8[q], id64, ALU.add)
            yield
            kb.vtr(pNA[q], attb[q], ident[0:64, 0:64])
            kb.vtt('pool', N8T[q], NTb[q], md8T, ALU.mult)
            yield
            kb.vcopy('act', attT[q4], pNA[q])
            kb.vtt('pool', TTb[q], N8T[q], id64, ALU.add)
            for k in range(3):
                kb.vtt('pool', BT[k][q], NTb[q], mkT[k], ALU.mult)
            yield
            M, MT = N8[q], N8T[q]
            for lev in range(2):
                Mn, MnT = (Ma[q], MaT[q]) if lev == 0 else (Mb[q], MbT[q])
                kb.vmm(ps_['Mk'], MT, M)
                kb.vmm(ps_['MkT'], M, MT)
                yield
                kb.vcopy('act', Mn, ps_['Mk'])
                kb.vcopy('dve', MnT, ps_['MkT'])
                yield
                kb.vmm(ps_['Tu'], MnT, Tb_[q])
                kb.vmm(ps_['Tv'], Mn, TTb[q])
                yield
                kb.vtt('dve', Tb_[q], Tb_[q], ps_['Tu'], ALU.add)
                kb.vtt('dve', TTb[q], TTb[q], ps_['Tv'], ALU.add)
                yield
                M, MT = Mn, MnT
            for k in range(3):
                kb.vmm(ps_['Mk'], BT[k][q], Tb_[q])
                yield
                kb.vcopy('act', Yb[q], ps_['Mk'])
                yield
                if k < 2:
                    kb.vmm(ps_['Tu'], TTb[q], Yb[q])
                kb.vmm(ps_['MkT'], Yb[q], TTb[q])
                yield
                if k < 2:
                    kb.vtt('dve', Tb_[q], Tb_[q], ps_['Tu'], ALU.add)
                kb.vtt('dve', TTb[q], TTb[q], ps_['MkT'], ALU.add)
                yield
            kb.vmm(ps_['U'], TTb[q], vbt[q])
            kb.vmm(ps_['wT'], kbg[q], TTb[q])
            yield
            kb.vcopy('act', ut[q4], ps_['U'])
            kb.vcopy('act', wTb[q4], ps_['wT'])
            yield

        def chunk_seq(c):
            n = tb * 8 + c
            q = n % 2
            q4 = n % 4
            kb.vmm(pwS, wTb[q4], Sbf)
            yield
            kb.vtt('dve', vnew[q], ut[q4], pwS, ALU.subtract)
            yield
            yield
            kb.vmm(pO, qdT[q4], Sbf, True, False)
            kb.vmm(pO, attT[q4], vnew[q], False, True)
            pn = nextA()
            pSn = pn[:, 0:128]
            kb.vmm(pSn, kdec[q4], vnew[q])
            yield
            kb.vstt(Sbf, Sst, egl[:, c:c + 1], pSn, ALU.mult, ALU.add)
            kb.vstt(Sst, Sst, egl[:, c:c + 1], pSn, ALU.mult, ALU.add)
            yield
            kb.vact(osq[q], pO, AF.Square, accum_out=oss[q])
            kb.vact(olg[q], oss[q], AF.Ln, bias=epsc[0:64, :], scale=1.0 / 128)
            kb.vact(orr[q], olg[q], AF.Exp, scale=-0.5)
            kb.vstt(oast[tb % 2][:, c, :].r([(f'oast{tb % 2}', c)]), pO, orr[q], gng[q4], ALU.mult, ALU.mult)
            yield

        def tile_work(i):
            j = tb * 4 + i
            tsl = slice(i * 128, (i + 1) * 128)
            gvi = gv4[:, i, :].r([('gv4', i)])
            kb.S.op('dve', (lambda e, o=bnst.ap, a=gvi.ap: e.bn_stats(out=o, in_=a)), gvi.res, bnst.res)
            kb.S.op('dve', (lambda e, o=bnag.ap, a=bnst.ap: e.bn_aggr(out=o, in_=a)), bnst.res, bnag.res)
            kb.vact(lrs, bnag[:, 1:2], AF.Ln, bias=epsc, scale=1.0)
            kb.vact(lrs, lrs, AF.Exp, scale=-0.5)
            yield
            kb.vts('dve', vn, gvi[:, GV0:GV0 + 128], bnag[:, 0:1], lrs, ALU.subtract, ALU.mult)
            vt = vtok[i % 2]
            kb.vtt('pool', vt, vn, sgg, ALU.mult)
            yield
            yield
            kb.vmm(pSG, vt, sgwb)
            kb.vtt('dve', sgt[:, tsl].r([('sgt', i)]), pSG, sgb[:, 0:128], ALU.add)
            kb.vtt('pool', odst[tb % 2][:, tsl].r([(f'odst{tb % 2}', i)]), sgt[:, tsl].r([('sgt', i)]), uT[:, tsl], ALU.mult)
            yield
            kt0 = max(0, j - 4)
            nk = j + 1 - kt0
            W = nk * 128
            off = (5 - nk) * 128
            kres = [('KTb', t) for t in range(kt0 * 128 // 512, tb + 1)]
            for (a, b2) in ((0, min(W, 512)), (512, W)):
                if b2 > a:
                    kb.vmm(V(pS[:, a:b2], ['ps:S0' if a == 0 else 'ps:S1']), QTb[:, tsl],
                           KTb[:, kt0 * 128 + a:kt0 * 128 + b2].r(kres))
            kb.vstt(sB[:, 0:W], V(pS[:, 0:W], SB_RES if W > 512 else ['ps:S0']), QSCALE, BM[:, off:off + W],
                    ALU.mult, ALU.add)
            yield
            kb.vreduce(mx, sB[:, 0:W], ALU.max)
            kb.vts('dve', nmx, mx, -1.0, None, ALU.mult)
            yield
            kb.vact(pB[:, 0:W], sB[:, 0:W], AF.Exp, bias=nmx, scale=1.0, accum_out=rsum)
            yield
            yield
            for t in range(nk):
                kb.vtr(V(pT[:, t * 128:(t + 1) * 128], ['ps:T']), pB[:, t * 128:(t + 1) * 128], ident)
            kb.vcopy('dve', PTs[:, 0:W], V(pT[:, 0:W], ['ps:T']))
            yield
            yield
            pn = nextA()
            pOb = pn[:, 0:128]
            for t in range(nk):
                kb.vmm(pOb, PTs[:, t * 128:(t + 1) * 128], Vb[:, kt0 + t, :].r([('Vb', kt0 + t)]), t == 0, t == nk - 1)
            kb.vrecip(rrec, rsum)
            kb.vts('dve', obst[tb % 2][:, i, :].r([(f'obst{tb % 2}', i)]), pOb, rrec, None, ALU.mult)
            yield
            kt0 = max(0, j - 1)
            nk = j + 1 - kt0
            W = nk * 128
            off = (2 - nk) * 128
            kres = [('KTc', t) for t in range(kt0 * 128 // 512, tb + 1)]
            kb.vmm(pSc[:, 0:W], QTc[:, tsl], KTc[:, kt0 * 128:kt0 * 128 + W].r(kres))
            kb.vstt(sC[:, 1:1 + W].r(['sC']), pSc[:, 0:W], QSCALE, amc[:, off:off + W], ALU.mult, ALU.add)
            scr = sC[:, 0:1 + W].r(['sC', 'sC_sink'])
            yield
            kb.vreduce(mxc, scr, ALU.max)
            kb.vts('dve', nmxc, mxc, -1.0, None, ALU.mult)
            yield
            kb.vact(pC[:, 0:1 + W], scr, AF.Exp, bias=nmxc, scale=1.0, accum_out=rsumc)
            yield
            yield
            for t in range(nk):
                kb.vtr(V(pT[:, t * 128:(t + 1) * 128], ['ps:T']), pC[:, 1 + t * 128:1 + (t + 1) * 128], ident)
            kb.vcopy('dve', PTs[:, 0:W], V(pT[:, 0:W], ['ps:T']))
            yield
            yield
            pn = nextA()
            pOc = pn[:, 0:128]
            for t in range(nk):
                kb.vmm(pOc, PTs[:, t * 128:(t + 1) * 128], Vc[:, kt0 + t, :].r([('Vc', kt0 + t)]), t == 0, t == nk - 1)
            kb.vrecip(rrecc, rsumc)
            kb.vts('dve', ocst[tb % 2][:, i, :].r([(f'ocst{tb % 2}', i)]), pOc, rrecc, None, ALU.mult)
            yield

        def seq_pair(i):
            for c in (2 * i, 2 * i + 1):
                yield from chunk_seq(c)

        def rr(gens):
            gens = list(gens)
            while gens:
                for g in list(gens):
                    try:
                        next(g)
                    except StopIteration:
                        gens.remove(g)

        for i in range(4):
            ga = chunk_par(2 * i)
            next(ga)
            gl = [ga, chunk_par(2 * i + 1), tile_work(i)]
            if i > 0:
                gl.append(seq_pair(i - 1))
            rr(gl)
        rr([seq_pair(3)])
        kb.out_toks.append(kb.vdma('sp', f'st_d{tb % 2}', V(od_d[:, bsl], []),
                                   odst[tb % 2].r([(f'odst{tb % 2}', i) for i in range(4)])))
        kb.out_toks.append(kb.vdma('sp', f'st_a{tb % 2}', V(oav[tb], []),
                                   oast[tb % 2].r([(f'oast{tb % 2}', c) for c in range(8)])))
        kb.out_toks.append(kb.vdma('sp', f'st_b{tb % 2}', V(obv[tb], []),
                                   obst[tb % 2].r([(f'obst{tb % 2}', i) for i in range(4)])))
        kb.out_toks.append(kb.vdma('sp', f'st_c{tb % 2}', V(ocv[tb], []),
                                   ocst[tb % 2].r([(f'ocst{tb % 2}', i) for i in range(4)])))
    return kb.finish()


OFF = {'a_q': 0, 'a_k': 512, 'a_v': 1024, 'a_gate': 1536, 'a_beta': 2048, 'a_alpha': 2052, 'b_q': 2056,
       'b_k': 2568, 'b_v': 3080, 'c_q': 3592, 'c_k': 4104, 'c_v': 4360, 'd_u': 4616, 'd_v': 5128}


def m_consts():
    q = np.arange(128)[:, None]
    kk = np.arange(640)[None, :]
    hi = (q >= 64).astype(np.int64)
    validb = (kk // 64 >= hi) & (kk // 64 <= 8 + hi)
    maskb = np.where(validb, 0.0, -30000.0).astype(np.float32)
    idxb = np.clip(512 + q - kk, -256, 256) + 256
    kc = np.arange(256)[None, :]
    validc = (kc // 64 >= hi) & (kc // 64 <= 2 + hi)
    distc = np.abs(128 + q - kc).astype(np.float32)
    i64 = np.arange(64)
    cge = (i64[:, None] >= i64[None, :])
    ci, si = i64[:, None], i64[None, :]
    md8 = ((ci // 8 == si // 8) & (ci > si)).astype(np.float32)

    def mk(b):
        return ((ci // (2 * b) == si // (2 * b)) & (ci % (2 * b) >= b) & (si % (2 * b) < b)).astype(np.float32)
    m64 = np.stack([np.where(cge, -1.0, 0.0), (ci > si).astype(np.float32), cge.astype(np.float32), np.eye(64),
                    md8, md8.T, mk(8).T, mk(16).T, mk(32).T], axis=1).astype(np.float32)
    i128 = np.arange(128)
    return {
        'maskb': maskb, 'idxb': idxb, 'validc': validc, 'distc': distc,
        'm64': np.ascontiguousarray(m64),
        'lt': (i64[:, None] <= i64[None, :]).astype(NPBF),
        'ident': np.eye(128).astype(NPBF),
        'tri': (i128[:, None] <= i128[None, :]).astype(np.float32),
    }


def host_M(hT_shards, P, l):
    C = m_consts()
    w_in = P['w_in'][l]
    in_maps = []
    for cid in range(NCORES):
        b, hd = cid // 4, cid % 4
        kvh = hd // 2

        def cols(name, h, n=128):
            return w_in[:, OFF[name] + h * n:OFF[name] + (h + 1) * n]
        wfm = np.concatenate([cols('a_q', hd), cols('a_k', hd), cols('a_v', hd), cols('b_q', hd), cols('b_k', hd),
                              cols('c_q', hd), cols('c_k', kvh), cols('d_u', hd)], axis=1)
        others = [g for g in range(4) if g != hd]
        wtm = np.concatenate([cols('d_v', hd)] + [cols('d_v', g) for g in others] +
                             [cols('b_v', hd), cols('c_v', kvh), cols('a_gate', hd),
                              w_in[:, OFF['a_beta'] + hd:OFF['a_beta'] + hd + 1],
                              w_in[:, OFF['a_alpha'] + hd:OFF['a_alpha'] + hd + 1]], axis=1)
        cw = P['dn_conv_w'][l]
        cwq = np.stack([cw[:, g * 512 + hd * 128:g * 512 + (hd + 1) * 128].T for g in range(3)], axis=1)
        slope = np.float32(2.0 ** (-8.0 * (hd + 1) / 4))
        amc = np.where(C['validc'], -slope * C['distc'], np.float32(-30000.0)).astype(np.float32)
        in_maps.append({
            'hT': np.ascontiguousarray(np.concatenate(hT_shards[b * 4:(b + 1) * 4], axis=1)),
            'wfm': np.ascontiguousarray(wfm), 'wtm': np.ascontiguousarray(wtm),
            'cwq': np.ascontiguousarray(cwq.astype(np.float32)),
            'dnc': np.ascontiguousarray(np.broadcast_to(
                np.array([P['dn_a_log'][l][hd], P['dn_dt_bias'][l][hd]], np.float32)[None, :], (128, 2))),
            'dng': np.ascontiguousarray(np.broadcast_to(P['dn_norm_g'][l][None, :], (64, 128))),
            'biasT': np.ascontiguousarray(P['rel_bias'][l][hd][C['idxb']]),
            'maskb': C['maskb'], 'amc': amc,
            'sink': np.full((128, 1), P['sinks'][l][hd], np.float32),
            'sgg': np.ascontiguousarray(np.broadcast_to(P['sgu_norm_g'][l][hd * 128:(hd + 1) * 128][None, :], (128, 128))),
            'sgwT': np.ascontiguousarray(P['sgu_w'][l][hd].T),
            'sgb': np.ascontiguousarray(np.broadcast_to(np.tile(P['sgu_b'][l][hd], 4)[None, :], (128, 512))),
            'tri': C['tri'], 'ident': C['ident'], 'lt': C['lt'], 'm64': C['m64'],
        })
    res = run('M', in_maps)
    shards = []
    for cid in range(NCORES):
        b, q = core_bq(cid)
        tsl = slice(q * TOK, (q + 1) * TOK)
        rows = []
        for nm in ('out_a', 'out_b', 'out_c'):
            for hd in range(4):
                rows.append(res[b * 4 + hd][nm][tsl, :].T)
        for hd in range(4):
            rows.append(res[b * 4 + hd]['out_d'][:, tsl])
        shards.append(np.ascontiguousarray(np.concatenate(rows, axis=0)))
    return shards


def kernel(x, c, ada_w, ada_b, mix_pre_g, mix_post_g, w_in, dn_conv_w, dn_a_log, dn_dt_bias, dn_norm_g,
           rel_bias, sinks, sgu_norm_g, sgu_w, sgu_b, w_out, ffn_pre_g, ffn_post_g, ffn_w_up, ffn_conv_w,
           ffn_conv_b, ffn_w_down):
    f = lambda a: np.asarray(a, dtype=np.float32)
    x, c, ada_w, ada_b = f(x), f(c), f(ada_w), f(ada_b)
    P = {'w_in': f(w_in), 'dn_conv_w': f(dn_conv_w), 'dn_a_log': f(dn_a_log), 'dn_dt_bias': f(dn_dt_bias),
         'dn_norm_g': f(dn_norm_g), 'rel_bias': f(rel_bias), 'sinks': f(sinks), 'sgu_norm_g': f(sgu_norm_g),
         'sgu_w': f(sgu_w), 'sgu_b': f(sgu_b)}
    mix_pre_g, mix_post_g, ffn_pre_g, ffn_post_g = f(mix_pre_g), f(mix_post_g), f(ffn_pre_g), f(ffn_post_g)
    w_out, ffn_w_up, ffn_conv_w, ffn_conv_b, ffn_w_down = f(w_out), f(ffn_w_up), f(ffn_conv_w), f(ffn_conv_b), f(ffn_w_down)
    mod = host_L0(c, ada_w, ada_b)
    xs = [tok_shard_T(x, cid) for cid in range(NCORES)]
    for l in range(2):
        m = split_mod(mod[l])
        hs = host_N1(xs, mix_pre_g[l], m)
        mo = host_M(hs, P, l)
        xs, h2 = host_C1(xs, mo, np.ascontiguousarray(w_out[l]), m, mix_post_g[l], ffn_pre_g[l])
        acts = host_C2a(h2, np.ascontiguousarray(ffn_w_up[l]), ffn_conv_w[l], ffn_conv_b[l])
        xs = host_C2b(acts, xs, np.ascontiguousarray(ffn_w_down[l]), m, ffn_post_g[l])
    out = np.zeros((2, S_LEN, D), np.float32)
    for cid in range(NCORES):
        b, q = core_bq(cid)
        out[b, q * TOK:(q + 1) * TOK, :] = xs[cid].T
    return out
```

```python
import numpy as np
import ml_dtypes
from contextlib import ExitStack
import concourse.bass as bass
import concourse.mybir as mybir
from concourse.bass_utils import run_bass_kernel_spmd

F32 = mybir.dt.float32
BF16 = mybir.dt.bfloat16
AF = mybir.ActivationFunctionType
ALU = mybir.AluOpType
AX = mybir.AxisListType
NPBF = ml_dtypes.bfloat16

NCORES = 8
D = 2048
KC = 16
S_LEN = 4096
TOK = 1024
DFF = 5504
NFF = 43
EPS = 1e-6
ENG = ['pe', 'dve', 'act', 'pool', 'sp']


class Sched:
    def __init__(self, nc, stack):
        self.nc = nc
        self.stack = stack
        self.plan = {e: [] for e in ENG}
        self.cnt = {e: 0 for e in ENG}
        self.seen = {e: {} for e in ENG}
        self.sems = {}
        self.dcnt = {}
        self.res = {}
        for e in ENG:
            self.sems[e] = stack.enter_context(nc.semaphore('s_' + e))

    def _st(self, r):
        st = self.res.get(r)
        if st is None:
            st = {'w': None, 'r': {}}
            self.res[r] = st
        return st

    def _deps(self, eng, reads, writes):
        deps = {}

        def add(k, v):
            if deps.get(k, 0) < v:
                deps[k] = v
        for r in reads:
            w = self._st(r)['w']
            if w is not None:
                add(*w)
        for wr in writes:
            st = self._st(wr)
            if st['w'] is not None:
                add(*st['w'])
            for k, v in st['r'].items():
                if k == eng:
                    continue
                add(k, v)
        return deps

    def _waits(self, eng, deps):
        waits = []
        for k, v in deps.items():
            if k == eng and eng == 'pe':
                continue
            if self.seen[eng].get(k, 0) >= v:
                continue
            self.seen[eng][k] = v
            waits.append((k, v))
        return waits

    def _commit(self, tok, reads, writes):
        k, v = tok
        for r in reads:
            st = self._st(r)
            if st['r'].get(k, 0) < v:
                st['r'][k] = v
        for w in writes:
            st = self._st(w)
            st['w'] = tok
            st['r'] = {}

    def op(self, eng, fn, reads=(), writes=()):
        deps = self._deps(eng, reads, writes)
        waits = self._waits(eng, deps)
        self.cnt[eng] += 1
        tok = (eng, self.cnt[eng])
        self.plan[eng].append((waits, fn, eng, 1))
        self._commit(tok, reads, writes)

    def dma(self, q, key, pairs, reads=(), writes=()):
        if key not in self.sems:
            self.sems[key] = self.stack.enter_context(self.nc.semaphore('d_' + str(key)))
            self.dcnt[key] = 0
        deps = self._deps(q, reads, writes)
        waits = self._waits(q, deps)
        for i, (o, a) in enumerate(pairs):
            self.dcnt[key] += 16
            self.plan[q].append((waits if i == 0 else [],
                                 (lambda e, o=o, a=a: e.dma_start(out=o, in_=a)), key, 16))
        tok = (key, self.dcnt[key])
        self._commit(tok, reads, writes)
        return tok

    def wait_tok(self, eng, tok):
        waits = self._waits(eng, {tok[0]: tok[1]})
        if waits:
            self.plan[eng].append((waits, None, None, 0))

    def emit(self):
        nc = self.nc
        engs = {'pe': 'tensor', 'dve': 'vector', 'act': 'scalar', 'pool': 'gpsimd', 'sp': 'sync'}
        with nc.Block() as block:
            for e in ENG:
                plan = self.plan[e]
                if not plan:
                    continue

                def body(engine, plan=plan):
                    for waits, fn, k, inc in plan:
                        for (wk, wv) in waits:
                            engine.wait_ge(self.sems[wk], wv)
                        if fn is not None:
                            fn(engine).then_inc(self.sems[k], inc)
                getattr(block, engs[e])(body)


class KB:
    def __init__(self):
        self.nc = bass.Bass("TRN2", target_bir_lowering=False)
        self.st = ExitStack()
        self.S = Sched(self.nc, self.st)
        self.out_toks = []
        self.nbank = 0

    def din(self, name, shape, dt=F32):
        return self.nc.dram_tensor(name, list(shape), dt, kind="ExternalInput").ap()

    def dout(self, name, shape, dt=F32):
        return self.nc.dram_tensor(name, list(shape), dt, kind="ExternalOutput").ap()

    def sb(self, name, shape, dt=F32):
        return self.st.enter_context(self.nc.sbuf_tensor('sb_' + name, list(shape), dt))

    def ps(self, name, shape, dt=F32):
        return self.st.enter_context(self.nc.psum_tensor('ps_' + name, list(shape), dt))

    def finish(self):
        for t in self.out_toks:
            self.S.wait_tok('sp', t)
        self.S.emit()
        self.st.close()
        return self.nc

    def mm(self, out, lhsT, rhs, start, stop, r, w):
        self.S.op('pe', lambda e: e.matmul(out, lhsT=lhsT, rhs=rhs, start=start, stop=stop), r, w)

    def tr(self, out, in_, ident, r, w):
        self.S.op('pe', lambda e: e.transpose(out=out, in_=in_, identity=ident), r, w)

    def act(self, out, in_, func, r, w, bias=None, scale=None, accum_out=None):
        kw = {}
        if bias is not None:
            kw['bias'] = bias
        if scale is not None:
            kw['scale'] = scale
        if accum_out is not None:
            kw['accum_out'] = accum_out
        self.S.op('act', lambda e: e.activation(out=out, in_=in_, func=func, **kw), r, w)

    def tt(self, eng, out, in0, in1, op, r, w):
        self.S.op(eng, lambda e: e.tensor_tensor(out=out, in0=in0, in1=in1, op=op), r, w)

    def ts(self, eng, out, in0, s1, s2, op0, op1, r, w, accum_out=None):
        if op1 is None:
            self.S.op(eng, lambda e: e.tensor_scalar(out=out, in0=in0, scalar1=s1, scalar2=None, op0=op0), r, w)
        elif accum_out is None:
            self.S.op(eng, lambda e: e.tensor_scalar(out=out, in0=in0, scalar1=s1, scalar2=s2, op0=op0, op1=op1), r, w)
        else:
            self.S.op(eng, lambda e: e.tensor_scalar(out=out, in0=in0, scalar1=s1, scalar2=s2, op0=op0, op1=op1,
                                                     accum_out=accum_out), r, w)

    def stt(self, out, in0, scalar, in1, op0, op1, r, w):
        self.S.op('dve', lambda e: e.scalar_tensor_tensor(out=out, in0=in0, scalar=scalar, in1=in1,
                                                          op0=op0, op1=op1), r, w)

    def copy(self, eng, out, in_, r, w):
        if eng == 'act':
            self.S.op('act', lambda e: e.copy(out=out, in_=in_), r, w)
        else:
            self.S.op(eng, lambda e: e.tensor_copy(out=out, in_=in_), r, w)

    def memset(self, eng, ap, val, w):
        self.S.op(eng, lambda e: e.memset(ap, val), (), w)

    def recip(self, out, in_, r, w):
        self.S.op('dve', lambda e: e.reciprocal(out=out, in_=in_), r, w)

    def dma(self, q, key, out, in_, r=(), w=()):
        return self.S.dma(q, key, [(out, in_)], r, w)


def fm(ap):
    return ap.rearrange('(kc p) t -> p kc t', p=128)


def vec_fm(v):
    v = np.asarray(v)
    return np.ascontiguousarray(v.reshape(-1, 128).T)


def setup_consts(kb):
    ones = kb.sb('ones', [128, 128], BF16)
    kb.memset('pool', ones[:], 1.0, ['ones'])
    epst = kb.sb('epst', [128, 1])
    kb.memset('pool', epst[:], EPS, ['epst'])
    return ones, epst


def rstd_from_ss(kb, ssp, ssp_res, out, out_res, epst, n, tag):
    kb.act(out, ssp, AF.Sqrt, [ssp_res, 'epst'], [out_res], bias=epst[:], scale=1.0 / n)
    kb.recip(out, out, [out_res], [out_res])


def norm_bufs(kb):
    return {
        'ssp': kb.ps('nm_ssp', [128, 512]),
        'sq': [kb.sb(f'nm_sq{i}', [128, 512], BF16) for i in range(2)],
        't': [kb.sb(f'nm_t{i}', [128, 512]) for i in range(2)],
        'rstd': kb.sb('nm_rstd', [128, 512]),
    }


def norm_mod_half(kb, nb, xt, xres, hf, gs, gsres, sh, shres, ones, epst, hb, hbres):
    sl = slice(hf * 512, (hf + 1) * 512)
    ssp, rstd = nb['ssp'], nb['rstd']
    for kc in range(KC):
        q = nb['sq'][kc % 2]
        qn = f'nm_sq{kc % 2}'
        kb.act(q[:], xt[:, kc, sl], AF.Square, [(xres, kc, hf)], [qn])
        kb.mm(ssp[:], ones[:], q[:], kc == 0, kc == KC - 1, ['ones', qn], ['nm_ssp'])
    rstd_from_ss(kb, ssp[:], 'nm_ssp', rstd[:], 'nm_rstd', epst, D, 'nm')
    for kc in range(KC):
        t = nb['t'][kc % 2]
        tn = f'nm_t{kc % 2}'
        kb.stt(t[:], xt[:, kc, sl], gs[:, kc:kc + 1], rstd[:], ALU.mult, ALU.mult,
               [(xres, kc, hf), gsres, 'nm_rstd'], [tn])
        kb.act(hb[:, kc, :], t[:], AF.Identity, [tn, shres], [(hbres, kc)], bias=sh[:, kc:kc + 1])


def build_L0():
    kb = KB()
    cT = kb.din('cT', [128, KC, 2])
    W = kb.din('W', [D, 3072])
    bias = kb.din('bias', [1, 3072])
    mod = kb.dout('mod', [2, 3072])
    sc = kb.sb('sc', [128, KC, 2])
    scs = kb.sb('scs', [128, KC, 2], BF16)
    bt = kb.sb('bt', [2, 3072])
    ot = kb.sb('ot', [2, 3072])
    wb = [kb.sb(f'wb{i}', [128, KC, 512], BF16) for i in range(2)]
    pb = [kb.ps(f'pb{i}', [2, 512]) for i in range(2)]
    kb.dma('sp', 'ld_c', sc[:], cT[:, :, :], w=['sc'])
    kb.dma('sp', 'ld_b', bt[:], bias.partition_broadcast(2), w=['bt'])
    kb.act(scs[:], sc[:], AF.Silu, ['sc'], ['scs'])
    Wv = fm(W)
    for j in range(6):
        i = j % 2
        kb.dma('pool', f'ld_w{i}', wb[i][:], Wv[:, :, j * 512:(j + 1) * 512], w=[f'wb{i}'])
        for kc in range(KC):
            kb.mm(pb[i][:], scs[:, kc, :], wb[i][:, kc, :], kc == 0, kc == KC - 1, ['scs', f'wb{i}'], [f'pb{i}'])
        kb.tt('dve', ot[:, j * 512:(j + 1) * 512], pb[i][:], bt[:, j * 512:(j + 1) * 512], ALU.add,
              [f'pb{i}', 'bt'], [('ot', j)])
    kb.out_toks.append(kb.dma('sp', 'st', mod[:, :], ot[:], r=[('ot', j) for j in range(6)]))
    return kb.finish()


def build_N1():
    kb = KB()
    xT = kb.din('xT', [D, TOK])
    nv = kb.din('nv', [128, 3, KC])
    hT = kb.dout('hT', [D, TOK], BF16)
    ones, epst = setup_consts(kb)
    xt = kb.sb('xt', [128, KC, TOK])
    nvt = kb.sb('nvt', [128, 3, KC])
    gs = kb.sb('gs', [128, KC])
    kb.dma('sp', 'ld_nv', nvt[:], nv[:, :, :], w=['nv'])
    xv = fm(xT)
    for hf in range(2):
        for g4 in range(4):
            kb.dma('sp', f'ld_x{hf}{g4}', xt[:, g4 * 4:(g4 + 1) * 4, hf * 512:(hf + 1) * 512],
                   xv[:, g4 * 4:(g4 + 1) * 4, hf * 512:(hf + 1) * 512],
                   w=[('x', kc, hf) for kc in range(g4 * 4, g4 * 4 + 4)])
    kb.ts('dve', gs[:], nvt[:, 1, :], 1.0, None, ALU.add, None, ['nv'], ['gs'])
    kb.tt('dve', gs[:], gs[:], nvt[:, 0, :], ALU.mult, ['gs', 'nv'], ['gs'])
    nb = norm_bufs(kb)
    hbs = [kb.sb(f'hb{i}', [128, KC, 512], BF16) for i in range(2)]
    hv = fm(hT)
    for hf in range(2):
        norm_mod_half(kb, nb, xt, 'x', hf, gs, 'gs', nvt[:, 2, :], 'nv', ones, epst, hbs[hf], f'hb{hf}')
        kb.out_toks.append(kb.dma('sp', f'st_h{hf}', hv[:, :, hf * 512:(hf + 1) * 512], hbs[hf][:],
                                  r=[(f'hb{hf}', kc) for kc in range(KC)]))
    return kb.finish()


def build_C1():
    kb = KB()
    xT = kb.din('xT', [D, TOK])
    moT = kb.din('moT', [D, TOK], BF16)
    w_out = kb.din('w_out', [D, D])
    cv = kb.din('cv', [128, 5, KC])
    xo = kb.dout('xo', [D, TOK])
    h2T = kb.dout('h2T', [D, TOK], BF16)
    ones, epst = setup_consts(kb)
    xt = kb.sb('xt', [128, KC, TOK])
    mo = kb.sb('mo', [128, KC, TOK], BF16)
    cvt = kb.sb('cvt', [128, 5, KC])
    gtg = kb.sb('gtg', [128, KC])
    gs = kb.sb('gs', [128, KC])
    kb.dma('sp', 'ld_cv', cvt[:], cv[:, :, :], w=['cv'])
    xv = fm(xT)
    mv = fm(moT)
    for hf in range(2):
        for g4 in range(4):
            kb.dma('sp', f'ld_m{hf}{g4}', mo[:, g4 * 4:(g4 + 1) * 4, hf * 512:(hf + 1) * 512],
                   mv[:, g4 * 4:(g4 + 1) * 4, hf * 512:(hf + 1) * 512],
                   w=[('mo', kc, hf) for kc in range(g4 * 4, g4 * 4 + 4)])
    for hf in range(2):
        for g4 in range(4):
            kb.dma('sp', f'ld_x{hf}{g4}', xt[:, g4 * 4:(g4 + 1) * 4, hf * 512:(hf + 1) * 512],
                   xv[:, g4 * 4:(g4 + 1) * 4, hf * 512:(hf + 1) * 512],
                   w=[('x', kc, hf) for kc in range(g4 * 4, g4 * 4 + 4)])
    kb.tt('dve', gtg[:], cvt[:, 0, :], cvt[:, 1, :], ALU.mult, ['cv'], ['gtg'])
    kb.ts('dve', gs[:], cvt[:, 3, :], 1.0, None, ALU.add, None, ['cv'], ['gs'])
    kb.tt('dve', gs[:], gs[:], cvt[:, 2, :], ALU.mult, ['gs', 'cv'], ['gs'])
    wg = [kb.sb(f'wg{i}', [128, KC, 256], BF16) for i in range(2)]
    yh = kb.sb('yh', [128, KC, 512])
    psy = [kb.ps(f'psy{i}', [128, 512]) for i in range(3)]
    ssy = kb.ps('ssy', [128, 512])
    sqb = [kb.sb(f'sq{i}', [128, 512], BF16) for i in range(2)]
    tb = [kb.sb(f'tb{i}', [128, 512]) for i in range(2)]
    rstd = kb.sb('rstd', [128, 512])
    nb = norm_bufs(kb)
    hbs = [kb.sb(f'hb{i}', [128, KC, 512], BF16) for i in range(2)]
    wv = fm(w_out)
    xov = fm(xo)
    hv = fm(h2T)
    nld = 0
    for hf in range(2):
        sl = slice(hf * 512, (hf + 1) * 512)
        pend = None
        for dc in range(KC):
            if dc % 2 == 0:
                wi = nld % 2
                nld += 1
                kb.dma('pool', f'ld_w{wi}', wg[wi][:], wv[:, :, dc * 128:dc * 128 + 256], w=[f'wg{wi}'])
            wcur = wg[wi]
            p = psy[dc % 3]
            pn = f'psy{dc % 3}'
            for kc in range(KC):
                kb.mm(p[:], wcur[:, kc, (dc % 2) * 128:(dc % 2) * 128 + 128], mo[:, kc, sl], kc == 0, kc == KC - 1,
                      [f'wg{wi}', ('mo', kc, hf)], [pn])
            kb.copy('act', yh[:, dc, :], p[:], [pn], [('yh', dc)])
            q = sqb[dc % 2]
            qn = f'sq{dc % 2}'
            kb.act(q[:], p[:], AF.Square, [pn], [qn])
            if pend is not None:
                pend()
            pend = (lambda q=q, qn=qn, dc=dc: kb.mm(ssy[:], ones[:], q[:], dc == 0, dc == KC - 1, ['ones', qn], ['ssy']))
        pend()
        rstd_from_ss(kb, ssy[:], 'ssy', rstd[:], 'rstd', epst, D, 'y')
        for dc in range(KC):
            t = tb[dc % 2]
            tn = f'tb{dc % 2}'
            kb.stt(t[:], yh[:, dc, :], gtg[:, dc:dc + 1], rstd[:], ALU.mult, ALU.mult,
                   [('yh', dc), 'gtg', 'rstd'], [tn])
            kb.tt('dve', xt[:, dc, sl], xt[:, dc, sl], t[:], ALU.add, [('x', dc, hf), tn], [('x', dc, hf)])
        kb.out_toks.append(kb.dma('sp', f'st_x{hf}', xov[:, :, sl], xt[:, :, sl],
                                  r=[('x', dc, hf) for dc in range(KC)]))
        norm_mod_half(kb, nb, xt, 'x', hf, gs, 'gs', cvt[:, 4, :], 'cv', ones, epst, hbs[hf], f'hb{hf}')
        kb.out_toks.append(kb.dma('sp', f'st_h{hf}', hv[:, :, sl], hbs[hf][:],
                                  r=[(f'hb{hf}', kc) for kc in range(KC)]))
    return kb.finish()


def build_C2a():
    kb = KB()
    h2e = kb.din('h2e', [D, TOK + 2], BF16)
    w_up = kb.din('w_up', [D, 2 * DFF])
    cw = kb.din('cw', [128, NFF, 4])
    actT = kb.dout('actT', [DFF, TOK], BF16)
    he = kb.sb('he', [128, KC, TOK + 2], BF16)
    cwt = kb.sb('cwt', [128, NFF, 4])
    kb.dma('sp', 'ld_cw', cwt[:], cw[:, :, :], w=['cw'])
    hv = fm(h2e)
    for g4 in range(4):
        kb.dma('sp', f'ld_h{g4}', he[:, g4 * 4:(g4 + 1) * 4, :], hv[:, g4 * 4:(g4 + 1) * 4, :],
               w=[('he', kc) for kc in range(g4 * 4, g4 * 4 + 4)])
    wa = [kb.sb(f'wa{i}', [128, KC, 256], BF16) for i in range(2)]
    wgt = [kb.sb(f'wgt{i}', [128, KC, 256], BF16) for i in range(2)]
    psa = [[kb.ps(f'psa{i}{s}', [128, 512]) for s in range(3)] for i in range(2)]
    psg = [kb.ps(f'psg{s}', [128, 512]) for s in range(2)]
    aext = [kb.sb(f'aext{i}', [128, TOK + 2]) for i in range(2)]
    gS = [kb.sb(f'gS{i}', [128, TOK]) for i in range(2)]
    acc = [kb.sb(f'acc{i}', [128, TOK]) for i in range(2)]
    ga = [kb.sb(f'ga{i}', [128, TOK]) for i in range(2)]
    ab = [kb.sb(f'ab{i}', [128, 2, TOK], BF16) for i in range(2)]
    wv = fm(w_up)
    av = actT.rearrange('(j p) t -> p j t', p=128)
    W3 = 342
    for jb in range(22):
        bi = jb % 2
        ncol = 256 if jb < 21 else 128
        kb.dma('pool', f'ld_wa{bi}', wa[bi][:, :, 0:ncol], wv[:, :, jb * 256:jb * 256 + ncol], w=[f'wa{bi}'])
        kb.dma('pool', f'ld_wg{bi}', wgt[bi][:, :, 0:ncol], wv[:, :, DFF + jb * 256:DFF + jb * 256 + ncol],
               w=[f'wgt{bi}'])
        njj = ncol // 128
        for jj in range(njj):
            j = jb * 2 + jj
            i = j % 2
            cs = slice(jj * 128, jj * 128 + 128)
            for s in range(3):
                for kc in range(KC):
                    kb.mm(psa[i][s][:, 0:W3], wa[bi][:, kc, cs], he[:, kc, s * W3:(s + 1) * W3], kc == 0, kc == KC - 1,
                          [f'wa{bi}', ('he', kc)], [f'psa{i}{s}'])
            for s in range(2):
                for kc in range(KC):
                    kb.mm(psg[s][:], wgt[bi][:, kc, cs], he[:, kc, 2 + s * 512:2 + (s + 1) * 512], kc == 0, kc == KC - 1,
                          [f'wgt{bi}', ('he', kc)], [f'psg{s}'])
            for s in range(3):
                kb.copy('act', aext[i][:, s * W3:(s + 1) * W3], psa[i][s][:, 0:W3], [f'psa{i}{s}'], [(f'aext{i}', s)])
            for s in range(2):
                kb.copy('act', gS[i][:, s * 512:(s + 1) * 512], psg[s][:], [f'psg{s}'], [(f'gS{i}', s)])
            ar = [(f'aext{i}', s) for s in range(3)]
            kb.ts('dve', acc[i][:], aext[i][:, 2:TOK + 2], cwt[:, j, 2:3], cwt[:, j, 3:4], ALU.mult, ALU.add,
                  ar + ['cw'], [f'acc{i}'])
            kb.stt(acc[i][:], aext[i][:, 1:TOK + 1], cwt[:, j, 1:2], acc[i][:], ALU.mult, ALU.add,
                   ar + ['cw', f'acc{i}'], [f'acc{i}'])
            kb.stt(acc[i][:], aext[i][:, 0:TOK], cwt[:, j, 0:1], acc[i][:], ALU.mult, ALU.add,
                   ar + ['cw', f'acc{i}'], [f'acc{i}'])
            kb.act(ga[i][:], acc[i][:], AF.Gelu_apprx_tanh, [f'acc{i}'], [f'ga{i}'])
            kb.tt('dve', ab[bi][:, jj, :], ga[i][:], gS[i][:], ALU.mult,
                  [f'ga{i}', (f'gS{i}', 0), (f'gS{i}', 1)], [(f'ab{bi}', jj)])
        kb.out_toks.append(kb.dma('sp', f'st_a{bi}', av[:, jb * 2:jb * 2 + njj, :], ab[bi][:, 0:njj, :],
                                  r=[(f'ab{bi}', jj) for jj in range(njj)]))
    return kb.finish()


def build_C2b():
    kb = KB()
    actT = kb.din('actT', [DFF, TOK], BF16)
    w_down = kb.din('w_down', [DFF, D])
    xT = kb.din('xT', [D, TOK])
    fv = kb.din('fv', [128, 2, KC])
    xo = kb.dout('xo', [D, TOK])
    ones, epst = setup_consts(kb)
    at = kb.sb('at', [128, NFF, TOK], BF16)
    y2 = kb.sb('y2', [128, KC, TOK])
    fvt = kb.sb('fvt', [128, 2, KC])
    gtg = kb.sb('gtg', [128, KC])
    kb.dma('sp', 'ld_fv', fvt[:], fv[:, :, :], w=['fv'])
    kb.tt('dve', gtg[:], fvt[:, 0, :], fvt[:, 1, :], ALU.mult, ['fv'], ['gtg'])
    av = actT.rearrange('(j p) t -> p j t', p=128)
    for g in range(0, NFF, 8):
        n = min(8, NFF - g)
        kb.dma('sp', f'ld_a{g}', at[:, g:g + n, :], av[:, g:g + n, :], w=[('at', j) for j in range(g, g + n)])
    wd = [kb.sb(f'wd{i}', [128, NFF, 128], BF16) for i in range(2)]
    psy = [kb.ps(f'psy{i}', [128, 512]) for i in range(3)]
    ss = [kb.ps(f'ss{i}', [128, 512]) for i in range(2)]
    sqb = [kb.sb(f'sq{i}', [128, 512], BF16) for i in range(2)]
    rstd = kb.sb('rstd', [128, TOK])
    wv = w_down.rearrange('(j p) n -> p j n', p=128)
    n = 0
    pend = None
    for dc in range(KC):
        wi = dc % 2
        kb.dma('pool', f'ld_w{wi}', wd[wi][:], wv[:, :, dc * 128:(dc + 1) * 128], w=[f'wd{wi}'])
        for hf in range(2):
            p = psy[n % 3]
            pn = f'psy{n % 3}'
            for j in range(NFF):
                kb.mm(p[:], wd[wi][:, j, :], at[:, j, hf * 512:(hf + 1) * 512], j == 0, j == NFF - 1,
                      [f'wd{wi}', ('at', j)], [pn])
            kb.copy('act', y2[:, dc, hf * 512:(hf + 1) * 512], p[:], [pn], [('y2', dc, hf)])
            q = sqb[n % 2]
            qn = f'sq{n % 2}'
            kb.act(q[:], p[:], AF.Square, [pn], [qn])
            if pend is not None:
                pend()
            pend = (lambda q=q, qn=qn, dc=dc, hf=hf: kb.mm(ss[hf][:], ones[:], q[:], dc == 0, dc == KC - 1,
                                                         ['ones', qn], [f'ss{hf}']))
            n += 1
    pend()
    for hf in range(2):
        rstd_from_ss(kb, ss[hf][:], f'ss{hf}', rstd[:, hf * 512:(hf + 1) * 512], ('rstd', hf), epst, D, f'y{hf}')
    xin = [kb.sb(f'xin{i}', [128, TOK]) for i in range(4)]
    xv = fm(xT)
    xov = fm(xo)
    for dc in range(4):
        kb.dma('sp', f'ld_x{dc}', xin[dc][:], xv[:, dc, :], w=[f'xin{dc}'])
    for dc in range(KC):
        xi = dc % 4
        yr = [('y2', dc, 0), ('y2', dc, 1)]
        kb.stt(y2[:, dc, :], y2[:, dc, :], gtg[:, dc:dc + 1], rstd[:], ALU.mult, ALU.mult,
               yr + ['gtg', ('rstd', 0), ('rstd', 1)], yr)
        kb.tt('dve', y2[:, dc, :], y2[:, dc, :], xin[xi][:], ALU.add, yr + [f'xin{xi}'], yr)
        kb.out_toks.append(kb.dma('act' if dc % 2 else 'sp', f'st_x{dc % 4}', xov[:, dc, :], y2[:, dc, :], r=yr))
        if dc + 4 < KC:
            kb.dma('sp', f'ld_x{xi}', xin[xi][:], xv[:, dc + 4, :], w=[f'xin{xi}'])
    return kb.finish()


_CACHE = {}


def get_prog(name):
    if name not in _CACHE:
        _CACHE[name] = {'L0': build_L0, 'N1': build_N1, 'C1': build_C1, 'C2a': build_C2a, 'C2b': build_C2b}[name]()
    return _CACHE[name]


def run(name, in_maps):
    nc = {'L0': build_L0, 'N1': build_N1, 'C1': build_C1, 'C2a': build_C2a, 'C2b': build_C2b, 'M': build_M}[name]()
    res = run_bass_kernel_spmd(nc, in_maps, core_ids=list(range(NCORES)))
    return res.results


def core_bq(cid):
    return cid // 4, cid % 4


def tok_shard_T(a, cid):
    b, q = core_bq(cid)
    return np.ascontiguousarray(a[b, q * TOK:(q + 1) * TOK, :].T)


def stack_vecs(vs):
    return np.ascontiguousarray(np.stack([vec_fm(v) for v in vs], axis=1).astype(np.float32))


def host_L0(c, ada_w, ada_b):
    cT = np.ascontiguousarray(c.T.reshape(KC, 128, 2).transpose(1, 0, 2))
    in_maps = []
    for cid in range(NCORES):
        l, j = cid // 4, cid % 4
        in_maps.append({'cT': cT, 'W': np.ascontiguousarray(ada_w[l][:, j * 3072:(j + 1) * 3072]),
                        'bias': np.ascontiguousarray(ada_b[l][None, j * 3072:(j + 1) * 3072])})
    res = run('L0', in_maps)
    mod = np.zeros((2, 2, 6 * D), np.float32)
    for cid in range(NCORES):
        l, j = cid // 4, cid % 4
        mod[l, :, j * 3072:(j + 1) * 3072] = res[cid]['mod']
    return mod


def split_mod(mod_l):
    names = ['sh_m', 'sc_m', 'gt_m', 'sh_f', 'sc_f', 'gt_f']
    return {n: mod_l[:, i * D:(i + 1) * D] for i, n in enumerate(names)}


def host_N1(xT_shards, g, m):
    in_maps = []
    for cid in range(NCORES):
        b, q = core_bq(cid)
        in_maps.append({'xT': xT_shards[cid], 'nv': stack_vecs([g, m['sc_m'][b], m['sh_m'][b]])})
    res = run('N1', in_maps)
    return [r['hT'] for r in res]


def host_C1(xT_shards, moT_shards, w_out, m, post_g, pre_g):
    in_maps = []
    for cid in range(NCORES):
        b, q = core_bq(cid)
        in_maps.append({'xT': xT_shards[cid], 'moT': moT_shards[cid], 'w_out': w_out,
                        'cv': stack_vecs([m['gt_m'][b], post_g, pre_g, m['sc_f'][b], m['sh_f'][b]])})
    res = run('C1', in_maps)
    return [r['xo'] for r in res], [r['h2T'] for r in res]


def host_C2a(h2T_shards, w_up, conv_w, conv_b):
    cw = np.zeros((128, NFF, 4), np.float32)
    for k in range(3):
        cw[:, :, k] = vec_fm(conv_w[k])
    cw[:, :, 3] = vec_fm(conv_b)
    in_maps = []
    for cid in range(NCORES):
        b, q = core_bq(cid)
        if q == 0:
            halo = np.zeros((D, 2), NPBF)
        else:
            halo = h2T_shards[cid - 1][:, -2:]
        in_maps.append({'h2e': np.ascontiguousarray(np.concatenate([halo, h2T_shards[cid]], axis=1)),
                        'w_up': w_up, 'cw': cw})
    res = run('C2a', in_maps)
    return [r['actT'] for r in res]


def host_C2b(actT_shards, xT_shards, w_down, m, post_g):
    in_maps = []
    for cid in range(NCORES):
        b, q = core_bq(cid)
        in_maps.append({'actT': actT_shards[cid], 'w_down': w_down, 'xT': xT_shards[cid],
                        'fv': stack_vecs([m['gt_f'][b], post_g])})
    res = run('C2b', in_maps)
    return [r['xo'] for r in res]


class V:
    def __init__(self, ap, res):
        self.ap = ap
        self.res = list(res) if isinstance(res, (list, tuple)) and not (len(res) and isinstance(res[0], str) and False) else [res]

    def __getitem__(self, idx):
        v = V.__new__(V)
        v.ap = self.ap[idx]
        v.res = self.res
        return v

    def r(self, res):
        v = V.__new__(V)
        v.ap = self.ap
        v.res = list(res)
        return v


def _rs(*vs):
    out = []
    for v in vs:
        if isinstance(v, V):
            out.extend(v.res)
    return out


def _isps(r):
    return isinstance(r, str) and r.startswith('ps:')


def _rw(ins, outs):
    rs = _rs(*ins)
    ws = _rs(*outs)
    return [r for r in rs if not _isps(r)], ws + [r for r in rs if _isps(r)]


def _ap(v):
    return v.ap if isinstance(v, V) else v


class KM(KB):
    def vsb(self, name, shape, dt=F32):
        return V(self.sb(name, shape, dt)[:], [name])

    def vmm(self, out, lhsT, rhs, start=True, stop=True):
        self.mm(_ap(out), _ap(lhsT), _ap(rhs), start, stop, *_rw((lhsT, rhs), (out,)))

    def vtr(self, out, in_, ident):
        self.tr(_ap(out), _ap(in_), _ap(ident), *_rw((in_, ident), (out,)))

    def vact(self, out, in_, func, bias=None, scale=None, accum_out=None):
        self.act(_ap(out), _ap(in_), func, *_rw((in_, bias, scale), (out, accum_out)),
                 bias=_ap(bias) if bias is not None else None, scale=_ap(scale) if scale is not None else None,
                 accum_out=_ap(accum_out) if accum_out is not None else None)

    def vtt(self, eng, out, in0, in1, op):
        self.tt(eng, _ap(out), _ap(in0), _ap(in1), op, *_rw((in0, in1), (out,)))

    def vts(self, eng, out, in0, s1, s2, op0, op1=None):
        self.ts(eng, _ap(out), _ap(in0), _ap(s1), _ap(s2), op0, op1, *_rw((in0, s1, s2), (out,)))

    def vstt(self, out, in0, scalar, in1, op0, op1):
        self.stt(_ap(out), _ap(in0), _ap(scalar), _ap(in1), op0, op1, *_rw((in0, scalar, in1), (out,)))

    def vcopy(self, eng, out, in_):
        self.copy(eng, _ap(out), _ap(in_), *_rw((in_,), (out,)))

    def vrecip(self, out, in_):
        self.recip(_ap(out), _ap(in_), *_rw((in_,), (out,)))

    def vdma(self, q, key, out, in_):
        return self.S.dma(q, key, [(_ap(out), _ap(in_))], _rs(in_), _rs(out))

    def vreduce(self, out, in_, op):
        o, i = _ap(out), _ap(in_)
        self.S.op('dve', lambda e: e.tensor_reduce(out=o, in_=i, axis=AX.X, op=op), *_rw((in_,), (out,)))


NBLK = 8
QSCALE = 128 ** -0.5
NFM = 8
NTM1 = 768
NTM2 = 130


def build_M():
    kb = KM()
    hT = kb.din('hT', [D, S_LEN], BF16)
    wfm_d = kb.din('wfm', [D, NFM * 128])
    wtm_d = kb.din('wtm', [D, NTM1 + NTM2])
    cwq_d = kb.din('cwq', [128, 3, 4])
    dnc_d = kb.din('dnc', [128, 2])
    dng_d = kb.din('dng', [64, 128])
    bias_d = kb.din('biasT', [128, 640])
    maskb_d = kb.din('maskb', [128, 640])
    am_d = kb.din('amc', [128, 256])
    sink_d = kb.din('sink', [128, 1])
    sgg_d = kb.din('sgg', [128, 128])
    sgw_d = kb.din('sgwT', [128, 128])
    sgb_d = kb.din('sgb', [128, 512])
    tri_d = kb.din('tri', [128, 128])
    id_d = kb.din('ident', [128, 128], BF16)
    lt_d = kb.din('lt', [64, 64], BF16)
    m64_d = kb.din('m64', [64, 9, 64])
    oa_d = kb.dout('out_a', [S_LEN, 128], BF16)
    ob_d = kb.dout('out_b', [S_LEN, 128], BF16)
    oc_d = kb.dout('out_c', [S_LEN, 128], BF16)
    od_d = kb.dout('out_d', [128, S_LEN], BF16)

    def load(name, d, shape, dt=F32, q='sp'):
        v = kb.vsb(name, shape, dt)
        kb.vdma(q, 'ld_' + name, v, V(d, []))
        return v
    wfm = kb.vsb('wfm', [128, KC, NFM * 128], BF16)
    wtm = kb.vsb('wtm', [128, KC, NTM1 + NTM2], BF16)
    for h2 in range(2):
        kb.vdma('pool', f'ld_wfm{h2}', wfm[:, h2 * 8:(h2 + 1) * 8, :].r([('wfm', h2)]),
                V(fm(wfm_d)[:, h2 * 8:(h2 + 1) * 8, :], []))
        kb.vdma('pool', f'ld_wtm{h2}', wtm[:, h2 * 8:(h2 + 1) * 8, :].r([('wtm', h2)]),
                V(fm(wtm_d)[:, h2 * 8:(h2 + 1) * 8, :], []))
    wfm = wfm.r([('wfm', 0), ('wfm', 1)])
    wtm = wtm.r([('wtm', 0), ('wtm', 1)])
    cwq = load('cwq', cwq_d[:, :, :], [128, 3, 4])
    dnc = load('dnc', dnc_d[:, :], [128, 2])
    dng = load('dng', dng_d[:, :], [64, 128])
    biasT = load('biasT', bias_d[:, :], [128, 640])
    maskb = load('maskb', maskb_d[:, :], [128, 640])
    amc = load('amc', am_d[:, :], [128, 256])
    sgg = load('sgg', sgg_d[:, :], [128, 128])
    sgw = load('sgw', sgw_d[:, :], [128, 128])
    sgb = load('sgb', sgb_d[:, 0:128], [128, 128])
    tri = load('tri', tri_d[:, :], [128, 128])
    ident = load('ident', id_d[:, :], [128, 128], BF16)
    lt = load('lt', lt_d[:, :], [64, 64], BF16)
    m64 = load('m64', m64_d[:, :, :], [64, 9, 64])
    negC, strict, causal, id64 = m64[:, 0, :], m64[:, 1, :], m64[:, 2, :], m64[:, 3, :]
    md8, md8T = m64[:, 4, :], m64[:, 5, :]
    mkT = [m64[:, 6, :], m64[:, 7, :], m64[:, 8, :]]
    onesb = kb.vsb('onesb', [128, 128], BF16)
    kb.S.op('pool', lambda e: e.memset(onesb.ap, 1.0), (), onesb.res)
    onec = kb.vsb('onec', [128, 1])
    kb.S.op('pool', lambda e: e.memset(onec.ap, 1.0), (), onec.res)
    epsc = kb.vsb('epsc', [128, 1])
    kb.S.op('pool', lambda e: e.memset(epsc.ap, EPS), (), epsc.res)
    kb.vtt('dve', biasT, biasT, maskb, ALU.add)
    BM = biasT
    sgwb = kb.vsb('sgwb', [128, 128], BF16)
    kb.vtt('dve', sgwb, sgw, tri, ALU.mult)
    negA = kb.vsb('negA', [128, 1])
    kb.vact(negA, dnc[:, 0:1], AF.Exp)
    kb.vts('dve', negA, negA, -1.0, None, ALU.mult)
    dtb = dnc[:, 1:2]
    sC = kb.vsb('sC', [128, 257])
    kb.vdma('sp', 'ld_sink', sC[:, 0:1].r(['sC_sink']), V(sink_d[:, :], []))

    KTb = kb.vsb('KTb', [128, S_LEN], BF16)
    KTc = kb.vsb('KTc', [128, S_LEN], BF16)
    Vb = kb.vsb('Vb', [128, 32, 128], BF16)
    Vc = kb.vsb('Vc', [128, 32, 128], BF16)
    raw = [kb.vsb(f'raw{g}', [128, 515]) for g in range(3)]
    for g in range(3):
        kb.S.op('pool', (lambda e, a=raw[g].ap[:, 0:3]: e.memset(a, 0.0)), (), [(f'raw{g}', 'h')])
    Sst = kb.vsb('Sst', [128, 128])
    Sbf = kb.vsb('Sbf', [128, 128], BF16)
    kb.S.op('pool', lambda e: e.memset(Sst.ap, 0.0), (), Sst.res)
    kb.S.op('pool', lambda e: e.memset(Sbf.ap, 0.0), (), Sbf.res)

    pA = [V(kb.ps(f'pA{i}', [128, 512])[:], [f'ps:A{i}']) for i in range(2)]
    pS = kb.ps('pS', [128, 1024])
    SB_RES = ['ps:S0', 'ps:S1']
    pSG = V(pS[:, 640:768], ['ps:S1'])
    pSc = V(pS[:, 768:1024], ['ps:S1'])
    pT = kb.ps('pT', [128, 1024], BF16)
    pKV = [V(pT[0:64, 640 + 192 * p:768 + 192 * p], ['ps:T']) for p in range(2)]
    pNA = [V(pT[0:64, 768 + 192 * p:832 + 192 * p], ['ps:T']) for p in range(2)]
    pset = []
    for p in range(2):
        pb_ = kb.ps(f'pset{p}', [128, 512])
        r = [f'ps:P{p}']
        pset.append({'G': V(pb_[:, 0:64], r), 'KK': V(pb_[0:64, 64:128], r), 'QK': V(pb_[0:64, 128:192], r),
                     'Mk': V(pb_[0:64, 192:256], r), 'MkT': V(pb_[0:64, 256:320], r),
                     'Tu': V(pb_[0:64, 320:384], r), 'Tv': V(pb_[0:64, 384:448], r),
                     'U': V(pb_[0:64, 192:320], r), 'wT': V(pb_[:, 0:64], r)})
    p7 = kb.ps('p7', [128, 512])
    pU = V(p7[0:64, 0:128], ['ps:7'])
    pwT = V(p7[:, 128:192], ['ps:7'])
    pwS = V(p7[0:64, 192:320], ['ps:7'])
    pO = V(p7[0:64, 320:448], ['ps:7'])
    pgc = V(p7[0:64, 448:456], ['ps:7'])
    pgl = V(p7[:, 456:464], ['ps:7'])
    GV0 = 0
    npA = [0]

    def nextA():
        npA[0] += 1
        return pA[npA[0] % 2]
    pP = [pA[0], pA[1], V(pS[:, 0:512], ['ps:S0']), V(pS[:, 512:1024], ['ps:S1'])]
    npP = [0]

    def nextP():
        npP[0] += 1
        return pP[npP[0] % 4]

    hb = [kb.vsb(f'hb{i}', [128, KC, 512], BF16) for i in range(2)]
    QTb = kb.vsb('QTb', [128, 512], BF16)
    QTc = kb.vsb('QTc', [128, 512], BF16)
    uT = kb.vsb('uT', [128, 512])
    cacc = [kb.vsb(f'cacc{g}', [128, 512]) for g in range(3)]
    xs = cacc
    sqb2 = [kb.vsb(f'sqb{i}', [128, 512], BF16) for i in range(2)]
    rn2 = [kb.vsb(f'rn{i}', [128, 512]) for i in range(2)]
    QTn = kb.vsb('QTn', [128, 512], BF16)
    KTn = kb.vsb('KTn', [128, 512], BF16)
    VTa = kb.vsb('VTa', [128, 512], BF16)
    gv4 = kb.vsb('gv4', [128, 4, 512], BF16)
    bnst = kb.vsb('bnst', [128, 6])
    bnag = kb.vsb('bnag', [128, 2])
    lrs = kb.vsb('lrs', [128, 1])
    vn = kb.vsb('vn', [128, 128])
    vtok = [kb.vsb(f'vtok{i}', [128, 128], BF16) for i in range(2)]
    sgt = kb.vsb('sgt', [128, 512])
    odst = [kb.vsb(f'odst{i}', [128, 512], BF16) for i in range(2)]
    sB = kb.vsb('sB', [128, 640])
    pB = kb.vsb('pB', [128, 640], BF16)
    PTs = kb.vsb('PTs', [128, 640], BF16)
    mx = kb.vsb('mx', [128, 1])
    nmx = kb.vsb('nmx', [128, 1])
    rsum = kb.vsb('rsum', [128, 1])
    rrec = kb.vsb('rrec', [128, 1])
    pC = kb.vsb('pC', [128, 257], BF16)
    mxc = kb.vsb('mxc', [128, 1])
    nmxc = kb.vsb('nmxc', [128, 1])
    rsumc = kb.vsb('rsumc', [128, 1])
    rrecc = kb.vsb('rrecc', [128, 1])
    obst = [kb.vsb(f'obst{i}', [128, 4, 128], BF16) for i in range(2)]
    ocst = [kb.vsb(f'ocst{i}', [128, 4, 128], BF16) for i in range(2)]
    oast = [kb.vsb(f'oast{i}', [64, 8, 128], BF16) for i in range(2)]
    ba = kb.vsb('ba', [64, 8, 2])
    beta = kb.vsb('beta', [64, 8])
    nbeta = kb.vsb('nbeta', [64, 8])
    spx = kb.vsb('spx', [64, 8])
    spa = kb.vsb('spa', [64, 8])
    spe = kb.vsb('spe', [64, 8])
    la = kb.vsb('la', [64, 8])
    lahi = kb.vsb('lahi', [64, 8], BF16)
    lalo = kb.vsb('lalo', [64, 8], BF16)
    gcol = kb.vsb('gcol', [64, 8])
    egc = kb.vsb('egc', [64, 8])
    bgc = kb.vsb('bgc', [64, 8])
    kds = kb.vsb('kds', [64, 8])
    egl = kb.vsb('egl', [128, 8])
    sgate = kb.vsb('sgate', [64, 8, 128])
    def two(name, shape, dt=F32):
        return [kb.vsb(f'{name}{i}', shape, dt) for i in range(2)]
    dfm, E_, Es_, Ec_ = two('dfm', [64, 64]), two('E', [64, 64]), two('Es', [64, 64]), two('Ec', [64, 64])
    Nb, NTb = two('Nb', [64, 64], BF16), two('NTb', [64, 64], BF16)
    Ma, MaT = two('Ma', [64, 64], BF16), two('MaT', [64, 64], BF16)
    Mb, MbT = two('Mb', [64, 64], BF16), two('MbT', [64, 64], BF16)
    Tb_, TTb = two('Tb', [64, 64], BF16), two('TTb', [64, 64], BF16)
    N8, N8T = two('N8', [64, 64], BF16), two('N8T', [64, 64], BF16)
    BT = [two(f'BT{k}', [64, 64], BF16) for k in range(3)]
    Yb = two('Yb', [64, 64], BF16)
    vbt, kbg = two('vbt', [64, 128], BF16), two('kbg', [64, 128], BF16)
    kdec = [kb.vsb(f'kdec{i}', [64, 128], BF16) for i in range(4)]
    def four(name, shape, dt=F32):
        return [kb.vsb(f'{name}{i}', shape, dt) for i in range(4)]
    ut, wTb = four('ut', [64, 128]), four('wTb', [128, 64], BF16)
    attb, attT = two('attb', [64, 64], BF16), four('attT', [64, 64], BF16)
    EGr, qdT = two('EGr', [128, 64]), four('qdT', [128, 64], BF16)
    vnew = two('vnew', [64, 128], BF16)
    gng = four('gng', [64, 128])
    olg = two('olg', [64, 1])
    osq = two('osq', [64, 128])
    oss, orr = two('oss', [64, 1]), two('orr', [64, 1])

    hTv = fm(hT)
    oav = oa_d.rearrange('(n c p) d -> n p c d', c=8, p=64)
    obv = ob_d.rearrange('(n i p) d -> n p i d', i=4, p=128)
    ocv = oc_d.rearrange('(n i p) d -> n p i d', i=4, p=128)

    kb.vdma('sp', 'ld_h0', hb[0], V(hTv[:, :, 0:512], []))
    for tb in range(NBLK):
        h = hb[tb % 2]
        if tb + 1 < NBLK:
            kb.vdma('sp', f'ld_h{(tb + 1) % 2}', hb[(tb + 1) % 2], V(hTv[:, :, (tb + 1) * 512:(tb + 2) * 512], []))
        bsl = slice(tb * 512, (tb + 1) * 512)
        for g in range(NFM):
            p = nextP()
            for kc in range(KC):
                kb.vmm(p, wfm[:, kc, g * 128:(g + 1) * 128], h[:, kc, :], kc == 0, kc == KC - 1)
            if g < 3:
                kb.vcopy('act', raw[g][:, 3:515].r([(f'raw{g}', 'c')]), p)
                rw = raw[g].r([(f'raw{g}', 'c'), (f'raw{g}', 'h')])
                kb.vts('dve', cacc[g], rw[:, 3:515], cwq[:, g, 3:4], None, ALU.mult)
                for k in (2, 1, 0):
                    kb.vstt(cacc[g], rw[:, k:k + 512], cwq[:, g, k:k + 1], cacc[g], ALU.mult, ALU.add)
                kb.S.op('pool', (lambda e, o=raw[g].ap[:, 0:3], i=raw[g].ap[:, 512:515]: e.tensor_copy(out=o, in_=i)),
                        [(f'raw{g}', 'c')], [(f'raw{g}', 'h')])
            elif g == 3:
                kb.vcopy('act', QTb, p)
            elif g == 4:
                kb.vcopy('act', KTb[:, bsl].r([('KTb', tb)]), p)
            elif g == 5:
                kb.vcopy('act', QTc, p)
            elif g == 6:
                kb.vcopy('act', KTc[:, bsl].r([('KTc', tb)]), p)
            else:
                kb.vact(uT, p, AF.Gelu_apprx_tanh)
        for i in range(4):
            j = tb * 4 + i
            tsl = slice(i * 128, (i + 1) * 128)
            p1 = nextP()
            for kc in range(KC):
                kb.vmm(p1, h[:, kc, tsl], wtm[:, kc, 0:512], kc == 0, kc == KC - 1)
            kb.vact(gv4[:, i, :].r([('gv4', i)]), p1, AF.Gelu_apprx_tanh)
            p2 = nextP()
            for kc in range(KC):
                kb.vmm(p2[:, 0:256], h[:, kc, tsl], wtm[:, kc, 512:768], kc == 0, kc == KC - 1)
            kb.vcopy('act', Vb[:, j, :].r([('Vb', j)]), p2[:, 0:128])
            kb.vcopy('act', Vc[:, j, :].r([('Vc', j)]), p2[:, 128:256])
        for c in range(8):
            p = nextP()
            pc = p[0:64, 0:NTM2]
            for kc in range(KC):
                kb.vmm(pc, h[:, kc, c * 64:(c + 1) * 64], wtm[:, kc, NTM1:NTM1 + NTM2], kc == 0, kc == KC - 1)
            kb.vact(sgate[:, c, :].r([('sgate', c)]), pc[:, 0:128], AF.Silu)
            kb.vcopy('dve', ba[:, c, :].r([('ba', c)]), pc[:, 128:130])
        kb.vact(xs[0], cacc[0], AF.Silu)
        kb.vact(xs[1], cacc[1], AF.Silu)
        kb.vact(VTa, cacc[2], AF.Silu)
        pq = []
        for g in range(2):
            kb.vact(sqb2[g], xs[g], AF.Square)
            p = nextP()
            kb.vmm(p, onesb, sqb2[g])
            pq.append(p)
        for g in range(2):
            kb.vact(rn2[g], pq[g], AF.Ln, bias=epsc, scale=1.0)
        for g in range(2):
            kb.vact(rn2[g], rn2[g], AF.Exp, scale=-0.5)
            kb.vtt('dve', QTn if g == 0 else KTn, xs[g], rn2[g], ALU.mult)
        bar = ba.r([('ba', c) for c in range(8)])
        kb.vact(beta, bar[:, :, 0], AF.Exp, scale=-1.0)
        kb.vts('dve', beta, beta, 1.0, None, ALU.add)
        kb.vrecip(beta, beta)
        kb.vts('dve', nbeta, beta, -1.0, None, ALU.mult)
        kb.vts('dve', spx, bar[:, :, 1], dtb[0:64, :], None, ALU.add)
        kb.vact(spa, spx, AF.Abs)
        kb.vact(spe, spa, AF.Exp, scale=-1.0)
        kb.vact(spe, spe, AF.Ln, bias=onec[0:64, :], scale=1.0)
        kb.vstt(spa, spx, 0.0, spe, ALU.max, ALU.add)
        kb.vts('dve', la, spa, negA[0:64, :], None, ALU.mult)
        kb.vcopy('dve', lahi, la)
        kb.vtt('dve', lalo, la, lahi, ALU.subtract)
        kb.vmm(pgc, lt, lahi, True, False)
        kb.vmm(pgc, lt, lalo, False, True)
        kb.vmm(pgl, onesb[0:64, :], lahi, True, False)
        kb.vmm(pgl, onesb[0:64, :], lalo, False, True)
        kb.vcopy('dve', gcol, pgc)
        kb.vact(egc, pgc, AF.Exp)
        kb.vact(egl, pgl, AF.Exp)
        kb.vtt('dve', kds, pgl[0:64, :], gcol, ALU.subtract)
        kb.vact(kds, kds, AF.Exp)
        kb.vtt('dve', bgc, beta, egc, ALU.mult)

        def chunk_par(c):
            n = tb * 8 + c
            q = n % 2
            q4 = n % 4
            ps_ = pset[q]
            csl = slice(c * 64, (c + 1) * 64)
            kb.vtr(pKV[q], KTn[:, csl], ident)
            kb.vts('dve', kbg[q], pKV[q], bgc[:, c:c + 1], None, ALU.mult)
            kb.vts('dve', kdec[q4], pKV[q], kds[:, c:c + 1], None, ALU.mult)
            kb.vmm(ps_['G'], V(lahi.ap[:, c:c + 1].to_broadcast([64, 128]), lahi.res), lt, True, False)
            kb.vmm(ps_['G'], V(lalo.ap[:, c:c + 1].to_broadcast([64, 128]), lalo.res), lt, False, True)
            kb.vmm(ps_['KK'], KTn[:, csl], KTn[:, csl])
            kb.vmm(ps_['QK'], QTn[:, csl], KTn[:, csl])
            yield
            kb.vtr(pKV[q], VTa[:, csl], ident)
            kb.vts('dve', vbt[q], pKV[q], beta[:, c:c + 1], None, ALU.mult)
            kb.vstt(dfm[q], ps_['G'][0:64, :], gcol[:, c:c + 1], negC, ALU.subtract, ALU.mult)
            kb.vact(EGr[q], ps_['G'], AF.Exp)
            yield
            kb.vact(E_[q], dfm[q], AF.Exp)
            kb.vstt(qdT[q4], QTn[:, csl], QSCALE, EGr[q], ALU.mult, ALU.mult)
            kb.vtt('pool', gng[q4], sgate[:, c, :].r([('sgate', c)]), dng, ALU.mult)
            yield
            kb.vtt('pool', Es_[q], E_[q], strict, ALU.mult)
            kb.vtt('pool', Ec_[q], E_[q], causal, ALU.mult)
            yield
            kb.vstt(Nb[q], ps_['KK'], nbeta[:, c:c + 1], Es_[q], ALU.mult, ALU.mult)
            kb.vstt(attb[q], ps_['QK'], QSCALE, Ec_[q], ALU.mult, ALU.mult)
            yield
            kb.vtr(pNA[q], Nb[q], ident[0:64, 0:64])
            kb.vtt('pool', N8[q], Nb[q], md8, ALU.mult)
            yield
            kb.vcopy('act', NTb[q], pNA[q])
            kb.vtt('pool', Tb_[q], N8[q], id64, ALU.add)
            yield
            kb.vtr(pNA[q], attb[q], ident[0:64, 0:64])
            kb.vtt('pool', N8T[q], NTb[q], md8T, ALU.mult)
            yield
            kb.vcopy('act', attT[q4], pNA[q])
            kb.vtt('pool', TTb[q], N8T[q], id64, ALU.add)
            for k in range(3):
                kb.vtt('pool', BT[k][q], NTb[q], mkT[k], ALU.mult)
            yield
            M, MT = N8[q], N8T[q]
            for lev in range(2):
                Mn, MnT = (Ma[q], MaT[q]) if lev == 0 else (Mb[q], MbT[q])
                kb.vmm(ps_['Mk'], MT, M)
                kb.vmm(ps_['MkT'], M, MT)
                yield
                kb.vcopy('act', Mn, ps_['Mk'])
                kb.vcopy('dve', MnT, ps_['MkT'])
                yield
                kb.vmm(ps_['Tu'], MnT, Tb_[q])
                kb.vmm(ps_['Tv'], Mn, TTb[q])
                yield
                kb.vtt('dve', Tb_[q], Tb_[q], ps_['Tu'], ALU.add)
                kb.vtt('dve', TTb[q], TTb[q], ps_['Tv'], ALU.add)
                yield
                M, MT = Mn, MnT
            for k in range(3):
                kb.vmm(ps_['Mk'], BT[k][q], Tb_[q])
                yield
                kb.vcopy('act', Yb[q], ps_['Mk'])
                yield
                if k < 2:
                    kb.vmm(ps_['Tu'], TTb[q], Yb[q])
                kb.vmm(ps_['MkT'], Yb[q], TTb[q])
                yield
                if k < 2:
                    kb.vtt('dve', Tb_[q], Tb_[q], ps_['Tu'], ALU.add)
                kb.vtt('dve', TTb[q], TTb[q], ps_['MkT'], ALU.add)
                yield
            kb.vmm(ps_['U'], TTb[q], vbt[q])
            kb.vmm(ps_['wT'], kbg[q], TTb[q])
            yield
            kb.vcopy('act', ut[q4], ps_['U'])
            kb.vcopy('act', wTb[q4], ps_['wT'])
            yield

        def chunk_seq(c):
            n = tb * 8 + c
            q = n % 2
            q4 = n % 4
            kb.vmm(pwS, wTb[q4], Sbf)
            yield
            kb.vtt('dve', vnew[q], ut[q4], pwS, ALU.subtract)
            yield
            yield
            kb.vmm(pO, qdT[q4], Sbf, True, False)
            kb.vmm(pO, attT[q4], vnew[q], False, True)
            pn = nextA()
            pSn = pn[:, 0:128]
            kb.vmm(pSn, kdec[q4], vnew[q])
            yield
            kb.vstt(Sbf, Sst, egl[:, c:c + 1], pSn, ALU.mult, ALU.add)
            kb.vstt(Sst, Sst, egl[:, c:c + 1], pSn, ALU.mult, ALU.add)
            yield
            kb.vact(osq[q], pO, AF.Square, accum_out=oss[q])
            kb.vact(olg[q], oss[q], AF.Ln, bias=epsc[0:64, :], scale=1.0 / 128)
            kb.vact(orr[q], olg[q], AF.Exp, scale=-0.5)
            kb.vstt(oast[tb % 2][:, c, :].r([(f'oast{tb % 2}', c)]), pO, orr[q], gng[q4], ALU.mult, ALU.mult)
            yield

        def tile_work(i):
            j = tb * 4 + i
            tsl = slice(i * 128, (i + 1) * 128)
            gvi = gv4[:, i, :].r([('gv4', i)])
            kb.S.op('dve', (lambda e, o=bnst.ap, a=gvi.ap: e.bn_stats(out=o, in_=a)), gvi.res, bnst.res)
            kb.S.op('dve', (lambda e, o=bnag.ap, a=bnst.ap: e.bn_aggr(out=o, in_=a)), bnst.res, bnag.res)
            kb.vact(lrs, bnag[:, 1:2], AF.Ln, bias=epsc, scale=1.0)
            kb.vact(lrs, lrs, AF.Exp, scale=-0.5)
            yield
            kb.vts('dve', vn, gvi[:, GV0:GV0 + 128], bnag[:, 0:1], lrs, ALU.subtract, ALU.mult)
            vt = vtok[i % 2]
            kb.vtt('pool', vt, vn, sgg, ALU.mult)
            yield
            yield
            kb.vmm(pSG, vt, sgwb)
            kb.vtt('dve', sgt[:, tsl].r([('sgt', i)]), pSG, sgb[:, 0:128], ALU.add)
            kb.vtt('pool', odst[tb % 2][:, tsl].r([(f'odst{tb % 2}', i)]), sgt[:, tsl].r([('sgt', i)]), uT[:, tsl], ALU.mult)
            yield
            kt0 = max(0, j - 4)
            nk = j + 1 - kt0
            W = nk * 128
            off = (5 - nk) * 128
            kres = [('KTb', t) for t in range(kt0 * 128 // 512, tb + 1)]
            for (a, b2) in ((0, min(W, 512)), (512, W)):
                if b2 > a:
                    kb.vmm(V(pS[:, a:b2], ['ps:S0' if a == 0 else 'ps:S1']), QTb[:, tsl],
                           KTb[:, kt0 * 128 + a:kt0 * 128 + b2].r(kres))
            kb.vstt(sB[:, 0:W], V(pS[:, 0:W], SB_RES if W > 512 else ['ps:S0']), QSCALE, BM[:, off:off + W],
                    ALU.mult, ALU.add)
            yield
            kb.vreduce(mx, sB[:, 0:W], ALU.max)
            kb.vts('dve', nmx, mx, -1.0, None, ALU.mult)
            yield
            kb.vact(pB[:, 0:W], sB[:, 0:W], AF.Exp, bias=nmx, scale=1.0, accum_out=rsum)
            yield
            yield
            for t in range(nk):
                kb.vtr(V(pT[:, t * 128:(t + 1) * 128], ['ps:T']), pB[:, t * 128:(t + 1) * 128], ident)
            kb.vcopy('dve', PTs[:, 0:W], V(pT[:, 0:W], ['ps:T']))
            yield
            yield
            pn = nextA()
            pOb = pn[:, 0:128]
            for t in range(nk):
                kb.vmm(pOb, PTs[:, t * 128:(t + 1) * 128], Vb[:, kt0 + t, :].r([('Vb', kt0 + t)]), t == 0, t == nk - 1)
            kb.vrecip(rrec, rsum)
            kb.vts('dve', obst[tb % 2][:, i, :].r([(f'obst{tb % 2}', i)]), pOb, rrec, None, ALU.mult)
            yield
            kt0 = max(0, j - 1)
            nk = j + 1 - kt0
            W = nk * 128
            off = (2 - nk) * 128
            kres = [('KTc', t) for t in range(kt0 * 128 // 512, tb + 1)]
            kb.vmm(pSc[:, 0:W], QTc[:, tsl], KTc[:, kt0 * 128:kt0 * 128 + W].r(kres))
            kb.vstt(sC[:, 1:1 + W].r(['sC']), pSc[:, 0:W], QSCALE, amc[:, off:off + W], ALU.mult, ALU.add)
            scr = sC[:, 0:1 + W].r(['sC', 'sC_sink'])
            yield
            kb.vreduce(mxc, scr, ALU.max)
            kb.vts('dve', nmxc, mxc, -1.0, None, ALU.mult)
            yield
            kb.vact(pC[:, 0:1 + W], scr, AF.Exp, bias=nmxc, scale=1.0, accum_out=rsumc)
            yield
            yield
            for t in range(nk):
                kb.vtr(V(pT[:, t * 128:(t + 1) * 128], ['ps:T']), pC[:, 1 + t * 128:1 + (t + 1) * 128], ident)
            kb.vcopy('dve', PTs[:, 0:W], V(pT[:, 0:W], ['ps:T']))
            yield
            yield
            pn = nextA()
            pOc = pn[:, 0:128]
            for t in range(nk):
                kb.vmm(pOc, PTs[:, t * 128:(t + 1) * 128], Vc[:, kt0 + t, :].r([('Vc', kt0 + t)]), t == 0, t == nk - 1)
            kb.vrecip(rrecc, rsumc)
            kb.vts('dve', ocst[tb % 2][:, i, :].r([(f'ocst{tb % 2}', i)]), pOc, rrecc, None, ALU.mult)
            yield

        def seq_pair(i):
            for c in (2 * i, 2 * i + 1):
                yield from chunk_seq(c)

        def rr(gens):
            gens = list(gens)
            while gens:
                for g in list(gens):
                    try:
                        next(g)
                    except StopIteration:
                        gens.remove(g)

        for i in range(4):
            ga = chunk_par(2 * i)
            next(ga)
            gl = [ga, chunk_par(2 * i + 1), tile_work(i)]
            if i > 0:
                gl.append(seq_pair(i - 1))
            rr(gl)
        rr([seq_pair(3)])
        kb.out_toks.append(kb.vdma('sp', f'st_d{tb % 2}', V(od_d[:, bsl], []),
                                   odst[tb % 2].r([(f'odst{tb % 2}', i) for i in range(4)])))
        kb.out_toks.append(kb.vdma('sp', f'st_a{tb % 2}', V(oav[tb], []),
                                   oast[tb % 2].r([(f'oast{tb % 2}', c) for c in range(8)])))
        kb.out_toks.append(kb.vdma('sp', f'st_b{tb % 2}', V(obv[tb], []),
                                   obst[tb % 2].r([(f'obst{tb % 2}', i) for i in range(4)])))
        kb.out_toks.append(kb.vdma('sp', f'st_c{tb % 2}', V(ocv[tb], []),
                                   ocst[tb % 2].r([(f'ocst{tb % 2}', i) for i in range(4)])))
    return kb.finish()


OFF = {'a_q': 0, 'a_k': 512, 'a_v': 1024, 'a_gate': 1536, 'a_beta': 2048, 'a_alpha': 2052, 'b_q': 2056,
       'b_k': 2568, 'b_v': 3080, 'c_q': 3592, 'c_k': 4104, 'c_v': 4360, 'd_u': 4616, 'd_v': 5128}


def m_consts():
    q = np.arange(128)[:, None]
    kk = np.arange(640)[None, :]
    hi = (q >= 64).astype(np.int64)
    validb = (kk // 64 >= hi) & (kk // 64 <= 8 + hi)
    maskb = np.where(validb, 0.0, -30000.0).astype(np.float32)
    idxb = np.clip(512 + q - kk, -256, 256) + 256
    kc = np.arange(256)[None, :]
    validc = (kc // 64 >= hi) & (kc // 64 <= 2 + hi)
    distc = np.abs(128 + q - kc).astype(np.float32)
    i64 = np.arange(64)
    cge = (i64[:, None] >= i64[None, :])
    ci, si = i64[:, None], i64[None, :]
    md8 = ((ci // 8 == si // 8) & (ci > si)).astype(np.float32)

    def mk(b):
        return ((ci // (2 * b) == si // (2 * b)) & (ci % (2 * b) >= b) & (si % (2 * b) < b)).astype(np.float32)
    m64 = np.stack([np.where(cge, -1.0, 0.0), (ci > si).astype(np.float32), cge.astype(np.float32), np.eye(64),
                    md8, md8.T, mk(8).T, mk(16).T, mk(32).T], axis=1).astype(np.float32)
    i128 = np.arange(128)
    return {
        'maskb': maskb, 'idxb': idxb, 'validc': validc, 'distc': distc,
        'm64': np.ascontiguousarray(m64),
        'lt': (i64[:, None] <= i64[None, :]).astype(NPBF),
        'ident': np.eye(128).astype(NPBF),
        'tri': (i128[:, None] <= i128[None, :]).astype(np.float32),
    }


def host_M(hT_shards, P, l):
    C = m_consts()
    w_in = P['w_in'][l]
    in_maps = []
    for cid in range(NCORES):
        b, hd = cid // 4, cid % 4
        kvh = hd // 2

        def cols(name, h, n=128):
            return w_in[:, OFF[name] + h * n:OFF[name] + (h + 1) * n]
        wfm = np.concatenate([cols('a_q', hd), cols('a_k', hd), cols('a_v', hd), cols('b_q', hd), cols('b_k', hd),
                              cols('c_q', hd), cols('c_k', kvh), cols('d_u', hd)], axis=1)
        others = [g for g in range(4) if g != hd]
        wtm = np.concatenate([cols('d_v', hd)] + [cols('d_v', g) for g in others] +
                             [cols('b_v', hd), cols('c_v', kvh), cols('a_gate', hd),
                              w_in[:, OFF['a_beta'] + hd:OFF['a_beta'] + hd + 1],
                              w_in[:, OFF['a_alpha'] + hd:OFF['a_alpha'] + hd + 1]], axis=1)
        cw = P['dn_conv_w'][l]
        cwq = np.stack([cw[:, g * 512 + hd * 128:g * 512 + (hd + 1) * 128].T for g in range(3)], axis=1)
        slope = np.float32(2.0 ** (-8.0 * (hd + 1) / 4))
        amc = np.where(C['validc'], -slope * C['distc'], np.float32(-30000.0)).astype(np.float32)
        in_maps.append({
            'hT': np.ascontiguousarray(np.concatenate(hT_shards[b * 4:(b + 1) * 4], axis=1)),
            'wfm': np.ascontiguousarray(wfm), 'wtm': np.ascontiguousarray(wtm),
            'cwq': np.ascontiguousarray(cwq.astype(np.float32)),
            'dnc': np.ascontiguousarray(np.broadcast_to(
                np.array([P['dn_a_log'][l][hd], P['dn_dt_bias'][l][hd]], np.float32)[None, :], (128, 2))),
            'dng': np.ascontiguousarray(np.broadcast_to(P['dn_norm_g'][l][None, :], (64, 128))),
            'biasT': np.ascontiguousarray(P['rel_bias'][l][hd][C['idxb']]),
            'maskb': C['maskb'], 'amc': amc,
            'sink': np.full((128, 1), P['sinks'][l][hd], np.float32),
            'sgg': np.ascontiguousarray(np.broadcast_to(P['sgu_norm_g'][l][hd * 128:(hd + 1) * 128][None, :], (128, 128))),
            'sgwT': np.ascontiguousarray(P['sgu_w'][l][hd].T),
            'sgb': np.ascontiguousarray(np.broadcast_to(np.tile(P['sgu_b'][l][hd], 4)[None, :], (128, 512))),
            'tri': C['tri'], 'ident': C['ident'], 'lt': C['lt'], 'm64': C['m64'],
        })
    res = run('M', in_maps)
    shards = []
    for cid in range(NCORES):
        b, q = core_bq(cid)
        tsl = slice(q * TOK, (q + 1) * TOK)
        rows = []
        for nm in ('out_a', 'out_b', 'out_c'):
            for hd in range(4):
                rows.append(res[b * 4 + hd][nm][tsl, :].T)
        for hd in range(4):
            rows.append(res[b * 4 + hd]['out_d'][:, tsl])
        shards.append(np.ascontiguousarray(np.concatenate(rows, axis=0)))
    return shards


def kernel(x, c, ada_w, ada_b, mix_pre_g, mix_post_g, w_in, dn_conv_w, dn_a_log, dn_dt_bias, dn_norm_g,
           rel_bias, sinks, sgu_norm_g, sgu_w, sgu_b, w_out, ffn_pre_g, ffn_post_g, ffn_w_up, ffn_conv_w,
           ffn_conv_b, ffn_w_down):
    f = lambda a: np.asarray(a, dtype=np.float32)
    x, c, ada_w, ada_b = f(x), f(c), f(ada_w), f(ada_b)
    P = {'w_in': f(w_in), 'dn_conv_w': f(dn_conv_w), 'dn_a_log': f(dn_a_log), 'dn_dt_bias': f(dn_dt_bias),
         'dn_norm_g': f(dn_norm_g), 'rel_bias': f(rel_bias), 'sinks': f(sinks), 'sgu_norm_g': f(sgu_norm_g),
         'sgu_w': f(sgu_w), 'sgu_b': f(sgu_b)}
    mix_pre_g, mix_post_g, ffn_pre_g, ffn_post_g = f(mix_pre_g), f(mix_post_g), f(ffn_pre_g), f(ffn_post_g)
    w_out, ffn_w_up, ffn_conv_w, ffn_conv_b, ffn_w_down = f(w_out), f(ffn_w_up), f(ffn_conv_w), f(ffn_conv_b), f(ffn_w_down)
    mod = host_L0(c, ada_w, ada_b)
    xs = [tok_shard_T(x, cid) for cid in range(NCORES)]
    for l in range(2):
        m = split_mod(mod[l])
        hs = host_N1(xs, mix_pre_g[l], m)
        mo = host_M(hs, P, l)
        xs, h2 = host_C1(xs, mo, np.ascontiguousarray(w_out[l]), m, mix_post_g[l], ffn_pre_g[l])
        acts = host_C2a(h2, np.ascontiguousarray(ffn_w_up[l]), ffn_conv_w[l], ffn_conv_b[l])
        xs = host_C2b(acts, xs, np.ascontiguousarray(ffn_w_down[l]), m, ffn_post_g[l])
    out = np.zeros((2, S_LEN, D), np.float32)
    for cid in range(NCORES):
        b, q = core_bq(cid)
        out[b, q * TOK:(q + 1) * TOK, :] = xs[cid].T
    return out
```

```python
import numpy as np
import ml_dtypes
from contextlib import ExitStack
import concourse.bass as bass
import concourse.mybir as mybir
from concourse.bass_utils import run_bass_kernel_spmd

F32 = mybir.dt.float32
BF16 = mybir.dt.bfloat16
AF = mybir.ActivationFunctionType
ALU = mybir.AluOpType
AX = mybir.AxisListType
NPBF = ml_dtypes.bfloat16

NCORES = 8
D = 2048
KC = 16
S_LEN = 4096
TOK = 1024
DFF = 5504
NFF = 43
EPS = 1e-6
ENG = ['pe', 'dve', 'act', 'pool', 'sp']


class Sched:
    def __init__(self, nc, stack):
        self.nc = nc
        self.stack = stack
        self.plan = {e: [] for e in ENG}
        self.cnt = {e: 0 for e in ENG}
        self.seen = {e: {} for e in ENG}
        self.sems = {}
        self.dcnt = {}
        self.res = {}
        for e in ENG:
            self.sems[e] = stack.enter_context(nc.semaphore('s_' + e))

    def _st(self, r):
        st = self.res.get(r)
        if st is None:
            st = {'w': None, 'r': {}}
            self.res[r] = st
        return st

    def _deps(self, eng, reads, writes):
        deps = {}

        def add(k, v):
            if deps.get(k, 0) < v:
                deps[k] = v
        for r in reads:
            w = self._st(r)['w']
            if w is not None:
                add(*w)
        for wr in writes:
            st = self._st(wr)
            if st['w'] is not None:
                add(*st['w'])
            for k, v in st['r'].items():
                if k == eng:
                    continue
                add(k, v)
        return deps

    def _waits(self, eng, deps):
        waits = []
        for k, v in deps.items():
            if k == eng and eng == 'pe':
                continue
            if self.seen[eng].get(k, 0) >= v:
                continue
            self.seen[eng][k] = v
            waits.append((k, v))
        return waits

    def _commit(self, tok, reads, writes):
        k, v = tok
        for r in reads:
            st = self._st(r)
            if st['r'].get(k, 0) < v:
                st['r'][k] = v
        for w in writes:
            st = self._st(w)
            st['w'] = tok
            st['r'] = {}

    def op(self, eng, fn, reads=(), writes=()):
        deps = self._deps(eng, reads, writes)
        waits = self._waits(eng, deps)
        self.cnt[eng] += 1
        tok = (eng, self.cnt[eng])
        self.plan[eng].append((waits, fn, eng, 1))
        self._commit(tok, reads, writes)

    def dma(self, q, key, pairs, reads=(), writes=()):
        if key not in self.sems:
            self.sems[key] = self.stack.enter_context(self.nc.semaphore('d_' + str(key)))
            self.dcnt[key] = 0
        deps = self._deps(q, reads, writes)
        waits = self._waits(q, deps)
        for i, (o, a) in enumerate(pairs):
            self.dcnt[key] += 16
            self.plan[q].append((waits if i == 0 else [],
                                 (lambda e, o=o, a=a: e.dma_start(out=o, in_=a)), key, 16))
        tok = (key, self.dcnt[key])
        self._commit(tok, reads, writes)
        return tok

    def wait_tok(self, eng, tok):
        waits = self._waits(eng, {tok[0]: tok[1]})
        if waits:
            self.plan[eng].append((waits, None, None, 0))

    def emit(self):
        nc = self.nc
        engs = {'pe': 'tensor', 'dve': 'vector', 'act': 'scalar', 'pool': 'gpsimd', 'sp': 'sync'}
        with nc.Block() as block:
            for e in ENG:
                plan = self.plan[e]
                if not plan:
                    continue

                def body(engine, plan=plan):
                    for waits, fn, k, inc in plan:
                        for (wk, wv) in waits:
                            engine.wait_ge(self.sems[wk], wv)
                        if fn is not None:
                            fn(engine).then_inc(self.sems[k], inc)
                getattr(block, engs[e])(body)


class KB:
    def __init__(self):
        self.nc = bass.Bass("TRN2", target_bir_lowering=False)
        self.st = ExitStack()
        self.S = Sched(self.nc, self.st)
        self.out_toks = []
        self.nbank = 0

    def din(self, name, shape, dt=F32):
        return self.nc.dram_tensor(name, list(shape), dt, kind="ExternalInput").ap()

    def dout(self, name, shape, dt=F32):
        return self.nc.dram_tensor(name, list(shape), dt, kind="ExternalOutput").ap()

    def sb(self, name, shape, dt=F32):
        return self.st.enter_context(self.nc.sbuf_tensor('sb_' + name, list(shape), dt))

    def ps(self, name, shape, dt=F32):
        return self.st.enter_context(self.nc.psum_tensor('ps_' + name, list(shape), dt))

    def finish(self):
        last = {}
        for k, v in self.out_toks:
            last[k] = max(last.get(k, 0), v)
        for k, v in last.items():
            self.S.wait_tok('sp', (k, v))
        self.S.emit()
        self.st.close()
        return self.nc

    def mm(self, out, lhsT, rhs, start, stop, r, w):
        self.S.op('pe', lambda e: e.matmul(out, lhsT=lhsT, rhs=rhs, start=start, stop=stop), r, w)

    def tr(self, out, in_, ident, r, w):
        self.S.op('pe', lambda e: e.transpose(out=out, in_=in_, identity=ident), r, w)

    def act(self, out, in_, func, r, w, bias=None, scale=None, accum_out=None):
        kw = {}
        if bias is not None:
            kw['bias'] = bias
        if scale is not None:
            kw['scale'] = scale
        if accum_out is not None:
            kw['accum_out'] = accum_out
        self.S.op('act', lambda e: e.activation(out=out, in_=in_, func=func, **kw), r, w)

    def tt(self, eng, out, in0, in1, op, r, w):
        self.S.op(eng, lambda e: e.tensor_tensor(out=out, in0=in0, in1=in1, op=op), r, w)

    def ts(self, eng, out, in0, s1, s2, op0, op1, r, w, accum_out=None):
        if op1 is None:
            self.S.op(eng, lambda e: e.tensor_scalar(out=out, in0=in0, scalar1=s1, scalar2=None, op0=op0), r, w)
        elif accum_out is None:
            self.S.op(eng, lambda e: e.tensor_scalar(out=out, in0=in0, scalar1=s1, scalar2=s2, op0=op0, op1=op1), r, w)
        else:
            self.S.op(eng, lambda e: e.tensor_scalar(out=out, in0=in0, scalar1=s1, scalar2=s2, op0=op0, op1=op1,
                                                     accum_out=accum_out), r, w)

    def stt(self, out, in0, scalar, in1, op0, op1, r, w):
        self.S.op('dve', lambda e: e.scalar_tensor_tensor(out=out, in0=in0, scalar=scalar, in1=in1,
                                                          op0=op0, op1=op1), r, w)

    def copy(self, eng, out, in_, r, w):
        if eng == 'act':
            self.S.op('act', lambda e: e.copy(out=out, in_=in_), r, w)
        else:
            self.S.op(eng, lambda e: e.tensor_copy(out=out, in_=in_), r, w)

    def memset(self, eng, ap, val, w):
        self.S.op(eng, lambda e: e.memset(ap, val), (), w)

    def recip(self, out, in_, r, w):
        self.S.op('dve', lambda e: e.reciprocal(out=out, in_=in_), r, w)

    def dma(self, q, key, out, in_, r=(), w=()):
        return self.S.dma(q, key, [(out, in_)], r, w)


def fm(ap):
    return ap.rearrange('(kc p) t -> p kc t', p=128)


def vec_fm(v):
    v = np.asarray(v)
    return np.ascontiguousarray(v.reshape(-1, 128).T)


def setup_consts(kb):
    ones = kb.sb('ones', [128, 128], BF16)
    kb.memset('pool', ones[:], 1.0, ['ones'])
    epst = kb.sb('epst', [128, 1])
    kb.memset('pool', epst[:], EPS, ['epst'])
    return ones, epst


def rstd_from_ss(kb, ssp, ssp_res, out, out_res, epst, n, tag):
    kb.act(out, ssp, AF.Sqrt, [ssp_res, 'epst'], [out_res], bias=epst[:], scale=1.0 / n)
    kb.recip(out, out, [out_res], [out_res])


def norm_bufs(kb):
    return {
        'ssp': kb.ps('nm_ssp', [128, 512]),
        'sq': [kb.sb(f'nm_sq{i}', [128, 512], BF16) for i in range(2)],
        't': [kb.sb(f'nm_t{i}', [128, 512]) for i in range(2)],
        'rstd': kb.sb('nm_rstd', [128, 512]),
    }


def norm_mod_half(kb, nb, xt, xres, hf, gs, gsres, sh, shres, ones, epst, hb, hbres):
    sl = slice(hf * 512, (hf + 1) * 512)
    ssp, rstd = nb['ssp'], nb['rstd']
    for kc in range(KC):
        q = nb['sq'][kc % 2]
        qn = f'nm_sq{kc % 2}'
        kb.act(q[:], xt[:, kc, sl], AF.Square, [(xres, kc, hf)], [qn])
        kb.mm(ssp[:], ones[:], q[:], kc == 0, kc == KC - 1, ['ones', qn], ['nm_ssp'])
    rstd_from_ss(kb, ssp[:], 'nm_ssp', rstd[:], 'nm_rstd', epst, D, 'nm')
    for kc in range(KC):
        t = nb['t'][kc % 2]
        tn = f'nm_t{kc % 2}'
        kb.stt(t[:], xt[:, kc, sl], gs[:, kc:kc + 1], rstd[:], ALU.mult, ALU.mult,
               [(xres, kc, hf), gsres, 'nm_rstd'], [tn])
        kb.act(hb[:, kc, :], t[:], AF.Identity, [tn, shres], [(hbres, kc)], bias=sh[:, kc:kc + 1])


def build_L0():
    kb = KB()
    cT = kb.din('cT', [128, KC, 2])
    W = kb.din('W', [D, 3072])
    bias = kb.din('bias', [1, 3072])
    mod = kb.dout('mod', [2, 3072])
    sc = kb.sb('sc', [128, KC, 2])
    scs = kb.sb('scs', [128, KC, 2], BF16)
    bt = kb.sb('bt', [2, 3072])
    ot = kb.sb('ot', [2, 3072])
    wb = [kb.sb(f'wb{i}', [128, KC, 512], BF16) for i in range(2)]
    pb = [kb.ps(f'pb{i}', [2, 512]) for i in range(2)]
    kb.dma('sp', 'ld_c', sc[:], cT[:, :, :], w=['sc'])
    kb.dma('sp', 'ld_b', bt[:], bias.partition_broadcast(2), w=['bt'])
    kb.act(scs[:], sc[:], AF.Silu, ['sc'], ['scs'])
    Wv = fm(W)
    for j in range(6):
        i = j % 2
        kb.dma('pool', f'ld_w{i}', wb[i][:], Wv[:, :, j * 512:(j + 1) * 512], w=[f'wb{i}'])
        for kc in range(KC):
            kb.mm(pb[i][:], scs[:, kc, :], wb[i][:, kc, :], kc == 0, kc == KC - 1, ['scs', f'wb{i}'], [f'pb{i}'])
        kb.tt('dve', ot[:, j * 512:(j + 1) * 512], pb[i][:], bt[:, j * 512:(j + 1) * 512], ALU.add,
              [f'pb{i}', 'bt'], [('ot', j)])
    kb.out_toks.append(kb.dma('sp', 'st', mod[:, :], ot[:], r=[('ot', j) for j in range(6)]))
    return kb.finish()


def build_N1():
    kb = KB()
    xT = kb.din('xT', [D, TOK])
    nv = kb.din('nv', [128, 3, KC])
    hT = kb.dout('hT', [D, TOK], BF16)
    ones, epst = setup_consts(kb)
    xt = kb.sb('xt', [128, KC, TOK])
    nvt = kb.sb('nvt', [128, 3, KC])
    gs = kb.sb('gs', [128, KC])
    kb.dma('sp', 'ld_nv', nvt[:], nv[:, :, :], w=['nv'])
    xv = fm(xT)
    for hf in range(2):
        for g4 in range(4):
            kb.dma('sp', f'ld_x{hf}{g4}', xt[:, g4 * 4:(g4 + 1) * 4, hf * 512:(hf + 1) * 512],
                   xv[:, g4 * 4:(g4 + 1) * 4, hf * 512:(hf + 1) * 512],
                   w=[('x', kc, hf) for kc in range(g4 * 4, g4 * 4 + 4)])
    kb.ts('dve', gs[:], nvt[:, 1, :], 1.0, None, ALU.add, None, ['nv'], ['gs'])
    kb.tt('dve', gs[:], gs[:], nvt[:, 0, :], ALU.mult, ['gs', 'nv'], ['gs'])
    nb = norm_bufs(kb)
    hbs = [kb.sb(f'hb{i}', [128, KC, 512], BF16) for i in range(2)]
    hv = fm(hT)
    for hf in range(2):
        norm_mod_half(kb, nb, xt, 'x', hf, gs, 'gs', nvt[:, 2, :], 'nv', ones, epst, hbs[hf], f'hb{hf}')
        kb.out_toks.append(kb.dma('sp', f'st_h{hf}', hv[:, :, hf * 512:(hf + 1) * 512], hbs[hf][:],
                                  r=[(f'hb{hf}', kc) for kc in range(KC)]))
    return kb.finish()


def build_C1():
    kb = KB()
    xT = kb.din('xT', [D, TOK])
    moT = kb.din('moT', [D, TOK], BF16)
    w_out = kb.din('w_out', [D, D])
    cv = kb.din('cv', [128, 5, KC])
    xo = kb.dout('xo', [D, TOK])
    h2T = kb.dout('h2T', [D, TOK], BF16)
    ones, epst = setup_consts(kb)
    xt = kb.sb('xt', [128, KC, TOK])
    mo = kb.sb('mo', [128, KC, TOK], BF16)
    cvt = kb.sb('cvt', [128, 5, KC])
    gtg = kb.sb('gtg', [128, KC])
    gs = kb.sb('gs', [128, KC])
    kb.dma('sp', 'ld_cv', cvt[:], cv[:, :, :], w=['cv'])
    xv = fm(xT)
    mv = fm(moT)
    for hf in range(2):
        for g4 in range(4):
            kb.dma('sp', f'ld_m{hf}{g4}', mo[:, g4 * 4:(g4 + 1) * 4, hf * 512:(hf + 1) * 512],
                   mv[:, g4 * 4:(g4 + 1) * 4, hf * 512:(hf + 1) * 512],
                   w=[('mo', kc, hf) for kc in range(g4 * 4, g4 * 4 + 4)])
    for hf in range(2):
        for g4 in range(4):
            kb.dma('sp', f'ld_x{hf}{g4}', xt[:, g4 * 4:(g4 + 1) * 4, hf * 512:(hf + 1) * 512],
                   xv[:, g4 * 4:(g4 + 1) * 4, hf * 512:(hf + 1) * 512],
                   w=[('x', kc, hf) for kc in range(g4 * 4, g4 * 4 + 4)])
    kb.tt('dve', gtg[:], cvt[:, 0, :], cvt[:, 1, :], ALU.mult, ['cv'], ['gtg'])
    kb.ts('dve', gs[:], cvt[:, 3, :], 1.0, None, ALU.add, None, ['cv'], ['gs'])
    kb.tt('dve', gs[:], gs[:], cvt[:, 2, :], ALU.mult, ['gs', 'cv'], ['gs'])
    wg = [kb.sb(f'wg{i}', [128, KC, 256], BF16) for i in range(2)]
    yh = kb.sb('yh', [128, KC, 512])
    psy = [kb.ps(f'psy{i}', [128, 512]) for i in range(3)]
    ssy = kb.ps('ssy', [128, 512])
    sqb = [kb.sb(f'sq{i}', [128, 512], BF16) for i in range(2)]
    tb = [kb.sb(f'tb{i}', [128, 512]) for i in range(2)]
    rstd = kb.sb('rstd', [128, 512])
    nb = norm_bufs(kb)
    hbs = [kb.sb(f'hb{i}', [128, KC, 512], BF16) for i in range(2)]
    wv = fm(w_out)
    xov = fm(xo)
    hv = fm(h2T)
    nld = 0
    for hf in range(2):
        sl = slice(hf * 512, (hf + 1) * 512)
        pend = None
        for dc in range(KC):
            if dc % 2 == 0:
                wi = nld % 2
                nld += 1
                kb.dma('pool', f'ld_w{wi}', wg[wi][:], wv[:, :, dc * 128:dc * 128 + 256], w=[f'wg{wi}'])
            wcur = wg[wi]
            p = psy[dc % 3]
            pn = f'psy{dc % 3}'
            for kc in range(KC):
                kb.mm(p[:], wcur[:, kc, (dc % 2) * 128:(dc % 2) * 128 + 128], mo[:, kc, sl], kc == 0, kc == KC - 1,
                      [f'wg{wi}', ('mo', kc, hf)], [pn])
            kb.copy('act', yh[:, dc, :], p[:], [pn], [('yh', dc)])
            q = sqb[dc % 2]
            qn = f'sq{dc % 2}'
            kb.act(q[:], p[:], AF.Square, [pn], [qn])
            if pend is not None:
                pend()
            pend = (lambda q=q, qn=qn, dc=dc: kb.mm(ssy[:], ones[:], q[:], dc == 0, dc == KC - 1, ['ones', qn], ['ssy']))
        pend()
        rstd_from_ss(kb, ssy[:], 'ssy', rstd[:], 'rstd', epst, D, 'y')
        for dc in range(KC):
            t = tb[dc % 2]
            tn = f'tb{dc % 2}'
            kb.stt(t[:], yh[:, dc, :], gtg[:, dc:dc + 1], rstd[:], ALU.mult, ALU.mult,
                   [('yh', dc), 'gtg', 'rstd'], [tn])
            kb.tt('dve', xt[:, dc, sl], xt[:, dc, sl], t[:], ALU.add, [('x', dc, hf), tn], [('x', dc, hf)])
        kb.out_toks.append(kb.dma('sp', f'st_x{hf}', xov[:, :, sl], xt[:, :, sl],
                                  r=[('x', dc, hf) for dc in range(KC)]))
        norm_mod_half(kb, nb, xt, 'x', hf, gs, 'gs', cvt[:, 4, :], 'cv', ones, epst, hbs[hf], f'hb{hf}')
        kb.out_toks.append(kb.dma('sp', f'st_h{hf}', hv[:, :, sl], hbs[hf][:],
                                  r=[(f'hb{hf}', kc) for kc in range(KC)]))
    return kb.finish()


def build_C2a():
    kb = KB()
    h2e = kb.din('h2e', [D, TOK + 2], BF16)
    w_up = kb.din('w_up', [D, 2 * DFF])
    cw = kb.din('cw', [128, NFF, 4])
    actT = kb.dout('actT', [DFF, TOK], BF16)
    he = kb.sb('he', [128, KC, TOK + 2], BF16)
    cwt = kb.sb('cwt', [128, NFF, 4])
    kb.dma('sp', 'ld_cw', cwt[:], cw[:, :, :], w=['cw'])
    hv = fm(h2e)
    for g4 in range(4):
        kb.dma('sp', f'ld_h{g4}', he[:, g4 * 4:(g4 + 1) * 4, :], hv[:, g4 * 4:(g4 + 1) * 4, :],
               w=[('he', kc) for kc in range(g4 * 4, g4 * 4 + 4)])
    wa = [kb.sb(f'wa{i}', [128, KC, 256], BF16) for i in range(2)]
    wgt = [kb.sb(f'wgt{i}', [128, KC, 256], BF16) for i in range(2)]
    psa = [[kb.ps(f'psa{i}{s}', [128, 512]) for s in range(3)] for i in range(2)]
    psg = [kb.ps(f'psg{s}', [128, 512]) for s in range(2)]
    aext = [kb.sb(f'aext{i}', [128, TOK + 2]) for i in range(2)]
    gS = [kb.sb(f'gS{i}', [128, TOK]) for i in range(2)]
    acc = [kb.sb(f'acc{i}', [128, TOK]) for i in range(2)]
    ga = [kb.sb(f'ga{i}', [128, TOK]) for i in range(2)]
    ab = [kb.sb(f'ab{i}', [128, 2, TOK], BF16) for i in range(2)]
    wv = fm(w_up)
    av = actT.rearrange('(j p) t -> p j t', p=128)
    W3 = 342
    for jb in range(22):
        bi = jb % 2
        ncol = 256 if jb < 21 else 128
        kb.dma('pool', f'ld_wa{bi}', wa[bi][:, :, 0:ncol], wv[:, :, jb * 256:jb * 256 + ncol], w=[f'wa{bi}'])
        kb.dma('pool', f'ld_wg{bi}', wgt[bi][:, :, 0:ncol], wv[:, :, DFF + jb * 256:DFF + jb * 256 + ncol],
               w=[f'wgt{bi}'])
        njj = ncol // 128
        for jj in range(njj):
            j = jb * 2 + jj
            i = j % 2
            cs = slice(jj * 128, jj * 128 + 128)
            for s in range(3):
                for kc in range(KC):
                    kb.mm(psa[i][s][:, 0:W3], wa[bi][:, kc, cs], he[:, kc, s * W3:(s + 1) * W3], kc == 0, kc == KC - 1,
                          [f'wa{bi}', ('he', kc)], [f'psa{i}{s}'])
            for s in range(2):
                for kc in range(KC):
                    kb.mm(psg[s][:], wgt[bi][:, kc, cs], he[:, kc, 2 + s * 512:2 + (s + 1) * 512], kc == 0, kc == KC - 1,
                          [f'wgt{bi}', ('he', kc)], [f'psg{s}'])
            for s in range(3):
                kb.copy('act', aext[i][:, s * W3:(s + 1) * W3], psa[i][s][:, 0:W3], [f'psa{i}{s}'], [(f'aext{i}', s)])
            for s in range(2):
                kb.copy('act', gS[i][:, s * 512:(s + 1) * 512], psg[s][:], [f'psg{s}'], [(f'gS{i}', s)])
            ar = [(f'aext{i}', s) for s in range(3)]
            kb.ts('dve', acc[i][:], aext[i][:, 2:TOK + 2], cwt[:, j, 2:3], cwt[:, j, 3:4], ALU.mult, ALU.add,
                  ar + ['cw'], [f'acc{i}'])
            kb.stt(acc[i][:], aext[i][:, 1:TOK + 1], cwt[:, j, 1:2], acc[i][:], ALU.mult, ALU.add,
                   ar + ['cw', f'acc{i}'], [f'acc{i}'])
            kb.stt(acc[i][:], aext[i][:, 0:TOK], cwt[:, j, 0:1], acc[i][:], ALU.mult, ALU.add,
                   ar + ['cw', f'acc{i}'], [f'acc{i}'])
            kb.act(ga[i][:], acc[i][:], AF.Gelu_apprx_tanh, [f'acc{i}'], [f'ga{i}'])
            kb.tt('dve', ab[bi][:, jj, :], ga[i][:], gS[i][:], ALU.mult,
                  [f'ga{i}', (f'gS{i}', 0), (f'gS{i}', 1)], [(f'ab{bi}', jj)])
        kb.out_toks.append(kb.dma('sp', f'st_a{bi}', av[:, jb * 2:jb * 2 + njj, :], ab[bi][:, 0:njj, :],
                                  r=[(f'ab{bi}', jj) for jj in range(njj)]))
    return kb.finish()


def build_C2b():
    kb = KB()
    actT = kb.din('actT', [DFF, TOK], BF16)
    w_down = kb.din('w_down', [DFF, D])
    xT = kb.din('xT', [D, TOK])
    fv = kb.din('fv', [128, 2, KC])
    xo = kb.dout('xo', [D, TOK])
    ones, epst = setup_consts(kb)
    at = kb.sb('at', [128, NFF, TOK], BF16)
    y2 = kb.sb('y2', [128, KC, TOK])
    fvt = kb.sb('fvt', [128, 2, KC])
    gtg = kb.sb('gtg', [128, KC])
    kb.dma('sp', 'ld_fv', fvt[:], fv[:, :, :], w=['fv'])
    kb.tt('dve', gtg[:], fvt[:, 0, :], fvt[:, 1, :], ALU.mult, ['fv'], ['gtg'])
    av = actT.rearrange('(j p) t -> p j t', p=128)
    for g in range(0, NFF, 8):
        n = min(8, NFF - g)
        kb.dma('sp', f'ld_a{g}', at[:, g:g + n, :], av[:, g:g + n, :], w=[('at', j) for j in range(g, g + n)])
    wd = [kb.sb(f'wd{i}', [128, NFF, 128], BF16) for i in range(2)]
    psy = [kb.ps(f'psy{i}', [128, 512]) for i in range(3)]
    ss = [kb.ps(f'ss{i}', [128, 512]) for i in range(2)]
    sqb = [kb.sb(f'sq{i}', [128, 512], BF16) for i in range(2)]
    rstd = kb.sb('rstd', [128, TOK])
    wv = w_down.rearrange('(j p) n -> p j n', p=128)
    n = 0
    pend = None
    for dc in range(KC):
        wi = dc % 2
        kb.dma('pool', f'ld_w{wi}', wd[wi][:], wv[:, :, dc * 128:(dc + 1) * 128], w=[f'wd{wi}'])
        for hf in range(2):
            p = psy[n % 3]
            pn = f'psy{n % 3}'
            for j in range(NFF):
                kb.mm(p[:], wd[wi][:, j, :], at[:, j, hf * 512:(hf + 1) * 512], j == 0, j == NFF - 1,
                      [f'wd{wi}', ('at', j)], [pn])
            kb.copy('act', y2[:, dc, hf * 512:(hf + 1) * 512], p[:], [pn], [('y2', dc, hf)])
            q = sqb[n % 2]
            qn = f'sq{n % 2}'
            kb.act(q[:], p[:], AF.Square, [pn], [qn])
            if pend is not None:
                pend()
            pend = (lambda q=q, qn=qn, dc=dc, hf=hf: kb.mm(ss[hf][:], ones[:], q[:], dc == 0, dc == KC - 1,
                                                         ['ones', qn], [f'ss{hf}']))
            n += 1
    pend()
    for hf in range(2):
        rstd_from_ss(kb, ss[hf][:], f'ss{hf}', rstd[:, hf * 512:(hf + 1) * 512], ('rstd', hf), epst, D, f'y{hf}')
    xin = [kb.sb(f'xin{i}', [128, TOK]) for i in range(4)]
    xv = fm(xT)
    xov = fm(xo)
    for dc in range(4):
        kb.dma('sp', f'ld_x{dc}', xin[dc][:], xv[:, dc, :], w=[f'xin{dc}'])
    for dc in range(KC):
        xi = dc % 4
        yr = [('y2', dc, 0), ('y2', dc, 1)]
        kb.stt(y2[:, dc, :], y2[:, dc, :], gtg[:, dc:dc + 1], rstd[:], ALU.mult, ALU.mult,
               yr + ['gtg', ('rstd', 0), ('rstd', 1)], yr)
        kb.tt('dve', y2[:, dc, :], y2[:, dc, :], xin[xi][:], ALU.add, yr + [f'xin{xi}'], yr)
        kb.out_toks.append(kb.dma('act' if dc % 2 else 'sp', f'st_x{dc % 4}', xov[:, dc, :], y2[:, dc, :], r=yr))
        if dc + 4 < KC:
            kb.dma('sp', f'ld_x{xi}', xin[xi][:], xv[:, dc + 4, :], w=[f'xin{xi}'])
    return kb.finish()


_CACHE = {}


def get_prog(name):
    if name not in _CACHE:
        _CACHE[name] = {'L0': build_L0, 'N1': build_N1, 'C1': build_C1, 'C2a': build_C2a, 'C2b': build_C2b}[name]()
    return _CACHE[name]


def run(name, in_maps):
    nc = {'L0': build_L0, 'N1': build_N1, 'C1': build_C1, 'C2a': build_C2a, 'C2b': build_C2b, 'M': build_M}[name]()
    res = run_bass_kernel_spmd(nc, in_maps, core_ids=list(range(NCORES)))
    return res.results


def core_bq(cid):
    return cid // 4, cid % 4


def tok_shard_T(a, cid):
    b, q = core_bq(cid)
    return np.ascontiguousarray(a[b, q * TOK:(q + 1) * TOK, :].T)


def stack_vecs(vs):
    return np.ascontiguousarray(np.stack([vec_fm(v) for v in vs], axis=1).astype(np.float32))


def host_L0(c, ada_w, ada_b):
    cT = np.ascontiguousarray(c.T.reshape(KC, 128, 2).transpose(1, 0, 2))
    in_maps = []
    for cid in range(NCORES):
        l, j = cid // 4, cid % 4
        in_maps.append({'cT': cT, 'W': np.ascontiguousarray(ada_w[l][:, j * 3072:(j + 1) * 3072]),
                        'bias': np.ascontiguousarray(ada_b[l][None, j * 3072:(j + 1) * 3072])})
    res = run('L0', in_maps)
    mod = np.zeros((2, 2, 6 * D), np.float32)
    for cid in range(NCORES):
        l, j = cid // 4, cid % 4
        mod[l, :, j * 3072:(j + 1) * 3072] = res[cid]['mod']
    return mod


def split_mod(mod_l):
    names = ['sh_m', 'sc_m', 'gt_m', 'sh_f', 'sc_f', 'gt_f']
    return {n: mod_l[:, i * D:(i + 1) * D] for i, n in enumerate(names)}


def host_N1(xT_shards, g, m):
    in_maps = []
    for cid in range(NCORES):
        b, q = core_bq(cid)
        in_maps.append({'xT': xT_shards[cid], 'nv': stack_vecs([g, m['sc_m'][b], m['sh_m'][b]])})
    res = run('N1', in_maps)
    return [r['hT'] for r in res]


def host_C1(xT_shards, moT_shards, w_out, m, post_g, pre_g):
    in_maps = []
    for cid in range(NCORES):
        b, q = core_bq(cid)
        in_maps.append({'xT': xT_shards[cid], 'moT': moT_shards[cid], 'w_out': w_out,
                        'cv': stack_vecs([m['gt_m'][b], post_g, pre_g, m['sc_f'][b], m['sh_f'][b]])})
    res = run('C1', in_maps)
    return [r['xo'] for r in res], [r['h2T'] for r in res]


def host_C2a(h2T_shards, w_up, conv_w, conv_b):
    cw = np.zeros((128, NFF, 4), np.float32)
    for k in range(3):
        cw[:, :, k] = vec_fm(conv_w[k])
    cw[:, :, 3] = vec_fm(conv_b)
    in_maps = []
    for cid in range(NCORES):
        b, q = core_bq(cid)
        if q == 0:
            halo = np.zeros((D, 2), NPBF)
        else:
            halo = h2T_shards[cid - 1][:, -2:]
        in_maps.append({'h2e': np.ascontiguousarray(np.concatenate([halo, h2T_shards[cid]], axis=1)),
                        'w_up': w_up, 'cw': cw})
    res = run('C2a', in_maps)
    return [r['actT'] for r in res]


def host_C2b(actT_shards, xT_shards, w_down, m, post_g):
    in_maps = []
    for cid in range(NCORES):
        b, q = core_bq(cid)
        in_maps.append({'actT': actT_shards[cid], 'w_down': w_down, 'xT': xT_shards[cid],
                        'fv': stack_vecs([m['gt_f'][b], post_g])})
    res = run('C2b', in_maps)
    return [r['xo'] for r in res]


class V:
    def __init__(self, ap, res):
        self.ap = ap
        self.res = list(res) if isinstance(res, (list, tuple)) and not (len(res) and isinstance(res[0], str) and False) else [res]

    def __getitem__(self, idx):
        v = V.__new__(V)
        v.ap = self.ap[idx]
        v.res = self.res
        return v

    def r(self, res):
        v = V.__new__(V)
        v.ap = self.ap
        v.res = list(res)
        return v


def _rs(*vs):
    out = []
    for v in vs:
        if isinstance(v, V):
            out.extend(v.res)
    return out


def _isps(r):
    return isinstance(r, str) and r.startswith('ps:')


def _rw(ins, outs):
    rs = _rs(*ins)
    ws = _rs(*outs)
    return [r for r in rs if not _isps(r)], ws + [r for r in rs if _isps(r)]


def _ap(v):
    return v.ap if isinstance(v, V) else v


class KM(KB):
    def vsb(self, name, shape, dt=F32):
        return V(self.sb(name, shape, dt)[:], [name])

    def vmm(self, out, lhsT, rhs, start=True, stop=True):
        self.mm(_ap(out), _ap(lhsT), _ap(rhs), start, stop, *_rw((lhsT, rhs), (out,)))

    def vtr(self, out, in_, ident):
        self.tr(_ap(out), _ap(in_), _ap(ident), *_rw((in_, ident), (out,)))

    def vact(self, out, in_, func, bias=None, scale=None, accum_out=None):
        self.act(_ap(out), _ap(in_), func, *_rw((in_, bias, scale), (out, accum_out)),
                 bias=_ap(bias) if bias is not None else None, scale=_ap(scale) if scale is not None else None,
                 accum_out=_ap(accum_out) if accum_out is not None else None)

    def vtt(self, eng, out, in0, in1, op):
        self.tt(eng, _ap(out), _ap(in0), _ap(in1), op, *_rw((in0, in1), (out,)))

    def vts(self, eng, out, in0, s1, s2, op0, op1=None):
        self.ts(eng, _ap(out), _ap(in0), _ap(s1), _ap(s2), op0, op1, *_rw((in0, s1, s2), (out,)))

    def vstt(self, out, in0, scalar, in1, op0, op1):
        self.stt(_ap(out), _ap(in0), _ap(scalar), _ap(in1), op0, op1, *_rw((in0, scalar, in1), (out,)))

    def vcopy(self, eng, out, in_):
        self.copy(eng, _ap(out), _ap(in_), *_rw((in_,), (out,)))

    def vrecip(self, out, in_):
        self.recip(_ap(out), _ap(in_), *_rw((in_,), (out,)))

    def vdma(self, q, key, out, in_):
        return self.S.dma(q, key, [(_ap(out), _ap(in_))], _rs(in_), _rs(out))

    def vreduce(self, out, in_, op):
        o, i = _ap(out), _ap(in_)
        self.S.op('dve', lambda e: e.tensor_reduce(out=o, in_=i, axis=AX.X, op=op), *_rw((in_,), (out,)))


NBLK = 8
QSCALE = 128 ** -0.5
NFM = 8
NTM1 = 768
NTM2 = 130


def build_M():
    kb = KM()
    hT = kb.din('hT', [D, S_LEN], BF16)
    wfm_d = kb.din('wfm', [D, NFM * 128])
    wtm_d = kb.din('wtm', [D, NTM1 + NTM2])
    cwq_d = kb.din('cwq', [128, 3, 4])
    dnc_d = kb.din('dnc', [128, 2])
    dng_d = kb.din('dng', [64, 128])
    bias_d = kb.din('biasT', [128, 640])
    maskb_d = kb.din('maskb', [128, 640])
    am_d = kb.din('amc', [128, 256])
    sink_d = kb.din('sink', [128, 1])
    sgg_d = kb.din('sgg', [128, 128])
    sgw_d = kb.din('sgwT', [128, 128])
    sgb_d = kb.din('sgb', [128, 512])
    tri_d = kb.din('tri', [128, 128])
    id_d = kb.din('ident', [128, 128], BF16)
    lt_d = kb.din('lt', [64, 64], BF16)
    m64_d = kb.din('m64', [64, 9, 64])
    oa_d = kb.dout('out_a', [S_LEN, 128], BF16)
    ob_d = kb.dout('out_b', [S_LEN, 128], BF16)
    oc_d = kb.dout('out_c', [S_LEN, 128], BF16)
    od_d = kb.dout('out_d', [128, S_LEN], BF16)

    def load(name, d, shape, dt=F32, q='sp'):
        v = kb.vsb(name, shape, dt)
        kb.vdma(q, 'ld_' + name, v, V(d, []))
        return v
    wfm = kb.vsb('wfm', [128, KC, NFM * 128], BF16)
    wtm = kb.vsb('wtm', [128, KC, NTM1 + NTM2], BF16)
    for h2 in range(2):
        kb.vdma('pool', f'ld_wfm{h2}', wfm[:, h2 * 8:(h2 + 1) * 8, :].r([('wfm', h2)]),
                V(fm(wfm_d)[:, h2 * 8:(h2 + 1) * 8, :], []))
        kb.vdma('pool', f'ld_wtm{h2}', wtm[:, h2 * 8:(h2 + 1) * 8, :].r([('wtm', h2)]),
                V(fm(wtm_d)[:, h2 * 8:(h2 + 1) * 8, :], []))
    wfm = wfm.r([('wfm', 0), ('wfm', 1)])
    wtm = wtm.r([('wtm', 0), ('wtm', 1)])
    cwq = load('cwq', cwq_d[:, :, :], [128, 3, 4])
    dnc = load('dnc', dnc_d[:, :], [128, 2])
    dng = load('dng', dng_d[:, :], [64, 128])
    biasT = load('biasT', bias_d[:, :], [128, 640])
    maskb = load('maskb', maskb_d[:, :], [128, 640])
    amc = load('amc', am_d[:, :], [128, 256])
    sgg = load('sgg', sgg_d[:, :], [128, 128])
    sgw = load('sgw', sgw_d[:, :], [128, 128])
    sgb = load('sgb', sgb_d[:, 0:128], [128, 128])
    tri = load('tri', tri_d[:, :], [128, 128])
    ident = load('ident', id_d[:, :], [128, 128], BF16)
    lt = load('lt', lt_d[:, :], [64, 64], BF16)
    m64 = load('m64', m64_d[:, :, :], [64, 9, 64])
    negC, strict, causal, id64 = m64[:, 0, :], m64[:, 1, :], m64[:, 2, :], m64[:, 3, :]
    md8, md8T = m64[:, 4, :], m64[:, 5, :]
    mkT = [m64[:, 6, :], m64[:, 7, :], m64[:, 8, :]]
    onesb = kb.vsb('onesb', [128, 128], BF16)
    kb.S.op('pool', lambda e: e.memset(onesb.ap, 1.0), (), onesb.res)
    onec = kb.vsb('onec', [128, 1])
    kb.S.op('pool', lambda e: e.memset(onec.ap, 1.0), (), onec.res)
    epsc = kb.vsb('epsc', [128, 1])
    kb.S.op('pool', lambda e: e.memset(epsc.ap, EPS), (), epsc.res)
    kb.vtt('dve', biasT, biasT, maskb, ALU.add)
    BM = biasT
    sgwb = kb.vsb('sgwb', [128, 128], BF16)
    kb.vtt('dve', sgwb, sgw, tri, ALU.mult)
    negA = kb.vsb('negA', [128, 1])
    kb.vact(negA, dnc[:, 0:1], AF.Exp)
    kb.vts('dve', negA, negA, -1.0, None, ALU.mult)
    dtb = dnc[:, 1:2]
    sC = kb.vsb('sC', [128, 257])
    kb.vdma('sp', 'ld_sink', sC[:, 0:1].r(['sC_sink']), V(sink_d[:, :], []))

    KTb = kb.vsb('KTb', [128, S_LEN], BF16)
    KTc = kb.vsb('KTc', [128, S_LEN], BF16)
    Vb = kb.vsb('Vb', [128, 32, 128], BF16)
    Vc = kb.vsb('Vc', [128, 32, 128], BF16)
    raw = [kb.vsb(f'raw{g}', [128, 515]) for g in range(3)]
    for g in range(3):
        kb.S.op('pool', (lambda e, a=raw[g].ap[:, 0:3]: e.memset(a, 0.0)), (), [(f'raw{g}', 'h')])
    Sst = kb.vsb('Sst', [128, 128])
    Sbf = kb.vsb('Sbf', [128, 128], BF16)
    kb.S.op('pool', lambda e: e.memset(Sst.ap, 0.0), (), Sst.res)
    kb.S.op('pool', lambda e: e.memset(Sbf.ap, 0.0), (), Sbf.res)

    pA = [V(kb.ps(f'pA{i}', [128, 512])[:], [f'ps:A{i}']) for i in range(2)]
    pS = kb.ps('pS', [128, 1024])
    SB_RES = ['ps:S0', 'ps:S1']
    pSG = V(pS[:, 640:768], ['ps:S1'])
    pSc = V(pS[:, 768:1024], ['ps:S1'])
    pT = kb.ps('pT', [128, 1024], BF16)
    pKV = [V(pT[0:64, 640 + 192 * p:768 + 192 * p], ['ps:T']) for p in range(2)]
    pNA = [V(pT[0:64, 768 + 192 * p:832 + 192 * p], ['ps:T']) for p in range(2)]
    pset = []
    for p in range(2):
        pb_ = kb.ps(f'pset{p}', [128, 512])
        r = [f'ps:P{p}']
        pset.append({'G': V(pb_[:, 0:64], r), 'KK': V(pb_[0:64, 64:128], r), 'QK': V(pb_[0:64, 128:192], r),
                     'Mk': V(pb_[0:64, 192:256], r), 'MkT': V(pb_[0:64, 256:320], r),
                     'Tu': V(pb_[0:64, 320:384], r), 'Tv': V(pb_[0:64, 384:448], r),
                     'U': V(pb_[0:64, 192:320], r), 'wT': V(pb_[:, 0:64], r)})
    p7 = kb.ps('p7', [128, 512])
    pU = V(p7[0:64, 0:128], ['ps:7'])
    pwT = V(p7[:, 128:192], ['ps:7'])
    pwS = V(p7[0:64, 192:320], ['ps:7'])
    pO = V(p7[0:64, 320:448], ['ps:7'])
    pgc = V(p7[0:64, 448:456], ['ps:7'])
    pgl = V(p7[:, 456:464], ['ps:7'])
    GV0 = 0
    npA = [0]

    def nextA():
        npA[0] += 1
        return pA[npA[0] % 2]
    pP = [pA[0], pA[1], V(pS[:, 0:512], ['ps:S0']), V(pS[:, 512:1024], ['ps:S1'])]
    npP = [0]

    def nextP():
        npP[0] += 1
        return pP[npP[0] % 4]

    hb = [kb.vsb(f'hb{i}', [128, KC, 512], BF16) for i in range(2)]
    QTb = kb.vsb('QTb', [128, 512], BF16)
    QTc = kb.vsb('QTc', [128, 512], BF16)
    uT = kb.vsb('uT', [128, 512])
    cacc = [kb.vsb(f'cacc{g}', [128, 512]) for g in range(3)]
    xs = cacc
    sqb2 = [kb.vsb(f'sqb{i}', [128, 512], BF16) for i in range(2)]
    rn2 = [kb.vsb(f'rn{i}', [128, 512]) for i in range(2)]
    QTn = kb.vsb('QTn', [128, 512], BF16)
    KTn = kb.vsb('KTn', [128, 512], BF16)
    VTa = kb.vsb('VTa', [128, 512], BF16)
    gv4 = kb.vsb('gv4', [128, 4, 512], BF16)
    bnst = kb.vsb('bnst', [128, 6])
    bnag = kb.vsb('bnag', [128, 2])
    lrs = kb.vsb('lrs', [128, 1])
    vn = kb.vsb('vn', [128, 128])
    vtok = [kb.vsb(f'vtok{i}', [128, 128], BF16) for i in range(2)]
    sgt = kb.vsb('sgt', [128, 512])
    odst = [kb.vsb(f'odst{i}', [128, 512], BF16) for i in range(2)]
    sB = kb.vsb('sB', [128, 640])
    pB = kb.vsb('pB', [128, 640], BF16)
    PTs = kb.vsb('PTs', [128, 640], BF16)
    mx = kb.vsb('mx', [128, 1])
    nmx = kb.vsb('nmx', [128, 1])
    rsum = kb.vsb('rsum', [128, 1])
    rrec = kb.vsb('rrec', [128, 1])
    pC = kb.vsb('pC', [128, 257], BF16)
    mxc = kb.vsb('mxc', [128, 1])
    nmxc = kb.vsb('nmxc', [128, 1])
    rsumc = kb.vsb('rsumc', [128, 1])
    rrecc = kb.vsb('rrecc', [128, 1])
    obst = [kb.vsb(f'obst{i}', [128, 4, 128], BF16) for i in range(2)]
    ocst = [kb.vsb(f'ocst{i}', [128, 4, 128], BF16) for i in range(2)]
    oast = [kb.vsb(f'oast{i}', [64, 8, 128], BF16) for i in range(2)]
    ba = kb.vsb('ba', [64, 8, 2])
    beta = kb.vsb('beta', [64, 8])
    nbeta = kb.vsb('nbeta', [64, 8])
    spx = kb.vsb('spx', [64, 8])
    spa = kb.vsb('spa', [64, 8])
    spe = kb.vsb('spe', [64, 8])
    la = kb.vsb('la', [64, 8])
    lahi = kb.vsb('lahi', [64, 8], BF16)
    lalo = kb.vsb('lalo', [64, 8], BF16)
    gcol = kb.vsb('gcol', [64, 8])
    egc = kb.vsb('egc', [64, 8])
    bgc = kb.vsb('bgc', [64, 8])
    kds = kb.vsb('kds', [64, 8])
    egl = kb.vsb('egl', [128, 8])
    sgate = kb.vsb('sgate', [64, 8, 128])
    def two(name, shape, dt=F32):
        return [kb.vsb(f'{name}{i}', shape, dt) for i in range(2)]
    dfm, E_, Es_, Ec_ = two('dfm', [64, 64]), two('E', [64, 64]), two('Es', [64, 64]), two('Ec', [64, 64])
    Nb, NTb = two('Nb', [64, 64], BF16), two('NTb', [64, 64], BF16)
    Ma, MaT = two('Ma', [64, 64], BF16), two('MaT', [64, 64], BF16)
    Mb, MbT = two('Mb', [64, 64], BF16), two('MbT', [64, 64], BF16)
    Tb_, TTb = two('Tb', [64, 64], BF16), two('TTb', [64, 64], BF16)
    N8, N8T = two('N8', [64, 64], BF16), two('N8T', [64, 64], BF16)
    BT = [two(f'BT{k}', [64, 64], BF16) for k in range(3)]
    Yb = two('Yb', [64, 64], BF16)
    vbt, kbg = two('vbt', [64, 128], BF16), two('kbg', [64, 128], BF16)
    kdec = [kb.vsb(f'kdec{i}', [64, 128], BF16) for i in range(4)]
    def four(name, shape, dt=F32):
        return [kb.vsb(f'{name}{i}', shape, dt) for i in range(4)]
    ut, wTb = four('ut', [64, 128]), four('wTb', [128, 64], BF16)
    attb, attT = two('attb', [64, 64], BF16), four('attT', [64, 64], BF16)
    EGr, qdT = two('EGr', [128, 64]), four('qdT', [128, 64], BF16)
    vnew = two('vnew', [64, 128], BF16)
    gng = four('gng', [64, 128])
    olg = two('olg', [64, 1])
    osq = two('osq', [64, 128])
    oss, orr = two('oss', [64, 1]), two('orr', [64, 1])

    hTv = fm(hT)
    oav = oa_d.rearrange('(n c p) d -> n p c d', c=8, p=64)
    obv = ob_d.rearrange('(n i p) d -> n p i d', i=4, p=128)
    ocv = oc_d.rearrange('(n i p) d -> n p i d', i=4, p=128)

    kb.vdma('sp', 'ld_h0', hb[0], V(hTv[:, :, 0:512], []))
    for tb in range(NBLK):
        h = hb[tb % 2]
        if tb + 1 < NBLK:
            kb.vdma('sp', f'ld_h{(tb + 1) % 2}', hb[(tb + 1) % 2], V(hTv[:, :, (tb + 1) * 512:(tb + 2) * 512], []))
        bsl = slice(tb * 512, (tb + 1) * 512)
        for g in range(NFM):
            p = nextP()
            for kc in range(KC):
                kb.vmm(p, wfm[:, kc, g * 128:(g + 1) * 128], h[:, kc, :], kc == 0, kc == KC - 1)
            if g < 3:
                kb.vcopy('act', raw[g][:, 3:515].r([(f'raw{g}', 'c')]), p)
                rw = raw[g].r([(f'raw{g}', 'c'), (f'raw{g}', 'h')])
                kb.vts('dve', cacc[g], rw[:, 3:515], cwq[:, g, 3:4], None, ALU.mult)
                for k in (2, 1, 0):
                    kb.vstt(cacc[g], rw[:, k:k + 512], cwq[:, g, k:k + 1], cacc[g], ALU.mult, ALU.add)
                kb.S.op('pool', (lambda e, o=raw[g].ap[:, 0:3], i=raw[g].ap[:, 512:515]: e.tensor_copy(out=o, in_=i)),
                        [(f'raw{g}', 'c')], [(f'raw{g}', 'h')])
            elif g == 3:
                kb.vcopy('act', QTb, p)
            elif g == 4:
                kb.vcopy('act', KTb[:, bsl].r([('KTb', tb)]), p)
            elif g == 5:
                kb.vcopy('act', QTc, p)
            elif g == 6:
                kb.vcopy('act', KTc[:, bsl].r([('KTc', tb)]), p)
            else:
                kb.vact(uT, p, AF.Gelu_apprx_tanh)
        for i in range(4):
            j = tb * 4 + i
            tsl = slice(i * 128, (i + 1) * 128)
            p1 = nextP()
            for kc in range(KC):
                kb.vmm(p1, h[:, kc, tsl], wtm[:, kc, 0:512], kc == 0, kc == KC - 1)
            kb.vact(gv4[:, i, :].r([('gv4', i)]), p1, AF.Gelu_apprx_tanh)
            p2 = nextP()
            for kc in range(KC):
                kb.vmm(p2[:, 0:256], h[:, kc, tsl], wtm[:, kc, 512:768], kc == 0, kc == KC - 1)
            kb.vcopy('act', Vb[:, j, :].r([('Vb', j)]), p2[:, 0:128])
            kb.vcopy('act', Vc[:, j, :].r([('Vc', j)]), p2[:, 128:256])
        for c in range(8):
            p = nextP()
            pc = p[0:64, 0:NTM2]
            for kc in range(KC):
                kb.vmm(pc, h[:, kc, c * 64:(c + 1) * 64], wtm[:, kc, NTM1:NTM1 + NTM2], kc == 0, kc == KC - 1)
            kb.vact(sgate[:, c, :].r([('sgate', c)]), pc[:, 0:128], AF.Silu)
            kb.vcopy('dve', ba[:, c, :].r([('ba', c)]), pc[:, 128:130])
        kb.vact(xs[0], cacc[0], AF.Silu)
        kb.vact(xs[1], cacc[1], AF.Silu)
        kb.vact(VTa, cacc[2], AF.Silu)
        pq = []
        for g in range(2):
            kb.vact(sqb2[g], xs[g], AF.Square)
            p = nextP()
            kb.vmm(p, onesb, sqb2[g])
            pq.append(p)
        for g in range(2):
            kb.vact(rn2[g], pq[g], AF.Ln, bias=epsc, scale=1.0)
        for g in range(2):
            kb.vact(rn2[g], rn2[g], AF.Exp, scale=-0.5)
            kb.vtt('dve', QTn if g == 0 else KTn, xs[g], rn2[g], ALU.mult)
        bar = ba.r([('ba', c) for c in range(8)])
        kb.vact(beta, bar[:, :, 0], AF.Exp, scale=-1.0)
        kb.vts('dve', beta, beta, 1.0, None, ALU.add)
        kb.vrecip(beta, beta)
        kb.vts('dve', nbeta, beta, -1.0, None, ALU.mult)
        kb.vts('dve', spx, bar[:, :, 1], dtb[0:64, :], None, ALU.add)
        kb.vact(spa, spx, AF.Abs)
        kb.vact(spe, spa, AF.Exp, scale=-1.0)
        kb.vact(spe, spe, AF.Ln, bias=onec[0:64, :], scale=1.0)
        kb.vstt(spa, spx, 0.0, spe, ALU.max, ALU.add)
        kb.vts('dve', la, spa, negA[0:64, :], None, ALU.mult)
        kb.vcopy('dve', lahi, la)
        kb.vtt('dve', lalo, la, lahi, ALU.subtract)
        kb.vmm(pgc, lt, lahi, True, False)
        kb.vmm(pgc, lt, lalo, False, True)
        kb.vmm(pgl, onesb[0:64, :], lahi, True, False)
        kb.vmm(pgl, onesb[0:64, :], lalo, False, True)
        kb.vcopy('dve', gcol, pgc)
        kb.vact(egc, pgc, AF.Exp)
        kb.vact(egl, pgl, AF.Exp)
        kb.vtt('dve', kds, pgl[0:64, :], gcol, ALU.subtract)
        kb.vact(kds, kds, AF.Exp)
        kb.vtt('dve', bgc, beta, egc, ALU.mult)

        def chunk_par(c):
            n = tb * 8 + c
            q = n % 2
            q4 = n % 4
            ps_ = pset[q]
            csl = slice(c * 64, (c + 1) * 64)
            kb.vtr(pKV[q], KTn[:, csl], ident)
            kb.vts('dve', kbg[q], pKV[q], bgc[:, c:c + 1], None, ALU.mult)
            kb.vts('dve', kdec[q4], pKV[q], kds[:, c:c + 1], None, ALU.mult)
            kb.vmm(ps_['G'], V(lahi.ap[:, c:c + 1].to_broadcast([64, 128]), lahi.res), lt, True, False)
            kb.vmm(ps_['G'], V(lalo.ap[:, c:c + 1].to_broadcast([64, 128]), lalo.res), lt, False, True)
            kb.vmm(ps_['KK'], KTn[:, csl], KTn[:, csl])
            kb.vmm(ps_['QK'], QTn[:, csl], KTn[:, csl])
            yield
            kb.vtr(pKV[q], VTa[:, csl], ident)
            kb.vts('dve', vbt[q], pKV[q], beta[:, c:c + 1], None, ALU.mult)
            kb.vstt(dfm[q], ps_['G'][0:64, :], gcol[:, c:c + 1], negC, ALU.subtract, ALU.mult)
            kb.vact(EGr[q], ps_['G'], AF.Exp)
            yield
            kb.vact(E_[q], dfm[q], AF.Exp)
            kb.vstt(qdT[q4], QTn[:, csl], QSCALE, EGr[q], ALU.mult, ALU.mult)
            kb.vtt('pool', gng[q4], sgate[:, c, :].r([('sgate', c)]), dng, ALU.mult)
            yield
            kb.vtt('pool', Es_[q], E_[q], strict, ALU.mult)
            kb.vtt('pool', Ec_[q], E_[q], causal, ALU.mult)
            yield
            kb.vstt(Nb[q], ps_['KK'], nbeta[:, c:c + 1], Es_[q], ALU.mult, ALU.mult)
            kb.vstt(attb[q], ps_['QK'], QSCALE, Ec_[q], ALU.mult, ALU.mult)
            yield
            kb.vtr(pNA[q], Nb[q], ident[0:64, 0:64])
            kb.vtt('pool', N8[q], Nb[q], md8, ALU.mult)
            yield
            kb.vcopy('act', NTb[q], pNA[q])
            kb.vtt('pool', Tb_[q], N8[q], id64, ALU.add)
            yield
            kb.vtr(pNA[q], attb[q], ident[0:64, 0:64])
            kb.vtt('pool', N8T[q], NTb[q], md8T, ALU.mult)
            yield
            kb.vcopy('act', attT[q4], pNA[q])
            kb.vtt('pool', TTb[q], N8T[q], id64, ALU.add)
            for k in range(3):
                kb.vtt('pool', BT[k][q], NTb[q], mkT[k], ALU.mult)
            yield
            M, MT = N8[q], N8T[q]
            for lev in range(2):
                Mn, MnT = (Ma[q], MaT[q]) if lev == 0 else (Mb[q], MbT[q])
                kb.vmm(ps_['Mk'], MT, M)
                kb.vmm(ps_['MkT'], M, MT)
                yield
                kb.vcopy('act', Mn, ps_['Mk'])
                kb.vcopy('dve', MnT, ps_['MkT'])
                yield
                kb.vmm(ps_['Tu'], MnT, Tb_[q])
                kb.vmm(ps_['Tv'], Mn, TTb[q])
                yield
                kb.vtt('dve', Tb_[q], Tb_[q], ps_['Tu'], ALU.add)
                kb.vtt('dve', TTb[q], TTb[q], ps_['Tv'], ALU.add)
                yield
                M, MT = Mn, MnT
            for k in range(3):
                kb.vmm(ps_['Mk'], BT[k][q], Tb_[q])
                yield
                kb.vcopy('act', Yb[q], ps_['Mk'])
                yield
                if k < 2:
                    kb.vmm(ps_['Tu'], TTb[q], Yb[q])
                kb.vmm(ps_['MkT'], Yb[q], TTb[q])
                yield
                if k < 2:
                    kb.vtt('dve', Tb_[q], Tb_[q], ps_['Tu'], ALU.add)
                kb.vtt('dve', TTb[q], TTb[q], ps_['MkT'], ALU.add)
                yield
            kb.vmm(ps_['U'], TTb[q], vbt[q])
            kb.vmm(ps_['wT'], kbg[q], TTb[q])
            yield
            kb.vcopy('act', ut[q4], ps_['U'])
            kb.vcopy('act', wTb[q4], ps_['wT'])
            yield

        def chunk_seq(c):
            n = tb * 8 + c
            q = n % 2
            q4 = n % 4
            kb.vmm(pwS, wTb[q4], Sbf)
            yield
            kb.vtt('dve', vnew[q], ut[q4], pwS, ALU.subtract)
            yield
            yield
            kb.vmm(pO, qdT[q4], Sbf, True, False)
            kb.vmm(pO, attT[q4], vnew[q], False, True)
            pn = nextA()
            pSn = pn[:, 0:128]
            kb.vmm(pSn, kdec[q4], vnew[q])
            yield
            kb.vstt(Sbf, Sst, egl[:, c:c + 1], pSn, ALU.mult, ALU.add)
            kb.vstt(Sst, Sst, egl[:, c:c + 1], pSn, ALU.mult, ALU.add)
            yield
            kb.vact(osq[q], pO, AF.Square, accum_out=oss[q])
            kb.vact(olg[q], oss[q], AF.Ln, bias=epsc[0:64, :], scale=1.0 / 128)
            kb.vact(orr[q], olg[q], AF.Exp, scale=-0.5)
            kb.vstt(oast[tb % 2][:, c, :].r([(f'oast{tb % 2}', c)]), pO, orr[q], gng[q4], ALU.mult, ALU.mult)
            yield

        def tile_work(i):
            j = tb * 4 + i
            tsl = slice(i * 128, (i + 1) * 128)
            gvi = gv4[:, i, :].r([('gv4', i)])
            kb.S.op('dve', (lambda e, o=bnst.ap, a=gvi.ap: e.bn_stats(out=o, in_=a)), gvi.res, bnst.res)
            kb.S.op('dve', (lambda e, o=bnag.ap, a=bnst.ap: e.bn_aggr(out=o, in_=a)), bnst.res, bnag.res)
            kb.vact(lrs, bnag[:, 1:2], AF.Ln, bias=epsc, scale=1.0)
            kb.vact(lrs, lrs, AF.Exp, scale=-0.5)
            yield
            kb.vts('dve', vn, gvi[:, GV0:GV0 + 128], bnag[:, 0:1], lrs, ALU.subtract, ALU.mult)
            vt = vtok[i % 2]
            kb.vtt('pool', vt, vn, sgg, ALU.mult)
            yield
            yield
            kb.vmm(pSG, vt, sgwb)
            kb.vtt('dve', sgt[:, tsl].r([('sgt', i)]), pSG, sgb[:, 0:128], ALU.add)
            kb.vtt('pool', odst[tb % 2][:, tsl].r([(f'odst{tb % 2}', i)]), sgt[:, tsl].r([('sgt', i)]), uT[:, tsl], ALU.mult)
            yield
            kt0 = max(0, j - 4)
            nk = j + 1 - kt0
            W = nk * 128
            off = (5 - nk) * 128
            kres = [('KTb', t) for t in range(kt0 * 128 // 512, tb + 1)]
            for (a, b2) in ((0, min(W, 512)), (512, W)):
                if b2 > a:
                    kb.vmm(V(pS[:, a:b2], ['ps:S0' if a == 0 else 'ps:S1']), QTb[:, tsl],
                           KTb[:, kt0 * 128 + a:kt0 * 128 + b2].r(kres))
            kb.vstt(sB[:, 0:W], V(pS[:, 0:W], SB_RES if W > 512 else ['ps:S0']), QSCALE, BM[:, off:off + W],
                    ALU.mult, ALU.add)
            yield
            kb.vreduce(mx, sB[:, 0:W], ALU.max)
            kb.vts('dve', nmx, mx, -1.0, None, ALU.mult)
            yield
            kb.vact(pB[:, 0:W], sB[:, 0:W], AF.Exp, bias=nmx, scale=1.0, accum_out=rsum)
            yield
            yield
            for t in range(nk):
                kb.vtr(V(pT[:, t * 128:(t + 1) * 128], ['ps:T']), pB[:, t * 128:(t + 1) * 128], ident)
            kb.vcopy('dve', PTs[:, 0:W], V(pT[:, 0:W], ['ps:T']))
            yield
            yield
            pn = nextA()
            pOb = pn[:, 0:128]
            for t in range(nk):
                kb.vmm(pOb, PTs[:, t * 128:(t + 1) * 128], Vb[:, kt0 + t, :].r([('Vb', kt0 + t)]), t == 0, t == nk - 1)
            kb.vrecip(rrec, rsum)
            kb.vts('dve', obst[tb % 2][:, i, :].r([(f'obst{tb % 2}', i)]), pOb, rrec, None, ALU.mult)
            yield
            kt0 = max(0, j - 1)
            nk = j + 1 - kt0
            W = nk * 128
            off = (2 - nk) * 128
            kres = [('KTc', t) for t in range(kt0 * 128 // 512, tb + 1)]
            kb.vmm(pSc[:, 0:W], QTc[:, tsl], KTc[:, kt0 * 128:kt0 * 128 + W].r(kres))
            kb.vstt(sC[:, 1:1 + W].r(['sC']), pSc[:, 0:W], QSCALE, amc[:, off:off + W], ALU.mult, ALU.add)
            scr = sC[:, 0:1 + W].r(['sC', 'sC_sink'])
            yield
            kb.vreduce(mxc, scr, ALU.max)
            kb.vts('dve', nmxc, mxc, -1.0, None, ALU.mult)
            yield
            kb.vact(pC[:, 0:1 + W], scr, AF.Exp, bias=nmxc, scale=1.0, accum_out=rsumc)
            yield
            yield
            for t in range(nk):
                kb.vtr(V(pT[:, t * 128:(t + 1) * 128], ['ps:T']), pC[:, 1 + t * 128:1 + (t + 1) * 128], ident)
            kb.vcopy('dve', PTs[:, 0:W], V(pT[:, 0:W], ['ps:T']))
            yield
            yield
            pn = nextA()
            pOc = pn[:, 0:128]
            for t in range(nk):
                kb.vmm(pOc, PTs[:, t * 128:(t + 1) * 128], Vc[:, kt0 + t, :].r([('Vc', kt0 + t)]), t == 0, t == nk - 1)
            kb.vrecip(rrecc, rsumc)
            kb.vts('dve', ocst[tb % 2][:, i, :].r([(f'ocst{tb % 2}', i)]), pOc, rrecc, None, ALU.mult)
            yield

        def seq_pair(i):
            for c in (2 * i, 2 * i + 1):
                yield from chunk_seq(c)

        def rr(gens):
            gens = list(gens)
            while gens:
                for g in list(gens):
                    try:
                        next(g)
                    except StopIteration:
                        gens.remove(g)

        for i in range(4):
            ga = chunk_par(2 * i)
            next(ga)
            gl = [ga, chunk_par(2 * i + 1), tile_work(i)]
            if i > 0:
                gl.append(seq_pair(i - 1))
            rr(gl)
        rr([seq_pair(3)])
        kb.out_toks.append(kb.vdma('sp', f'st_d{tb % 2}', V(od_d[:, bsl], []),
                                   odst[tb % 2].r([(f'odst{tb % 2}', i) for i in range(4)])))
        kb.out_toks.append(kb.vdma('sp', f'st_a{tb % 2}', V(oav[tb], []),
                                   oast[tb % 2].r([(f'oast{tb % 2}', c) for c in range(8)])))
        kb.out_toks.append(kb.vdma('sp', f'st_b{tb % 2}', V(obv[tb], []),
                                   obst[tb % 2].r([(f'obst{tb % 2}', i) for i in range(4)])))
        kb.out_toks.append(kb.vdma('sp', f'st_c{tb % 2}', V(ocv[tb], []),
                                   ocst[tb % 2].r([(f'ocst{tb % 2}', i) for i in range(4)])))
    return kb.finish()


OFF = {'a_q': 0, 'a_k': 512, 'a_v': 1024, 'a_gate': 1536, 'a_beta': 2048, 'a_alpha': 2052, 'b_q': 2056,
       'b_k': 2568, 'b_v': 3080, 'c_q': 3592, 'c_k': 4104, 'c_v': 4360, 'd_u': 4616, 'd_v': 5128}


def m_consts():
    q = np.arange(128)[:, None]
    kk = np.arange(640)[None, :]
    hi = (q >= 64).astype(np.int64)
    validb = (kk // 64 >= hi) & (kk // 64 <= 8 + hi)
    maskb = np.where(validb, 0.0, -30000.0).astype(np.float32)
    idxb = np.clip(512 + q - kk, -256, 256) + 256
    kc = np.arange(256)[None, :]
    validc = (kc // 64 >= hi) & (kc // 64 <= 2 + hi)
    distc = np.abs(128 + q - kc).astype(np.float32)
    i64 = np.arange(64)
    cge = (i64[:, None] >= i64[None, :])
    ci, si = i64[:, None], i64[None, :]
    md8 = ((ci // 8 == si // 8) & (ci > si)).astype(np.float32)

    def mk(b):
        return ((ci // (2 * b) == si // (2 * b)) & (ci % (2 * b) >= b) & (si % (2 * b) < b)).astype(np.float32)
    m64 = np.stack([np.where(cge, -1.0, 0.0), (ci > si).astype(np.float32), cge.astype(np.float32), np.eye(64),
                    md8, md8.T, mk(8).T, mk(16).T, mk(32).T], axis=1).astype(np.float32)
    i128 = np.arange(128)
    return {
        'maskb': maskb, 'idxb': idxb, 'validc': validc, 'distc': distc,
        'm64': np.ascontiguousarray(m64),
        'lt': (i64[:, None] <= i64[None, :]).astype(NPBF),
        'ident': np.eye(128).astype(NPBF),
        'tri': (i128[:, None] <= i128[None, :]).astype(np.float32),
    }


def host_M(hT_shards, P, l):
    C = m_consts()
    w_in = P['w_in'][l]
    in_maps = []
    for cid in range(NCORES):
        b, hd = cid // 4, cid % 4
        kvh = hd // 2

        def cols(name, h, n=128):
            return w_in[:, OFF[name] + h * n:OFF[name] + (h + 1) * n]
        wfm = np.concatenate([cols('a_q', hd), cols('a_k', hd), cols('a_v', hd), cols('b_q', hd), cols('b_k', hd),
                              cols('c_q', hd), cols('c_k', kvh), cols('d_u', hd)], axis=1)
        others = [g for g in range(4) if g != hd]
        wtm = np.concatenate([cols('d_v', hd)] + [cols('d_v', g) for g in others] +
                             [cols('b_v', hd), cols('c_v', kvh), cols('a_gate', hd),
                              w_in[:, OFF['a_beta'] + hd:OFF['a_beta'] + hd + 1],
                              w_in[:, OFF['a_alpha'] + hd:OFF['a_alpha'] + hd + 1]], axis=1)
        cw = P['dn_conv_w'][l]
        cwq = np.stack([cw[:, g * 512 + hd * 128:g * 512 + (hd + 1) * 128].T for g in range(3)], axis=1)
        slope = np.float32(2.0 ** (-8.0 * (hd + 1) / 4))
        amc = np.where(C['validc'], -slope * C['distc'], np.float32(-30000.0)).astype(np.float32)
        in_maps.append({
            'hT': np.ascontiguousarray(np.concatenate(hT_shards[b * 4:(b + 1) * 4], axis=1)),
            'wfm': np.ascontiguousarray(wfm), 'wtm': np.ascontiguousarray(wtm),
            'cwq': np.ascontiguousarray(cwq.astype(np.float32)),
            'dnc': np.ascontiguousarray(np.broadcast_to(
                np.array([P['dn_a_log'][l][hd], P['dn_dt_bias'][l][hd]], np.float32)[None, :], (128, 2))),
            'dng': np.ascontiguousarray(np.broadcast_to(P['dn_norm_g'][l][None, :], (64, 128))),
            'biasT': np.ascontiguousarray(P['rel_bias'][l][hd][C['idxb']]),
            'maskb': C['maskb'], 'amc': amc,
            'sink': np.full((128, 1), P['sinks'][l][hd], np.float32),
            'sgg': np.ascontiguousarray(np.broadcast_to(P['sgu_norm_g'][l][hd * 128:(hd + 1) * 128][None, :], (128, 128))),
            'sgwT': np.ascontiguousarray(P['sgu_w'][l][hd].T),
            'sgb': np.ascontiguousarray(np.broadcast_to(np.tile(P['sgu_b'][l][hd], 4)[None, :], (128, 512))),
            'tri': C['tri'], 'ident': C['ident'], 'lt': C['lt'], 'm64': C['m64'],
        })
    res = run('M', in_maps)
    shards = []
    for cid in range(NCORES):
        b, q = core_bq(cid)
        tsl = slice(q * TOK, (q + 1) * TOK)
        rows = []
        for nm in ('out_a', 'out_b', 'out_c'):
            for hd in range(4):
                rows.append(res[b * 4 + hd][nm][tsl, :].T)
        for hd in range(4):
            rows.append(res[b * 4 + hd]['out_d'][:, tsl])
        shards.append(np.ascontiguousarray(np.concatenate(rows, axis=0)))
    return shards


def kernel(x, c, ada_w, ada_b, mix_pre_g, mix_post_g, w_in, dn_conv_w, dn_a_log, dn_dt_bias, dn_norm_g,
           rel_bias, sinks, sgu_norm_g, sgu_w, sgu_b, w_out, ffn_pre_g, ffn_post_g, ffn_w_up, ffn_conv_w,
           ffn_conv_b, ffn_w_down):
    f = lambda a: np.asarray(a, dtype=np.float32)
    x, c, ada_w, ada_b = f(x), f(c), f(ada_w), f(ada_b)
    P = {'w_in': f(w_in), 'dn_conv_w': f(dn_conv_w), 'dn_a_log': f(dn_a_log), 'dn_dt_bias': f(dn_dt_bias),
         'dn_norm_g': f(dn_norm_g), 'rel_bias': f(rel_bias), 'sinks': f(sinks), 'sgu_norm_g': f(sgu_norm_g),
         'sgu_w': f(sgu_w), 'sgu_b': f(sgu_b)}
    mix_pre_g, mix_post_g, ffn_pre_g, ffn_post_g = f(mix_pre_g), f(mix_post_g), f(ffn_pre_g), f(ffn_post_g)
    w_out, ffn_w_up, ffn_conv_w, ffn_conv_b, ffn_w_down = f(w_out), f(ffn_w_up), f(ffn_conv_w), f(ffn_conv_b), f(ffn_w_down)
    mod = host_L0(c, ada_w, ada_b)
    xs = [tok_shard_T(x, cid) for cid in range(NCORES)]
    for l in range(2):
        m = split_mod(mod[l])
        hs = host_N1(xs, mix_pre_g[l], m)
        mo = host_M(hs, P, l)
        xs, h2 = host_C1(xs, mo, np.ascontiguousarray(w_out[l]), m, mix_post_g[l], ffn_pre_g[l])
        acts = host_C2a(h2, np.ascontiguousarray(ffn_w_up[l]), ffn_conv_w[l], ffn_conv_b[l])
        xs = host_C2b(acts, xs, np.ascontiguousarray(ffn_w_down[l]), m, ffn_post_g[l])
    out = np.zeros((2, S_LEN, D), np.float32)
    for cid in range(NCORES):
        b, q = core_bq(cid)
        out[b, q * TOK:(q + 1) * TOK, :] = xs[cid].T
    return out
```
